# Optimizing a Trainium2 kernel written in Bass

```python
import math
import jax, jax.numpy as jnp
from jax import lax
import numpy as np

D_MODEL = 1024
BATCH = 8
SEQ = 4096
DEPTH = 2

CTX_LEN = 256
GRID_W = 64
Q_BLOCK = 128
ROPE_BASE = 10000.0
EPS = 1e-6

GQA_HEADS = 8
GQA_KV_HEADS = 2
GQA_HEAD_DIM = 64
MLA_HEADS = 8
MLA_Q_RANK = 256
MLA_KV_RANK = 128
MLA_NOPE_DIM = 64
MLA_ROPE_DIM = 32
MLA_V_DIM = 64
ATTN_WIDTH = GQA_HEADS * GQA_HEAD_DIM + MLA_HEADS * MLA_V_DIM
ATTN_SPLITS = (GQA_HEADS * GQA_HEAD_DIM, GQA_KV_HEADS * GQA_HEAD_DIM, GQA_KV_HEADS * GQA_HEAD_DIM,
               MLA_Q_RANK, MLA_KV_RANK, MLA_ROPE_DIM, ATTN_WIDTH)
ATTN_IN = (GQA_HEADS + 2 * GQA_KV_HEADS) * GQA_HEAD_DIM + MLA_Q_RANK + MLA_KV_RANK + MLA_ROPE_DIM + ATTN_WIDTH
HY_WIDTH = D_MODEL
HY_ORDER = 2
HY_SHORT = 3
HY_BANDS = 16
HY_EMB = 1 + 2 * HY_BANDS
HY_FFN = 64
HY_FAST_DECAY = 0.3
HY_SLOW_DECAY = 1.5
HY_DECAY_TARGET = 1e-2

kernel_name = "hybrid_gqa_mla_hyena_prefix_dit"


def _f32(a):
    return a.astype(jnp.float32)


def split_cols(p, sizes):
    out, start = [], 0
    for n in sizes:
        out.append(p[..., start:start + n])
        start += n
    return out


def rms_norm(x, w):
    xf = _f32(x)
    y = xf * lax.rsqrt(jnp.mean(xf * xf, axis=-1, keepdims=True) + EPS)
    return (y * _f32(w)).astype(x.dtype)


def modulate(x, norm_w, shift, scale):
    return rms_norm(x, norm_w) * (1 + scale) + shift


def axial_rope_tables(n_tokens, rot_dim):
    n_rows = n_tokens // GRID_W
    rows = jnp.repeat(jnp.arange(n_rows, dtype=jnp.int32), GRID_W)
    cols = jnp.tile(jnp.arange(GRID_W, dtype=jnp.int32), n_rows)
    quarter = rot_dim // 4
    inv_freq = ROPE_BASE ** (-jnp.arange(quarter, dtype=jnp.float32) / quarter)
    ang_r = _f32(rows)[:, None] * inv_freq
    ang_c = _f32(cols)[:, None] * inv_freq
    return (jnp.cos(ang_r), jnp.sin(ang_r), jnp.cos(ang_c), jnp.sin(ang_c))


def rotate_pairs(x, cos, sin):
    a, b = jnp.split(x, 2, axis=-1)
    return jnp.concatenate([a * cos - b * sin, b * cos + a * sin], axis=-1)


def apply_axial_rope(x, tables):
    cr, sr, cc, sc = (t[:, None, :].astype(x.dtype) for t in tables)
    xr, xc = jnp.split(x, 2, axis=-1)
    return jnp.concatenate([rotate_pairs(xr, cr, sr), rotate_pairs(xc, cc, sc)], axis=-1)


def block_attention(q, k, v, scale):
    B, Lq, Hkv, G, d = q.shape
    nb = Lq // Q_BLOCK
    kf, vf = _f32(k), _f32(v)
    qb = jnp.moveaxis(q.reshape(B, nb, Q_BLOCK, Hkv, G, d), 1, 0)

    def one_block(qi):
        s = jnp.einsum("bqhgd,bkhd->bhgqk", _f32(qi), kf) * scale
        p = jax.nn.softmax(s, axis=-1)
        return jnp.einsum("bhgqk,bkhe->bqhge", p, vf)

    o = lax.map(one_block, qb)
    return jnp.moveaxis(o, 0, 1).reshape(B, Lq, Hkv * G, v.shape[-1]).astype(v.dtype)


def attn_project(h, w_in, q_norm_w, k_norm_w, cq_norm_w, ckv_norm_w, w_uq, w_ukv, rope_a, rope_m):
    B, L, _ = h.shape
    q_a, k_a, v_a, c_q, c_kv, k_pe, gate = split_cols(h @ w_in, ATTN_SPLITS)
    q_a = rms_norm(q_a.reshape(B, L, GQA_HEADS, GQA_HEAD_DIM), q_norm_w)
    k_a = rms_norm(k_a.reshape(B, L, GQA_KV_HEADS, GQA_HEAD_DIM), k_norm_w)
    v_a = v_a.reshape(B, L, GQA_KV_HEADS, GQA_HEAD_DIM)
    q_m = (rms_norm(c_q, cq_norm_w) @ w_uq).reshape(B, L, MLA_HEADS, MLA_NOPE_DIM + MLA_ROPE_DIM)
    kv_m = (rms_norm(c_kv, ckv_norm_w) @ w_ukv).reshape(B, L, MLA_HEADS, MLA_NOPE_DIM + MLA_V_DIM)
    q_nope, q_pe = q_m[..., :MLA_NOPE_DIM], q_m[..., MLA_NOPE_DIM:]
    k_nope, v_m = kv_m[..., :MLA_NOPE_DIM], kv_m[..., MLA_NOPE_DIM:]
    k_pe = k_pe.reshape(B, L, 1, MLA_ROPE_DIM)
    if rope_a is not None:
        q_a = apply_axial_rope(q_a, rope_a)
        k_a = apply_axial_rope(k_a, rope_a)
        q_pe = apply_axial_rope(q_pe, rope_m)
        k_pe = apply_axial_rope(k_pe, rope_m)
    q_m = jnp.concatenate([q_nope, q_pe], axis=-1)
    k_m = jnp.concatenate([k_nope, jnp.broadcast_to(k_pe, (B, L, MLA_HEADS, MLA_ROPE_DIM))], axis=-1)
    return q_a, k_a, v_a, q_m, k_m, v_m, gate


def attn_mixer(h_lat, h_ctx, w_in, q_norm_w, k_norm_w, cq_norm_w, ckv_norm_w, w_uq, w_ukv, w_out, with_ctx_out):
    L = h_lat.shape[1]
    rope_a = axial_rope_tables(L, GQA_HEAD_DIM)
    rope_m = axial_rope_tables(L, MLA_ROPE_DIM)
    lq_a, lk_a, lv_a, lq_m, lk_m, lv_m, l_gate = attn_project(
        h_lat, w_in, q_norm_w, k_norm_w, cq_norm_w, ckv_norm_w, w_uq, w_ukv, rope_a, rope_m)
    cq_a, ck_a, cv_a, cq_m, ck_m, cv_m, c_gate = attn_project(
        h_ctx, w_in, q_norm_w, k_norm_w, cq_norm_w, ckv_norm_w, w_uq, w_ukv, None, None)

    def mix(q_a, q_m, gate, k_a, v_a, k_m, v_m):
        B, Lq = q_a.shape[:2]
        o_a = block_attention(q_a.reshape(B, Lq, GQA_KV_HEADS, GQA_HEADS // GQA_KV_HEADS, GQA_HEAD_DIM),
                              k_a, v_a, GQA_HEAD_DIM ** -0.5)
        o_m = block_attention(q_m[:, :, :, None, :], k_m, v_m, (MLA_NOPE_DIM + MLA_ROPE_DIM) ** -0.5)
        o = jnp.concatenate([o_a.reshape(B, Lq, -1), o_m.reshape(B, Lq, -1)], axis=-1)
        return (o * jax.nn.silu(gate)) @ w_out

    y_lat = mix(lq_a, lq_m, l_gate,
                jnp.concatenate([ck_a, lk_a], axis=1), jnp.concatenate([cv_a, lv_a], axis=1),
                jnp.concatenate([ck_m, lk_m], axis=1), jnp.concatenate([cv_m, lv_m], axis=1))
    y_ctx = mix(cq_a, cq_m, c_gate, ck_a, cv_a, ck_m, cv_m) if with_ctx_out else None
    return y_lat, y_ctx


def hyena_filters(L, w1, b1, w2, b2, w3, b3, freq):
    t = jnp.linspace(0.0, 1.0, L, dtype=jnp.float32)[:, None]
    w = (2.0 * math.pi / L) * jnp.arange(L, dtype=jnp.float32)[:, None]
    bands = jnp.linspace(1e-4, HY_BANDS - 1, HY_BANDS, dtype=jnp.float32)
    emb = jnp.concatenate([t, jnp.cos(w * bands), -jnp.sin(w * bands)], axis=-1)
    hid = jnp.sin(_f32(freq) * (emb @ _f32(w1) + _f32(b1)))
    hid = jnp.sin(_f32(freq) * (hid @ _f32(w2) + _f32(b2)))
    h = (hid @ _f32(w3) + _f32(b3)).reshape(L, HY_ORDER, 2, HY_WIDTH)
    max_decay = math.log(HY_DECAY_TARGET) / HY_FAST_DECAY
    min_decay = math.log(HY_DECAY_TARGET) / HY_SLOW_DECAY
    deltas = jnp.linspace(min_decay, max_decay, HY_WIDTH, dtype=jnp.float32)
    h = h * jnp.exp(-t * jnp.abs(deltas))[:, None, None, :]
    fwd, bwd = h[:, :, 0], h[:, :, 1]
    kern = jnp.concatenate([fwd, jnp.zeros((1, HY_ORDER, HY_WIDTH), jnp.float32), bwd[:0:-1]], axis=0)
    kern = kern / jnp.sum(jnp.abs(kern), axis=0, keepdims=True)
    return jnp.fft.rfft(kern, axis=0)


def long_conv(u, k_f, skip):
    L = u.shape[1]
    uf = _f32(u)
    y = jnp.fft.irfft(jnp.fft.rfft(uf, n=2 * L, axis=1) * k_f, n=2 * L, axis=1)[:, :L]
    return (y + uf * _f32(skip)).astype(u.dtype)


def short_conv(u, w, b):
    L = u.shape[1]
    pad = HY_SHORT // 2
    up = jnp.pad(u, ((0, 0), (pad, pad), (0, 0)))
    return sum(up[:, j:j + L] * w[j] for j in range(HY_SHORT)) + b


def hyena_mixer(h, w_in, conv_w, conv_b, f_w1, f_b1, f_w2, f_b2, f_w3, f_b3, freq, skip, w_out):
    L = h.shape[1]
    p = h @ w_in
    u = short_conv(p[..., :(HY_ORDER + 1) * HY_WIDTH], conv_w, conv_b)
    gate = p[..., (HY_ORDER + 1) * HY_WIDTH:]
    parts = jnp.split(u, HY_ORDER + 1, axis=-1)
    k_f = hyena_filters(L, f_w1, f_b1, f_w2, f_b2, f_w3, f_b3, freq)
    z = parts[0]
    for o in range(HY_ORDER):
        z = parts[o + 1] * long_conv(z, k_f[:, o], skip[o])
    return (z * jax.nn.silu(gate)) @ w_out


def setup_inputs(seed: int = 0) -> dict:
    key = jax.random.key(seed)
    ks = jax.random.split(key, 32)
    n_attn = (DEPTH + 1) // 2
    n_hy = DEPTH // 2
    f32 = jnp.float32

    def nrm(k, shape, fan_in):
        return jax.random.normal(k, shape, f32) * fan_in ** -0.5

    def gain(k, shape):
        return 1.0 + 0.05 * jax.random.normal(k, shape, f32)

    def small(k, shape, s=0.02):
        return s * jax.random.normal(k, shape, f32)

    return {
        "x": jax.random.normal(ks[0], (BATCH, SEQ, D_MODEL), f32),
        "c": jax.random.normal(ks[1], (BATCH, D_MODEL), f32),
        "ctx": jax.random.normal(ks[2], (BATCH, CTX_LEN, D_MODEL), f32),
        "c_ctx": jax.random.normal(ks[3], (D_MODEL,), f32),
        "ada_w": nrm(ks[4], (DEPTH, D_MODEL, 3 * D_MODEL), D_MODEL),
        "ada_b": small(ks[5], (DEPTH, 3 * D_MODEL)),
        "norm_w": gain(ks[6], (DEPTH, D_MODEL)),
        "attn_w_in": nrm(ks[7], (n_attn, D_MODEL, ATTN_IN), D_MODEL),
        "attn_q_norm": gain(ks[8], (n_attn, GQA_HEAD_DIM)),
        "attn_k_norm": gain(ks[9], (n_attn, GQA_HEAD_DIM)),
        "mla_q_norm": gain(ks[10], (n_attn, MLA_Q_RANK)),
        "mla_kv_norm": gain(ks[11], (n_attn, MLA_KV_RANK)),
        "mla_w_uq": nrm(ks[12], (n_attn, MLA_Q_RANK, MLA_HEADS * (MLA_NOPE_DIM + MLA_ROPE_DIM)), MLA_Q_RANK),
        "mla_w_ukv": nrm(ks[13], (n_attn, MLA_KV_RANK, MLA_HEADS * (MLA_NOPE_DIM + MLA_V_DIM)), MLA_KV_RANK),
        "attn_w_out": nrm(ks[14], (n_attn, ATTN_WIDTH, D_MODEL), ATTN_WIDTH),
        "hy_w_in": nrm(ks[15], (n_hy, D_MODEL, (HY_ORDER + 2) * HY_WIDTH), D_MODEL),
        "hy_conv_w": nrm(ks[16], (n_hy, HY_SHORT, (HY_ORDER + 1) * HY_WIDTH), HY_SHORT),
        "hy_conv_b": small(ks[17], (n_hy, (HY_ORDER + 1) * HY_WIDTH)),
        "hy_ffn_w1": nrm(ks[18], (n_hy, HY_EMB, HY_FFN), HY_EMB),
        "hy_ffn_b1": small(ks[19], (n_hy, HY_FFN), 0.1),
        "hy_ffn_w2": nrm(ks[20], (n_hy, HY_FFN, HY_FFN), HY_FFN),
        "hy_ffn_b2": small(ks[21], (n_hy, HY_FFN), 0.1),
        "hy_ffn_w3": nrm(ks[22], (n_hy, HY_FFN, HY_ORDER * 2 * HY_WIDTH), HY_FFN),
        "hy_ffn_b3": small(ks[23], (n_hy, HY_ORDER * 2 * HY_WIDTH), 0.1),
        "hy_freq": gain(ks[24], (n_hy, HY_FFN)),
        "hy_skip": small(ks[25], (n_hy, HY_ORDER, HY_WIDTH), 0.1),
        "hy_w_out": nrm(ks[26], (n_hy, HY_WIDTH, D_MODEL), HY_WIDTH),
        "final_norm_w": gain(ks[27], (D_MODEL,)),
    }


def reference(x, c, ctx, c_ctx, ada_w, ada_b, norm_w, attn_w_in, attn_q_norm, attn_k_norm,
              mla_q_norm, mla_kv_norm, mla_w_uq, mla_w_ukv, attn_w_out, hy_w_in, hy_conv_w, hy_conv_b,
              hy_ffn_w1, hy_ffn_b1, hy_ffn_w2, hy_ffn_b2, hy_ffn_w3, hy_ffn_b3, hy_freq, hy_skip,
              hy_w_out, final_norm_w):
    last_attn = (DEPTH - 1) - ((DEPTH - 1) % 2)
    s_lat = jax.nn.silu(c)
    s_ctx = jax.nn.silu(c_ctx)
    x_lat, x_ctx = x, ctx
    for i in range(DEPTH):
        j = i // 2
        ctx_update = i < last_attn
        shift, scale, gate = jnp.split(s_lat @ ada_w[i] + ada_b[i], 3, axis=-1)
        h_lat = modulate(x_lat, norm_w[i], shift[:, None], scale[:, None])
        need_ctx = (i % 2 == 0) or ctx_update
        if need_ctx:
            shift_c, scale_c, gate_c = jnp.split(s_ctx @ ada_w[i] + ada_b[i], 3, axis=-1)
            h_ctx = modulate(x_ctx, norm_w[i], shift_c, scale_c)
        if i % 2 == 0:
            y_lat, y_ctx = attn_mixer(h_lat, h_ctx, attn_w_in[j], attn_q_norm[j], attn_k_norm[j],
                                      mla_q_norm[j], mla_kv_norm[j], mla_w_uq[j], mla_w_ukv[j],
                                      attn_w_out[j], ctx_update)
        else:
            hy_args = (hy_w_in[j], hy_conv_w[j], hy_conv_b[j], hy_ffn_w1[j], hy_ffn_b1[j], hy_ffn_w2[j],
                       hy_ffn_b2[j], hy_ffn_w3[j], hy_ffn_b3[j], hy_freq[j], hy_skip[j], hy_w_out[j])
            y_lat = hyena_mixer(h_lat, *hy_args)
            y_ctx = hyena_mixer(h_ctx, *hy_args) if ctx_update else None
        x_lat = x_lat + gate[:, None] * y_lat
        if ctx_update:
            x_ctx = x_ctx + gate_c * y_ctx
    return rms_norm(x_lat, final_norm_w)
```

```python
import numpy as np
import concourse.bass as bass
import concourse.mybir as mybir
from concourse.bass_utils import run_bass_kernel_spmd
from contextlib import ExitStack

F32 = mybir.dt.float32
BF16 = mybir.dt.bfloat16
AF = mybir.ActivationFunctionType
ALU = mybir.AluOpType
AX = mybir.AxisListType

NDMA_SLOTS = 24


class _Op:
    __slots__ = ("eng", "idx", "fn", "deps", "is_dma", "dma_id", "waits", "signal", "snap", "semval")

    def __init__(self, eng, idx, fn, deps, is_dma, dma_id):
        self.eng = eng
        self.idx = idx
        self.fn = fn
        self.deps = deps
        self.is_dma = is_dma
        self.dma_id = dma_id
        self.waits = []
        self.signal = False
        self.snap = None
        self.semval = 0


class V:
    __slots__ = ("ap", "key")

    def __init__(self, ap, key):
        self.ap = ap
        self.key = key

    def __getitem__(self, idx):
        return V(self.ap[idx], self.key)

    def r(self, pat, **kw):
        return V(self.ap.rearrange(pat, **kw), self.key)

    def k(self, key):
        return V(self.ap, key)

    def bitcast(self, dt):
        return V(self.ap.bitcast(dt), self.key)


def U(a):
    return a.ap if isinstance(a, V) else a


def bc(a, pos, n):
    ap = U(a)
    dims = [list(d) for d in ap.ap]
    dims.insert(1 + pos, [0, n])
    r = bass.AP(ap.tensor, ap.offset, dims)
    return V(r, a.key) if isinstance(a, V) else r


class Prog:
    ENGS = ("pe", "act", "dve", "pool", "sp")

    def __init__(self, nc, same_engine_sync=True):
        self.nc = nc
        self.streams = {e: [] for e in self.ENGS}
        self.order = []
        self.last_writer = {}
        self.readers = {}
        self.ndma = 0
        self.dma_ops = []
        self.same_engine_sync = same_engine_sync
        self.out_dmas = []
        self.barrier_op = None

    @staticmethod
    def _keys(aps):
        ks = []
        for a in aps:
            if a is None or isinstance(a, (int, float)):
                continue
            if isinstance(a, (str, tuple)):
                ks.append(a)
            elif isinstance(a, V):
                ks.append(a.key)
            elif hasattr(a, "tensor"):
                ks.append(a.tensor.name)
            else:
                ks.append(a.name)
        return ks

    def add(self, eng, fn, rd, wr, is_dma=False):
        rd = self._keys(rd)
        wr = self._keys(wr)
        wr = wr + [k for k in rd if isinstance(k, str) and k.startswith("ps") and k not in wr]
        deps = []
        seen = set()

        def _dep(o):
            if o is not None and id(o) not in seen:
                seen.add(id(o))
                deps.append(o)

        for k in rd:
            _dep(self.last_writer.get(k))
        for k in wr:
            _dep(self.last_writer.get(k))
            for r in self.readers.get(k, ()):
                _dep(r)
        _dep(self.barrier_op)
        dma_id = None
        if is_dma:
            dma_id = self.ndma
            self.ndma += 1
            if dma_id >= NDMA_SLOTS:
                _dep(self.dma_ops[dma_id - NDMA_SLOTS])
        op = _Op(eng, len(self.streams[eng]), fn, deps, is_dma, dma_id)
        if is_dma:
            self.dma_ops.append(op)
        self.streams[eng].append(op)
        self.order.append(op)
        for k in wr:
            self.last_writer[k] = op
            self.readers[k] = []
        for k in rd:
            self.readers.setdefault(k, []).append(op)
        return op

    def mm(self, out, lhsT, rhs, start=True, stop=True, rd=None, wr=None, **kw):
        o_, l_, r_ = U(out), U(lhsT), U(rhs)
        return self.add("pe", lambda e: e.matmul(o_, l_, r_, start=start, stop=stop, **kw),
                        rd if rd is not None else [lhsT, rhs], wr if wr is not None else [out])

    def tr(self, out, in_, ident, rd=None, wr=None):
        o_, i_, d_ = U(out), U(in_), U(ident)
        return self.add("pe", lambda e: e.transpose(o_, i_, d_),
                        rd if rd is not None else [in_, ident], wr if wr is not None else [out])

    def tt(self, out, in0, in1, op, eng="dve"):
        o_, a_, b_ = U(out), U(in0), U(in1)
        return self.add(eng, lambda e: e.tensor_tensor(o_, a_, b_, op), [in0, in1], [out])

    def ts(self, out, in0, s1, s2, op0, op1=None, eng="dve"):
        o_, a_, s1_, s2_ = U(out), U(in0), U(s1), U(s2)
        if op1 is None:
            return self.add(eng, lambda e: e.tensor_scalar(o_, a_, s1_, None, op0), [in0, s1], [out])
        return self.add(eng, lambda e: e.tensor_scalar(o_, a_, s1_, s2_, op0, op1), [in0, s1, s2], [out])

    def stt(self, out, in0, sc, in1, op0, op1, eng="dve"):
        o_, a_, s_, b_ = U(out), U(in0), U(sc), U(in1)
        return self.add(eng, lambda e: e.scalar_tensor_tensor(o_, a_, s_, b_, op0, op1), [in0, sc, in1], [out])

    def cp(self, out, in_, eng="dve"):
        o_, i_ = U(out), U(in_)
        if eng == "act":
            return self.add(eng, lambda e: e.copy(o_, i_), [in_], [out])
        return self.add(eng, lambda e: e.tensor_copy(o_, i_), [in_], [out])

    def red(self, out, in_, op=None, axis=None, eng="dve"):
        o_, i_ = U(out), U(in_)
        op = op or ALU.add
        axis = axis or AX.X
        return self.add(eng, lambda e: e.tensor_reduce(o_, i_, axis, op), [in_], [out])

    def rcp(self, out, in_):
        o_, i_ = U(out), U(in_)
        return self.add("dve", lambda e: e.reciprocal(o_, i_), [in_], [out])

    def mset(self, out, val, eng="dve"):
        o_ = U(out)
        return self.add(eng, lambda e: e.memset(o_, val), [], [out])

    def barrier(self, scr_out, scr_in):
        deps = []
        for e in self.ENGS:
            if self.streams[e]:
                deps.append(self.streams[e][-1])
        last = {}
        for d in self.dma_ops:
            last[d.dma_id % NDMA_SLOTS] = d
        deps.extend(last.values())
        op = self.add("sp", lambda e: e.dma_start(out=scr_out, in_=scr_in), [], [], is_dma=True)
        ids = set(id(x) for x in op.deps)
        for d in deps:
            if id(d) not in ids and d is not op:
                op.deps.append(d)
        self.barrier_op = op
        return op

    def act(self, out, in_, func, bias=None, scale=None, accum_out=None, rd=None, wr=None, eng="act"):
        kw = {}
        if bias is not None:
            kw["bias"] = U(bias)
        if scale is not None:
            kw["scale"] = U(scale)
        if accum_out is not None:
            kw["accum_out"] = U(accum_out)
        r = [in_, bias, scale] if rd is None else rd
        w = [out, accum_out] if wr is None else wr
        o_, i_ = U(out), U(in_)
        return self.add(eng, lambda e: e.activation(o_, i_, func, **kw), r, w)

    def v(self, fn, rd, wr, eng="dve"):
        return self.add(eng, fn, rd, wr)

    def dma(self, out, in_, rd=None, wr=None, q=None, is_output=False, **kw):
        if q is None:
            q = "sp" if isinstance(out, V) else "pool"
        o_, i_ = U(out), U(in_)
        op = self.add(q, lambda e: e.dma_start(out=o_, in_=i_, **kw),
                      rd if rd is not None else [in_], wr if wr is not None else [out], is_dma=True)
        if is_output:
            self.out_dmas.append(op)
        return op

    def emit(self, stack):
        nc = self.nc
        esem = {e: stack.enter_context(nc.semaphore("s_" + e)) for e in self.ENGS}
        dsem = [stack.enter_context(nc.semaphore("d_%d" % i)) for i in range(NDMA_SLOTS)]
        know = {e: ({x: -1 for x in self.ENGS}, {}) for e in self.ENGS}
        for op in self.order:
            kc, kd = know[op.eng]
            for d in op.deps:
                if d.is_dma:
                    if kd.get(d.dma_id % NDMA_SLOTS, -1) >= d.dma_id:
                        continue
                    op.waits.append(d)
                    kd[d.dma_id % NDMA_SLOTS] = d.dma_id
                    sc, sd = d.snap
                    for x, v_ in sc.items():
                        if v_ > kc[x]:
                            kc[x] = v_
                    for x, v_ in sd.items():
                        if v_ > kd.get(x, -1):
                            kd[x] = v_
                else:
                    if d.eng == op.eng and (op.eng == "pe" or not self.same_engine_sync or op.is_dma and False):
                        continue
                    if kc[d.eng] >= d.idx:
                        continue
                    op.waits.append(d)
                    d.signal = True
                    sc, sd = d.snap
                    for x, v_ in sc.items():
                        if v_ > kc[x]:
                            kc[x] = v_
                    for x, v_ in sd.items():
                        if v_ > kd.get(x, -1):
                            kd[x] = v_
                    if d.idx > kc[d.eng]:
                        kc[d.eng] = d.idx
            if not op.is_dma:
                snapc = dict(kc)
                snapc[op.eng] = op.idx
                op.snap = (snapc, dict(kd))
            else:
                snapc = dict(kc)
                op.snap = (snapc, dict(kd))
        for e in self.ENGS:
            c = 0
            for op in self.streams[e]:
                if op.is_dma:
                    continue
                if op.signal:
                    c += 1
                op.semval = c
        nwaits = sum(len(o.waits) for o in self.order)
        self.stats = dict(nops=len(self.order), nwaits=nwaits, ndma=self.ndma,
                          per_eng={e: len(s) for e, s in self.streams.items()})

        def run_stream(e):
            def body(eng):
                for op in self.streams[e]:
                    for d in op.waits:
                        if d.is_dma:
                            eng.wait_ge(dsem[d.dma_id % NDMA_SLOTS], 16 * (d.dma_id // NDMA_SLOTS + 1))
                        else:
                            eng.wait_ge(esem[d.eng], d.semval)
                    ins = op.fn(eng)
                    if op.is_dma:
                        ins.then_inc(dsem[op.dma_id % NDMA_SLOTS], 16)
                    elif op.signal:
                        ins.then_inc(esem[e], 1)
                if e == "sp":
                    last = {}
                    for d in self.dma_ops:
                        last[d.dma_id % NDMA_SLOTS] = d
                    for s, d in last.items():
                        eng.wait_ge(dsem[s], 16 * (d.dma_id // NDMA_SLOTS + 1))
            return body

        with nc.Block() as block:
            block.tensor(run_stream("pe"))
            block.scalar(run_stream("act"))
            block.vector(run_stream("dve"))
            block.gpsimd(run_stream("pool"))
            block.sync(run_stream("sp"))


import math
import ml_dtypes

I32 = mybir.dt.int32
NT_LAT = 32
NT_ALL = 34
EPS = 1e-6

CV = dict(c=0, c_ctx=8, nw0=16, nw1=24, adab0=32, adab1=56, cw0=80, cw1=104, cw2=128, cb=152,
          sk0=176, sk1=184, b3=192, ndelta=224)


def _rope_tab(rot_dim):
    L, GW = 4096, 64
    rows = np.repeat(np.arange(L // GW, dtype=np.int32), GW).astype(np.float32)
    cols = np.tile(np.arange(GW, dtype=np.int32), L // GW).astype(np.float32)
    q = rot_dim // 4
    inv = (np.float32(10000.0) ** (-np.arange(q, dtype=np.float32) / np.float32(q))).astype(np.float32)
    ar = (rows[:, None] * inv).astype(np.float32)
    ac = (cols[:, None] * inv).astype(np.float32)
    cr, sr, cc, sc = np.cos(ar), np.sin(ar), np.cos(ac), np.sin(ac)
    C = np.concatenate([cr, cr, cc, cc], 1)
    S = np.concatenate([-sr, sr, -sc, sc], 1)
    return np.concatenate([C, S], 1).astype(np.float32)


_CONSTS = None


def host_consts():
    global _CONSTS
    if _CONSTS is not None:
        return _CONSTS
    bf = ml_dtypes.bfloat16
    c = {}
    c["ident_bf"] = np.eye(128, dtype=np.float32).astype(bf)
    c["ident_f"] = np.eye(128, dtype=np.float32)
    c["ropeA"] = _rope_tab(64)
    c["ropeM"] = _rope_tab(32)
    L = 4096
    n = np.arange(8192)
    tpos = np.where(n < L, n, np.where(n == L, 0, 8192 - n))
    tl = np.linspace(0.0, 1.0, L, dtype=np.float32)
    w = ((2.0 * math.pi / L) * np.arange(L, dtype=np.float32)).astype(np.float32)
    bands = np.linspace(1e-4, 15, 16, dtype=np.float32)
    emb = np.concatenate([tl[:, None], np.cos(w[:, None] * bands), -np.sin(w[:, None] * bands)], 1).astype(np.float32)
    c["embT"] = np.ascontiguousarray(emb[tpos].T)
    sgn = np.where(n < L, 1.0, np.where(n == L, 0.0, -1.0)).astype(np.float32)
    c["tlsgn"] = np.stack([tl[tpos], sgn]).astype(np.float32)
    a = np.arange(64)[:, None]
    kap = np.arange(64)[None, :]
    f1 = np.exp(-2j * np.pi * a * (2 * kap + 1) / 128.0)
    c["F1c"] = np.concatenate([f1.real, f1.imag], 1).astype(np.float32).astype(bf)
    b = np.arange(128)[:, None, None]
    kp = np.arange(64)[None, :, None]
    be = np.arange(64)[None, None, :]
    f2 = np.exp(-2j * np.pi * (b * be / 128.0 + b * (kp + 0.5) / 8192.0))
    f2a = np.concatenate([f2.real, f2.imag], 2)
    f2b = np.concatenate([-f2.imag, f2.real], 2)
    c["F2c"] = np.stack([f2a, f2b], 2).reshape(128, 64 * 2 * 128).astype(np.float32).astype(bf)
    be2 = np.arange(64)[:, None]
    bb = np.arange(128)[None, :]
    g = np.exp(2j * np.pi * bb * be2 / 128.0)
    G = np.zeros((128, 2, 2, 128), np.float64)
    for bh in range(2):
        gs = g[:, bh * 64:(bh + 1) * 64]
        g1 = np.concatenate([np.concatenate([gs.real, gs.imag], 1), np.concatenate([-gs.imag, gs.real], 1)], 0)
        g2 = np.concatenate([g1[64:], -g1[:64]], 0)
        G[:, bh, 0, :] = g1
        G[:, bh, 1, :] = g2
    c["Gc"] = G.reshape(128, 512).astype(np.float32).astype(bf)
    kq = np.arange(64)[:, None, None]
    b3 = np.arange(128)[None, :, None]
    a3 = np.arange(32)[None, None, :]
    h = (2.0 / 8192.0) * np.exp(2j * np.pi * (a3 * (2 * kq + 1) / 128.0 + b3 * (kq + 0.5) / 8192.0))
    c["Hc"] = np.concatenate([h.real, -h.imag], 0).reshape(128, 128 * 32).astype(np.float32).astype(bf)
    deltas = np.linspace(math.log(1e-2) / 1.5, math.log(1e-2) / 0.3, 1024, dtype=np.float32)
    c["_ndelta"] = (-np.abs(deltas)).astype(np.float32)
    _CONSTS = c
    return c


def pack_inputs(inp, b):
    c = host_consts()
    f = lambda a: np.ascontiguousarray(np.asarray(a, dtype=np.float32))
    vec = np.zeros((256, 128), np.float32)

    def put(name, arr):
        arr = f(arr).reshape(-1, 128)
        vec[CV[name]:CV[name] + arr.shape[0]] = arr

    put("c", inp["c"][b]); put("c_ctx", inp["c_ctx"]); put("nw0", inp["norm_w"][0]); put("nw1", inp["norm_w"][1])
    put("adab0", inp["ada_b"][0]); put("adab1", inp["ada_b"][1])
    put("cw0", inp["hy_conv_w"][0][0]); put("cw1", inp["hy_conv_w"][0][1]); put("cw2", inp["hy_conv_w"][0][2])
    put("cb", inp["hy_conv_b"][0]); put("sk0", inp["hy_skip"][0][0]); put("sk1", inp["hy_skip"][0][1])
    put("b3", inp["hy_ffn_b3"][0]); put("ndelta", c["_ndelta"])
    smallv = np.zeros((64, 4), np.float32)
    smallv[:, 0] = f(inp["hy_freq"][0]); smallv[:, 1] = f(inp["hy_ffn_b1"][0]); smallv[:, 2] = f(inp["hy_ffn_b2"][0])
    rowv = np.zeros((4, 1024), np.float32)
    rowv[0, :640] = np.concatenate([np.tile(f(inp["attn_q_norm"][0]), 8), np.tile(f(inp["attn_k_norm"][0]), 2)])
    rowv[0, 640:896] = f(inp["mla_q_norm"][0]); rowv[0, 896:1024] = f(inp["mla_kv_norm"][0])
    rowv[1] = f(inp["final_norm_w"]); rowv[2] = f(inp["ada_b"][0][2048:]); rowv[3] = f(inp["ada_b"][1][2048:])
    d = dict(x=f(inp["x"][b]), ctx=f(inp["ctx"][b]), vecs=vec, smallv=smallv, rowv=rowv,
             ada_w=f(inp["ada_w"]), w_in=f(inp["attn_w_in"][0]), w_uq=f(inp["mla_w_uq"][0]),
             w_ukv=f(inp["mla_w_ukv"][0]), w_out=f(inp["attn_w_out"][0]), hy_w_in=f(inp["hy_w_in"][0]),
             f_w1=f(inp["hy_ffn_w1"][0]), f_w2=f(inp["hy_ffn_w2"][0]), f_w3=f(inp["hy_ffn_w3"][0]),
             hy_w_out=f(inp["hy_w_out"][0]))
    for k in ("ident_bf", "ident_f", "ropeA", "ropeM", "embT", "tlsgn", "F1c", "F2c", "Gc", "Hc"):
        d[k] = c[k]
    return d


ARENA = 50000
DBGSET = {6: ('U_', 'GS2'), 5: ('KERN',), 2: ('QaT', 'KaT', 'QmT', 'KmT', 'Vm', 'Gs'), 3: ('Oall', 'XL1'), 7: ('KSP',), 8: ('ZF', 'Z1B')}


def build(dbg=False, stop_after=99):
    nc = bass.Bass("TRN2", target_bir_lowering=False)
    skind = "Internal"

    def din(name, shape, dt=F32):
        return nc.dram_tensor(name, list(shape), dt, kind="ExternalInput").ap()

    def dscr(name, shape, dt=F32):
        kind = "ExternalOutput" if (dbg and name in DBGSET.get(stop_after, ())) else "Internal"
        return nc.dram_tensor(name, list(shape), dt, kind=kind).ap()

    def ddbg(name, shape, dt=F32):
        return nc.dram_tensor(name, list(shape), dt, kind="ExternalOutput").ap()

    x = din("x", [4096, 1024]); ctx = din("ctx", [256, 1024])
    vecs = din("vecs", [256, 128]); smallv = din("smallv", [64, 4]); rowv = din("rowv", [4, 1024])
    ada_w = din("ada_w", [2, 1024, 3072]); w_in = din("w_in", [1024, 2208]); w_uq = din("w_uq", [256, 768])
    w_ukv = din("w_ukv", [128, 1024]); w_out = din("w_out", [1024, 1024]); hy_w_in = din("hy_w_in", [1024, 4096])
    f_w1 = din("f_w1", [33, 64]); f_w2 = din("f_w2", [64, 64]); f_w3 = din("f_w3", [64, 4096])
    hy_w_out = din("hy_w_out", [1024, 1024])
    ident_bf_d = din("ident_bf", [128, 128], BF16); ident_f_d = din("ident_f", [128, 128])
    ropeA = din("ropeA", [4096, 128]); ropeM = din("ropeM", [4096, 64]); embT = din("embT", [33, 8192])
    tlsgn = din("tlsgn", [2, 8192]); F1c = din("F1c", [64, 128], BF16); F2c = din("F2c", [128, 16384], BF16)
    Gc = din("Gc", [128, 512], BF16); Hc = din("Hc", [128, 4096], BF16)
    out = nc.dram_tensor("out", [4096, 1024], F32, kind="ExternalOutput").ap()

    QaT = dscr("QaT", [8, 64, 4096], BF16); KaT = dscr("KaT", [2, 64, 4352], BF16)
    Va = dscr("Va", [2, 128, 34, 66], BF16)
    QmT = dscr("QmT", [8, 96, 4096], BF16); KmT = dscr("KmT", [8, 96, 4352], BF16)
    Vm = dscr("Vm", [8, 128, 34, 66], BF16)
    Gs = dscr("Gs", [4096, 1024]); Oall = dscr("Oall", [4096, 1024]); XL1 = dscr("XL1", [4096, 1024])
    bscr = dscr("bscr", [2, 16])

    st = ExitStack()
    with st:
        Aten = st.enter_context(nc.sbuf_tensor("A", [128, ARENA], F32))
        PSt = [st.enter_context(nc.psum_tensor("ps%d" % i, [128, 512], F32)) for i in range(8)]
        PS = [V(t[:], "ps%d" % i) for i, t in enumerate(PSt)]
        PSb = [V(t[:].bitcast(BF16), "ps%d" % i) for i, t in enumerate(PSt)]
        P = Prog(nc)
        al = {"off": 0, "base": 0, "n": 0}

        def alloc(name, ncols, parts=128, dt=F32):
            nf = ncols if dt == F32 or dt == I32 else (ncols + 1) // 2
            nf = (nf + 7) // 8 * 8
            o = al["off"]
            assert o + nf <= ARENA, (name, o, nf)
            al["off"] = o + nf
            ap = Aten[0:parts, o:o + nf]
            if dt != F32:
                ap = ap.bitcast(dt)
            ap = ap[:, 0:ncols]
            al["n"] += 1
            return V(ap, "%s#%d" % (name, al["n"]))

        def phase_reset():
            P.barrier(bscr[0:1, :], bscr[1:2, :])
            al["off"] = al["base"]

        ident_bf = alloc("identb", 128, dt=BF16); ident_f = alloc("identf", 128)
        colv = alloc("colv", 256); smv = alloc("smv", 4, parts=64)
        s2 = alloc("s2", 16); sbc = alloc("sbc", 1024)
        ada_sb = [alloc("ada0", 48), alloc("ada1", 48)]
        gate_bc = [alloc("gbc0", 1024), alloc("gbc1", 1024)]
        nw_bc = alloc("nwbc", 1024); fnw_bc = alloc("fnwbc", 1024)
        Acol = [[alloc("A00", 8), alloc("A01", 8)], [alloc("A10", 8), None]]
        epsc = alloc("eps", 1)
        al["base"] = al["off"]

        P.dma(ident_bf, ident_bf_d); P.dma(ident_f, ident_f_d); P.dma(smv, smallv)
        P.dma(nw_bc, rowv[0].partition_broadcast(128)); P.dma(fnw_bc, rowv[1].partition_broadcast(128))
        P.mset(epsc, EPS)

        vrow = alloc("vrow", 128)
        for j in range(2):
            P.dma(vrow, vecs[j * 128:(j + 1) * 128, :])
            P.tr(PS[0][:, 0:128], vrow, ident_f)
            P.cp(colv[:, j * 128:(j + 1) * 128], PS[0][:, 0:128])
        s2v = s2.r("p (k j) -> p k j", j=2)
        P.act(s2v[:, :, 0], colv[:, 0:8], AF.Silu)
        P.act(s2v[:, :, 1], colv[:, 8:16], AF.Silu)
        sbcv = sbc.r("p (k m) -> p k m", m=128)
        P.cp(sbcv, bc(s2v[:, :, 0], 2, 128))
        awc = [alloc("awc0", 1024), alloc("awc1", 1024)]
        gb_tmp = alloc("gbtmp", 1024)
        for l in range(2):
            awl = ada_w[l].rearrange("(kt p) n -> p kt n", p=128)
            P.dma(gb_tmp, rowv[2 + l].partition_broadcast(128))
            for m in range(24):
                cw = awc[m % 2]
                cwv = cw.r("p (k n) -> p k n", n=128)
                P.dma(cwv, awl[:, :, m * 128:(m + 1) * 128])
                for kt in range(8):
                    P.mm(PS[1][:, 2 * m:2 * m + 2], cwv[:, kt, :], s2v[:, kt, :], start=(kt == 0), stop=(kt == 7))
                if m >= 16:
                    g = m - 16
                    for kt in range(8):
                        P.mm(PS[2 + g // 4][:, (g % 4) * 128:(g % 4 + 1) * 128], sbcv[:, kt, :], cwv[:, kt, :],
                             start=(kt == 0), stop=(kt == 7))
            adv = ada_sb[l].r("p (m j) -> p m j", j=2)
            P.tt(adv, PS[1][:, 0:48].r("p (m j) -> p m j", j=2), bc(colv[:, CV["adab%d" % l]:CV["adab%d" % l] + 24], 1, 2), ALU.add)
            P.tt(gate_bc[l][:, 0:512], PS[2], gb_tmp[:, 0:512], ALU.add)
            P.tt(gate_bc[l][:, 512:1024], PS[3], gb_tmp[:, 512:1024], ALU.add)
            for j in range(2):
                if Acol[l][j] is None:
                    continue
                P.ts(Acol[l][j], adv[:, 8:16, j], 1.0, None, ALU.add)
                P.tt(Acol[l][j], Acol[l][j], colv[:, CV["nw%d" % l]:CV["nw%d" % l] + 8], ALU.mult)
        if stop_after <= 0:
            dbgo = ddbg("dbg0", [128, 1024 + 96 + 256])
            P.dma(dbgo[:, 0:1024], gate_bc[0]); P.dma(dbgo[:, 1024:1072], ada_sb[0]); P.dma(dbgo[:, 1072:1120], ada_sb[1])
            P.dma(dbgo[:, 1120:1376], colv)
            P.emit(st)
            return nc, P

        U_ = dscr("U_", [3072, 4096]); Vbf = dscr("Vbf", [1024, 4096], BF16); GS2 = dscr("GS2", [1024, 4096])
        KERN = dscr("KERN", [2, 1024, 8192], BF16); KSP = dscr("KSP", [2, 16, 128, 4096], BF16)
        Z1B = dscr("Z1B", [1024, 4096], BF16); ZF = dscr("ZF", [1024, 4096], BF16)
        import os
        RUN_L0 = not int(os.environ.get('ONLY_HY', 0))
        if RUN_L0:
            phase_reset()
            hT = alloc("hT", 8 * 4352, dt=BF16)
            hTv = hT.r("p (k t) -> p k t", k=8)

            def make_hT_builder():
                bufs = dict(xts=[alloc("xt%d" % i, 1024) for i in range(3)], junk=alloc("junk", 1024),
                            ss=[alloc("ss%d" % i, 1) for i in range(3)], rs=[alloc("rs%d" % i, 1) for i in range(3)],
                            xs=[alloc("xs%d" % i, 1024, dt=BF16) for i in range(2)])

                def tile(i, src, l, j, hTv_, key):
                    xt = bufs["xts"][i % 3]
                    if not isinstance(src, V):
                        P.dma(xt, src)
                    else:
                        xt = src
                    junk, ss, rs, xs = bufs["junk"], bufs["ss"], bufs["rs"], bufs["xs"]
                    P.act(junk, xt, AF.Square)
                    P.red(ss[i % 3], junk)
                    P.act(rs[i % 3], ss[i % 3], AF.Sqrt, scale=1.0 / 1024.0, bias=epsc)
                    P.rcp(rs[i % 3], rs[i % 3])
                    P.ts(xs[i % 2], xt, rs[i % 3], None, ALU.mult)
                    pb = PSb[6 + (i % 2)]
                    for kt in range(8):
                        P.tr(pb[:, kt * 128:(kt + 1) * 128], xs[i % 2][:, kt * 128:(kt + 1) * 128], ident_bf)
                    adv_ = ada_sb[l].r("p (m j) -> p m j", j=2)
                    for kt in range(8):
                        dst = hTv_[:, kt, i * 128:(i + 1) * 128].k((key, i))
                        if kt % 2 == 0:
                            P.act(dst, pb[:, kt * 128:(kt + 1) * 128], AF.Identity, scale=Acol[l][j][:, kt:kt + 1], bias=adv_[:, kt, j:j + 1])
                        else:
                            P.ts(dst, pb[:, kt * 128:(kt + 1) * 128], Acol[l][j][:, kt:kt + 1], adv_[:, kt, j:j + 1], ALU.mult, ALU.add)
                return tile

            def build_hT(specs, hTv_):
                tile = make_hT_builder()
                for i, (src, l, j) in enumerate(specs):
                    tile(i, src, l, j, hTv_, "hT")

            specs0 = [(ctx[i * 128:(i + 1) * 128, :], 0, 1) for i in range(2)] + [(x[i * 128:(i + 1) * 128, :], 0, 0) for i in range(32)]
            mark = al["off"]
            build_hT(specs0, hTv)
            if stop_after <= 1:
                dbgo = ddbg("dbg1", [128, 8 * 4352], BF16)
                P.dma(dbgo, hT, rd=[("hT", i) for i in range(34)])
                P.emit(st)
                return nc, P

            P.barrier(bscr[0:1, :], bscr[1:2, :])
            al["off"] = mark
            w_in_b = alloc("w_in_b", 8 * 2208, dt=BF16)
            w_in_bv = w_in_b.r("p (k n) -> p k n", k=8)
            wst = [alloc("wst%d" % i, 2208) for i in range(2)]
            for kt in range(8):
                P.dma(wst[kt % 2], w_in[kt * 128:(kt + 1) * 128, :])
                if kt % 2 == 0:
                    P.cp(w_in_bv[:, kt, :], wst[kt % 2], eng="act")
                else:
                    P.cp(w_in_bv[:, kt, :], wst[kt % 2])
            w_uq_b = alloc("w_uq_b", 2 * 768, dt=BF16)
            w_uq_bv = w_uq_b.r("p (k n) -> p k n", k=2)
            for kt in range(2):
                P.dma(wst[kt % 2][:, 0:768], w_uq[kt * 128:(kt + 1) * 128, :])
                P.cp(w_uq_bv[:, kt, :], wst[kt % 2][:, 0:768])
            w_ukv_b = alloc("w_ukv_b", 1024, dt=BF16)
            P.dma(wst[0][:, 0:1024], w_ukv)
            P.cp(w_ukv_b, wst[0][:, 0:1024])

            pr = wst[0]
            sq = alloc("sq", 640)
            ss10 = alloc("ss10", 16); rs10 = alloc("rs10", 16)
            qn = alloc("qn", 640); t1 = alloc("t1", 640); t2 = alloc("t2", 640)
            qkb = alloc("qkb", 640, dt=BF16)
            rp = [alloc("rp%d" % i, 192) for i in range(2)]
            cqn = alloc("cqn", 384); cqb = alloc("cqb", 384, dt=BF16)
            cT = alloc("cT", 384, dt=BF16)
            qm = alloc("qm", 768); qmb = alloc("qmb", 768, dt=BF16)
            kpe = alloc("kpe", 32); kt1 = alloc("kt1", 32); kt2 = alloc("kt2", 32)
            kmb = alloc("kmb", 768, dt=BF16)
            vab = alloc("vab", 132, dt=BF16); vmb = alloc("vmb", 528, dt=BF16)
            gsb = [alloc("gsb0", 1024)] * 2
            stQa = [alloc("stQa0", 8 * 256, parts=64, dt=BF16)]
            stKa = [alloc("stKa0", 2 * 256, parts=64, dt=BF16)]
            stQm = [alloc("stQm0", 8 * 256, parts=96, dt=BF16)]
            stKm = [alloc("stKm0", 8 * 256, parts=96, dt=BF16)]
            P.mset(vab, 1.0)
            P.mset(vmb, 1.0)

            def rope(dst, src, tab, nh, D, tmp1, tmp2):
                q = D // 4
                sv = src.r("p (h f) -> p h f", f=D)
                Cb = bc(tab[:, 0:D], 0, nh)
                P.tt(tmp1.r("p (h f) -> p h f", f=D), sv, Cb, ALU.mult)
                s5 = src.r("p (h a s f) -> p (h a) s f", a=2, s=2, f=q)
                t5 = tmp2.r("p (h a s f) -> p (h a) s f", a=2, s=2, f=q)
                S4 = tab[:, D:2 * D].r("p (a s f) -> p a s f", a=2, s=2)
                for s_ in range(2):
                    for a_ in range(2):
                        srcv = src.r("p (h a s f) -> p h a s f", a=2, s=2, f=q)[:, :, a_, 1 - s_, :]
                        dstv = tmp2.r("p (h a s f) -> p h a s f", a=2, s=2, f=q)[:, :, a_, s_, :]
                        Sb = bc(S4[:, a_, s_, :], 0, nh)
                        P.tt(dstv, srcv, Sb, ALU.mult)
                P.tt(dst, tmp1, tmp2, ALU.add)

            import os
            for tt_ in range(int(os.environ.get('P2_TILES', NT_ALL))):
                lat = tt_ >= 2
                STG = int(os.environ.get('P2_STAGE', 99))
                li = tt_ - 2
                grp = 0
                sl = tt_ % 2
                chunks = [(0, 512), (512, 1024), (1024, 1536), (1536, 2048), (2048, 2208)]
                nchunk = 5 if lat else 3
                for ci in range(nchunk):
                    c0, c1 = chunks[ci]
                    for kt in range(8):
                        P.mm(PS[ci][:, 0:c1 - c0], hTv[:, kt, tt_ * 128:(tt_ + 1) * 128].k(("hT", tt_)), w_in_bv[:, kt, c0:c1],
                             start=(kt == 0), stop=(kt == 7))
                    if ci % 2 == 0:
                        P.cp(pr[:, c0:c1], PS[ci][:, 0:c1 - c0], eng="act")
                    else:
                        P.cp(pr[:, c0:c1], PS[ci][:, 0:c1 - c0])
                if STG <= 1:
                    continue
                if lat:
                    P.dma(rp[tt_ % 2][:, 0:128], ropeA[li * 128:(li + 1) * 128, :])
                    P.dma(rp[tt_ % 2][:, 128:192], ropeM[li * 128:(li + 1) * 128, :])
                h0 = 0 if lat else 512
                nh = 10 if lat else 2
                P.tt(sq[:, h0:640], pr[:, h0:640], pr[:, h0:640], ALU.mult)
                P.red(ss10[:, 0:nh], sq[:, h0:640].r("p (h f) -> p h f", f=64))
                P.act(rs10[:, 0:nh], ss10[:, 0:nh], AF.Sqrt, scale=1.0 / 64.0, bias=epsc)
                P.rcp(rs10[:, 0:nh], rs10[:, 0:nh])
                P.tt(qn[:, h0:640].r("p (h f) -> p h f", f=64), pr[:, h0:640].r("p (h f) -> p h f", f=64), bc(rs10[:, 0:nh], 1, 64), ALU.mult)
                if lat:
                    P.tt(qn, qn, nw_bc[:, 0:640], ALU.mult)
                    rope(qkb, qn, rp[tt_ % 2][:, 0:128], 10, 64, t1, t2)
                else:
                    P.tt(qkb[:, 512:640], qn[:, 512:640], nw_bc[:, 512:640], ALU.mult)
                if STG <= 2:
                    continue
                pb = PSb[5]
                for h in range(h0 // 64, 10):
                    P.tr(pb[0:64, (h % 8) * 128:(h % 8 + 1) * 128] if h < 8 else PSb[6][0:64, (h - 8) * 128:(h - 7) * 128],
                         qkb[:, h * 64:(h + 1) * 64], ident_bf)
                if lat:
                    P.cp(stQa[grp].r("p (h t) -> p h t", h=8)[:, :, sl * 128:(sl + 1) * 128], pb[0:64, :].r("p (h t) -> p h t", h=8), eng="act")
                P.cp(stKa[grp].r("p (h t) -> p h t", h=2)[:, :, sl * 128:(sl + 1) * 128], PSb[6][0:64, 0:256].r("p (h t) -> p h t", h=2))
                if STG <= 3:
                    continue
                P.cp(vab.r("p (g d) -> p g d", d=66)[:, :, 0:64], pr[:, 640:768].r("p (g d) -> p g d", d=64), eng="act")
                P.dma(Va[:, :, tt_, :].rearrange("g p d -> p g d"), vab.r("p (g d) -> p g d", d=66))
                if STG <= 4:
                    continue
                P.tt(sq[:, 0:384], pr[:, 768:1152], pr[:, 768:1152], ALU.mult)
                P.red(ss10[:, 10:11], sq[:, 0:256]); P.red(ss10[:, 11:12], sq[:, 256:384])
                P.act(rs10[:, 10:11], ss10[:, 10:11], AF.Sqrt, scale=1.0 / 256.0, bias=epsc)
                P.act(rs10[:, 11:12], ss10[:, 11:12], AF.Sqrt, scale=1.0 / 128.0, bias=epsc)
                P.rcp(rs10[:, 10:12], rs10[:, 10:12])
                P.stt(cqb[:, 0:256], pr[:, 768:1024], rs10[:, 10:11], nw_bc[:, 640:896], ALU.mult, ALU.mult)
                P.stt(cqb[:, 256:384], pr[:, 1024:1152], rs10[:, 11:12], nw_bc[:, 896:1024], ALU.mult, ALU.mult)
                pb7 = PSb[7]
                j0 = 0 if lat else 2
                for j in range(j0, 3):
                    P.tr(pb7[:, j * 128:(j + 1) * 128], cqb[:, j * 128:(j + 1) * 128], ident_bf)
                P.cp(cT[:, j0 * 128:384], pb7[:, j0 * 128:384], eng="act")
                if STG <= 5:
                    continue
                for hh in range(2):
                    P.mm(PS[hh], cT[:, 256:384], w_ukv_b[:, hh * 512:(hh + 1) * 512])
                if lat:
                    rope(kpe, pr[:, 1152:1184], rp[tt_ % 2][:, 128:192], 1, 32, kt1, kt2)
                    kpe_src = kpe
                else:
                    kpe_src = pr[:, 1152:1184]
                SUB = int(os.environ.get('P2_SUB', 99))
                if SUB <= 0:
                    continue
                kmv = kmb.r("p (h f) -> p h f", f=96)
                for hh in range(2):
                    psv = PS[hh].r("p (h f) -> p h f", f=128)
                    if hh == 0:
                        P.cp(kmv[:, 0:4, 0:64], psv[:, :, 0:64], eng="act")
                    else:
                        P.cp(kmv[:, 4:8, 0:64], psv[:, :, 0:64])
                    if SUB <= 1:
                        continue
                    P.cp(vmb.r("p (h f) -> p h f", f=66)[:, hh * 4:(hh + 1) * 4, 0:64], psv[:, :, 64:128], eng="act")
                if SUB <= 2:
                    continue
                for h in range(8):
                    P.cp(kmv[:, h, 64:96], kpe_src, eng=("act" if h % 2 else "dve"))
                if STG <= 6:
                    continue
                P.dma(Vm[:, :, tt_, :].rearrange("h p d -> p h d"), vmb.r("p (h d) -> p h d", d=66))
                if STG <= 7:
                    continue
                for h in range(8):
                    P.tr(PSb[2 + h // 4][0:96, (h % 4) * 128:(h % 4 + 1) * 128] if False else PSb[3][0:96, h * 128:(h + 1) * 128], kmb[:, h * 96:(h + 1) * 96], ident_bf)
                P.cp(stKm[grp].r("p (h t) -> p h t", h=8)[:, :, sl * 128:(sl + 1) * 128], PSb[3][0:96, :].r("p (h t) -> p h t", h=8), eng="act")
                if lat:
                    for (c0, c1, bank) in ((0, 512, 2), (512, 768, 4)):
                        for j in range(2):
                            P.mm(PS[bank][:, 0:c1 - c0], cT[:, j * 128:(j + 1) * 128], w_uq_bv[:, j, c0:c1], start=(j == 0), stop=(j == 1))
                        P.cp(qm[:, c0:c1], PS[bank][:, 0:c1 - c0], eng=("act" if bank == 2 else "dve"))
                    qmv = qm.r("p (h f) -> p h f", f=96)
                    qpe = t1[:, 0:256]; qpo = t1[:, 256:512]
                    P.cp(qpe.r("p (h f) -> p h f", f=32), qmv[:, :, 64:96])
                    rope(qpo, qpe, rp[tt_ % 2][:, 128:192], 8, 32, t2[:, 0:256], t2[:, 256:512])
                    qmbv = qmb.r("p (h f) -> p h f", f=96)
                    P.cp(qmbv[:, :, 0:64], qmv[:, :, 0:64], eng="act")
                    P.cp(qmbv[:, :, 64:96], qpo.r("p (h f) -> p h f", f=32))
                    for h in range(8):
                        P.tr(PSb[4][0:96, h * 128:(h + 1) * 128], qmb[:, h * 96:(h + 1) * 96], ident_bf)
                    P.cp(stQm[grp].r("p (h t) -> p h t", h=8)[:, :, sl * 128:(sl + 1) * 128], PSb[4][0:96, :].r("p (h t) -> p h t", h=8))
                    P.act(gsb[tt_ % 2], pr[:, 1184:2208], AF.Silu)
                    P.dma(Gs[li * 128:(li + 1) * 128, :], gsb[tt_ % 2])
                flush = (sl == 1)
                if flush:
                    g0 = (tt_ // 2) * 2
                    ntk = tt_ - g0 + 1
                    kcol0 = g0 * 128
                    P.dma(KaT[:, :, kcol0:kcol0 + ntk * 128].rearrange("h d t -> d h t"), stKa[grp].r("p (h t) -> p h t", h=2)[:, :, 0:ntk * 128])
                    P.dma(KmT[:, :, kcol0:kcol0 + ntk * 128].rearrange("h d t -> d h t"), stKm[grp].r("p (h t) -> p h t", h=8)[:, :, 0:ntk * 128])
                    if lat:
                        t_first = max(g0, 2)
                        so = (t_first - g0) * 128
                        nq = tt_ - t_first + 1
                        qc0 = (t_first - 2) * 128
                        P.dma(QaT[:, :, qc0:qc0 + nq * 128].rearrange("h d t -> d h t"), stQa[grp].r("p (h t) -> p h t", h=8)[:, :, so:so + nq * 128])
                        P.dma(QmT[:, :, qc0:qc0 + nq * 128].rearrange("h d t -> d h t"), stQm[grp].r("p (h t) -> p h t", h=8)[:, :, so:so + nq * 128])
            if stop_after <= 2:
                P.barrier(bscr[0:1, :], bscr[1:2, :])
                P.emit(st)
                return nc, P

            phase_reset()
            Kb = [alloc("Kb%d" % i, 4352, parts=96, dt=BF16) for i in range(2)]
            Vb_ = [alloc("Vb%d" % i, 34 * 66, dt=BF16) for i in range(2)]
            Qb = [alloc("Qb%d" % i, 512, parts=96, dt=BF16) for i in range(2)]
            Pt = [alloc("Pt%d" % i, 512, dt=BF16) for i in range(3)]
            osb = [alloc("osb%d" % i, 256) for i in range(2)]
            rc = [alloc("rc%d" % i, 4) for i in range(2)]
            OTs = [alloc("OTs%d" % i, 512, parts=66) for i in range(2)]
            NQT = int(os.environ.get('P3_QT', 32))
            NGRP = int(os.environ.get('P3_GRPS', 10))

            def load_kv(g):
                kb__ = Kb[g % 2]; vbv__ = Vb_[g % 2].r("p (t d) -> p t d", d=66)
                if g < 2:
                    P.dma(kb__[0:64, :], KaT[g]); P.dma(vbv__, Va[g])
                else:
                    P.dma(kb__[0:96, :], KmT[g - 2]); P.dma(vbv__, Vm[g - 2])

            units = []
            for g in range(NGRP):
                for u in range(NQT if g < 2 else NQT // 4):
                    units.append((g, u))

            def load_q(ui):
                g, u = units[ui]
                qb_ = Qb[ui % 2]
                if g < 2:
                    P.dma(qb_[0:64, :].r("p (h t) -> p h t", h=4),
                          QaT[4 * g:4 * g + 4, :, u * 128:(u + 1) * 128].rearrange("h d t -> d h t"))
                else:
                    P.dma(qb_[0:96, :], QmT[g - 2][:, u * 512:(u + 1) * 512])

            load_kv(0)
            load_q(0)
            itc = {"n": 0}
            for ui, (g, u) in enumerate(units):
                gqa = g < 2
                kd = 64 if gqa else 96
                scale = (64.0 ** -0.5) if gqa else (96.0 ** -0.5)
                kb_ = Kb[g % 2]
                vbv = Vb_[g % 2].r("p (t d) -> p t d", d=66)
                qb = Qb[ui % 2]
                if u == 0 and g + 1 < NGRP:
                    load_kv(g + 1)
                if ui + 1 < len(units):
                    load_q(ui + 1)
                po = PS[2 + ui % 2]
                it0 = itc["n"]

                def qk(kb):
                    it = it0 + kb
                    P.mm(PS[it % 2], kb_[0:kd, kb * 128:(kb + 1) * 128], qb[0:kd, :])
                    P.act(Pt[it % 3], PS[it % 2], AF.Exp, scale=scale)

                qk(0)
                for kb in range(34):
                    if kb + 1 < 34:
                        qk(kb + 1)
                    pt = Pt[(it0 + kb) % 3]
                    for h in range(4):
                        P.mm(po[:, h * 66:(h + 1) * 66], pt[:, h * 128:(h + 1) * 128], vbv[:, kb, :], start=(kb == 0 and h == 0), stop=(kb == 33), skip_group_check=True)
                itc["n"] += 34
                pov = po[:, 0:264].r("p (h d) -> p h d", d=66)
                P.rcp(rc[ui % 2], pov[:, :, 64])
                ob = osb[ui % 2]
                P.tt(ob.r("p (h d) -> p h d", d=64), pov[:, :, 0:64], bc(rc[ui % 2], 1, 64), ALU.mult)
                if gqa:
                    P.dma(Oall[u * 128:(u + 1) * 128, g * 256:(g + 1) * 256], ob)
                else:
                    hcol = 512 + (g - 2) * 64
                    P.dma(Oall[u * 512:(u + 1) * 512, hcol:hcol + 64].rearrange("(qi p) d -> p qi d", p=128), ob.r("p (h d) -> p h d", d=64))

            phase_reset()
            h1T = alloc("h1T", 8 * 4096, dt=BF16)
            h1Tv = h1T.r("p (k t) -> p k t", k=8)
            mark_h1 = al["off"]
            hT1_tile = make_hT_builder()
            wo_b = alloc("wo_b", 8 * 1024, dt=BF16)
            wo_bv = wo_b.r("p (k n) -> p k n", k=8)
            wst2 = [alloc("wst2_%d" % i, 1024) for i in range(2)]
            for kt in range(8):
                P.dma(wst2[kt % 2], w_out[kt * 128:(kt + 1) * 128, :])
                P.cp(wo_bv[:, kt, :], wst2[kt % 2], eng=("act" if kt % 2 else "dve"))
            ot = [alloc("ot%d" % i, 1024) for i in range(2)]
            gt = [alloc("gt%d" % i, 1024) for i in range(2)]
            xt2 = [alloc("xt2_%d" % i, 1024) for i in range(2)]
            ogb = alloc("ogb", 1024, dt=BF16)
            ogT = alloc("ogT", 1024, dt=BF16)
            xl = [alloc("xl%d" % i, 1024) for i in range(2)]
            for i in range(NQT if int(os.environ.get('P3_B', 1)) else 0):
                b2 = i % 2
                P.dma(ot[b2], Oall[i * 128:(i + 1) * 128, :])
                P.dma(gt[b2], Gs[i * 128:(i + 1) * 128, :])
                P.dma(xt2[b2], x[i * 128:(i + 1) * 128, :])
                P.tt(ogb, ot[b2], gt[b2], ALU.mult)
                for ct in range(8):
                    P.tr(PSb[5][:, ct * 128:(ct + 1) * 128], ogb[:, ct * 128:(ct + 1) * 128], ident_bf)
                P.cp(ogT, PSb[5], eng="act")
                ogTv = ogT.r("p (k t) -> p k t", k=8)
                for hh in range(2):
                    for ct in range(8):
                        P.mm(PS[hh], ogTv[:, ct, :], wo_bv[:, ct, hh * 512:(hh + 1) * 512], start=(ct == 0), stop=(ct == 7))
                    P.tt(xl[b2][:, hh * 512:(hh + 1) * 512], PS[hh], gate_bc[0][:, hh * 512:(hh + 1) * 512], ALU.mult)
                P.tt(xl[b2], xl[b2], xt2[b2], ALU.add)
                P.dma(XL1[i * 128:(i + 1) * 128, :], xl[b2])
                hT1_tile(i, xl[b2], 1, 0, h1Tv, "h1T")
            if stop_after <= 3:
                dbgo = ddbg("dbg3", [128, 8 * 4096], BF16)
                P.dma(dbgo, h1T, rd=[("h1T", i) for i in range(NQT)])
                P.emit(st)
                return nc, P

            NT6 = int(os.environ.get('P6_TILES', 32))
            P.barrier(bscr[0:1, :], bscr[1:2, :])
            al["off"] = mark_h1
            hwl = hy_w_in.rearrange("(kt p) n -> p kt n", p=128)
            wst6 = [alloc("wst6_%d" % i, 1024) for i in range(2)]
            wcb = [alloc("wcb%d" % i, 1024, dt=BF16) for i in range(2)]
            psb = [alloc("psb%d" % i, 4104) for i in range(2)]
            ub = [alloc("ub%d" % i, 4096) for i in range(2)]
            vbf_sb = alloc("vbf_sb", 4096, dt=BF16)
            for i in range(2):
                P.mset(psb[i][:, 0:1], 0.0)
                P.mset(psb[i][:, 4097:4098], 0.0)
            for j in list(range(NT6)) if NT6 == 32 else [0, 8, 16, 24][:NT6]:
                b2 = j % 2
                P.dma(wst6[b2].r("p (k n) -> p k n", k=8), hwl[:, :, j * 128:(j + 1) * 128])
                P.cp(wcb[b2], wst6[b2], eng=("act" if b2 else "dve"))
                wv = wcb[b2].r("p (k n) -> p k n", k=8)
                for tc in range(8):
                    pp = PS[tc % 4]
                    for kt in range(8):
                        P.mm(pp, wv[:, kt, :], h1Tv[:, kt, tc * 512:(tc + 1) * 512],
                             start=(kt == 0), stop=(kt == 7), rd=[wcb[b2]] + [("h1T", tc * 4 + q_) for q_ in range(4)])
                    if j < 24:
                        P.cp(psb[b2][:, 1 + tc * 512:1 + (tc + 1) * 512], pp, eng=("act" if tc % 2 else "dve"))
                    else:
                        P.act(ub[b2][:, tc * 512:(tc + 1) * 512], pp, AF.Silu)
                if j < 24:
                    P.act(ub[b2], psb[b2][:, 1:4097], AF.Identity, scale=colv[:, CV["cw1"] + j:CV["cw1"] + j + 1], bias=colv[:, CV["cb"] + j:CV["cb"] + j + 1])
                    P.stt(ub[b2], psb[b2][:, 0:4096], colv[:, CV["cw0"] + j:CV["cw0"] + j + 1], ub[b2], ALU.mult, ALU.add)
                    P.stt(ub[b2], psb[b2][:, 2:4098], colv[:, CV["cw2"] + j:CV["cw2"] + j + 1], ub[b2], ALU.mult, ALU.add)
                    P.dma(U_[j * 128:(j + 1) * 128, :], ub[b2])
                    if j < 8:
                        P.cp(vbf_sb, ub[b2], eng="act")
                        P.dma(Vbf[j * 128:(j + 1) * 128, :], vbf_sb)
                else:
                    P.dma(GS2[(j - 24) * 128:(j - 23) * 128, :], ub[b2])
            if stop_after <= 6:
                P.emit(st)
                return nc, P

        phase_reset()
        TWO_PI = 2.0 * math.pi
        big5 = alloc("big5", 8192); hk = alloc("hk", 8192)
        w1s = alloc("w1s", 64, parts=33); embs = big5[0:33, :]
        P.dma(w1s, f_w1); P.dma(embs, embT)
        w2f = alloc("w2f", 64, parts=64); w2b = alloc("w2b", 64, parts=64, dt=BF16)
        P.dma(w2f, f_w2); P.cp(w2b, w2f)
        w3f = hk[0:64, 0:4096]; w3b = alloc("w3b", 4096, parts=64, dt=BF16)
        P.dma(w3f, f_w3); P.cp(w3b, w3f)
        fs = alloc("fs", 4, parts=64)
        P.ts(fs[:, 0:1], smv[:, 0:1], 1.0 / TWO_PI, None, ALU.mult)
        P.tt(fs[:, 1:2], fs[:, 0:1], smv[:, 1:2], ALU.mult)
        P.tt(fs[:, 2:3], fs[:, 0:1], smv[:, 2:3], ALU.mult)
        hid1 = alloc("hid1", 8192, parts=64, dt=BF16); hid2 = alloc("hid2", 8192, parts=64, dt=BF16)
        ubuf = alloc("ubuf", 512, parts=64); ibuf = alloc("ibuf", 512, parts=64, dt=I32); rbuf = alloc("rbuf", 512, parts=64)

        def sin_layer(dst, lhsT, rhs_fn, bias_col):
            for ncx in range(16):
                pp = PS[ncx % 2]
                P.mm(pp[0:64, :], lhsT, rhs_fn(ncx))
                P.act(ubuf, pp[0:64, :], AF.Identity, scale=fs[:, 0:1], bias=fs[:, bias_col:bias_col + 1])
                P.cp(ibuf, ubuf)
                P.cp(rbuf, ibuf)
                P.tt(ubuf, ubuf, rbuf, ALU.subtract)
                P.act(dst[:, ncx * 512:(ncx + 1) * 512], ubuf, AF.Sin, scale=TWO_PI)

        sin_layer(hid1, w1s, lambda ncx: embs[:, ncx * 512:(ncx + 1) * 512], 1)
        sin_layer(hid2, w2b, lambda ncx: hid1[:, ncx * 512:(ncx + 1) * 512], 2)
        tl_bc = big5
        P.dma(tl_bc, tlsgn[0].partition_broadcast(128))
        sdec = alloc("sdec", 8192)
        ksum = alloc("ksum", 1); kernb = alloc("kernb", 8192, dt=BF16); kjunk = kernb
        NCT5 = int(os.environ.get('P5_CT', 8))
        for ct in range(NCT5):
            P.act(sdec, tl_bc, AF.Exp, scale=colv[:, CV["ndelta"] + ct:CV["ndelta"] + ct + 1])
            P.ts(sdec[:, 4096:8192], sdec[:, 4096:8192], -1.0, None, ALU.mult)
            P.mset(sdec[:, 4096:4097], 0.0)
            for o in range(2):
                for dr in range(2):
                    col = (o * 2 + dr) * 1024 + ct * 128
                    bcol = CV["b3"] + (o * 2 + dr) * 8 + ct
                    for ncx in range(8):
                        n0 = dr * 4096 + ncx * 512
                        pp = PS[ncx % 2]
                        P.mm(pp, w3b[:, col:col + 128], hid2[:, n0:n0 + 512])
                        P.act(hk[:, n0:n0 + 512], pp, AF.Identity, bias=colv[:, bcol:bcol + 1])
                P.tt(hk, hk, sdec, ALU.mult)
                P.act(kjunk, hk, AF.Abs)
                P.red(ksum, kjunk)
                P.rcp(ksum, ksum)
                P.act(kernb, hk, AF.Copy, scale=ksum)
                P.dma(KERN[o, ct * 128:(ct + 1) * 128, :], kernb)
        if stop_after <= 5:
            d5 = ddbg("dbg5", [128, 64])
            P.dma(d5[:, 0:16], hk[:, 0:16]); P.dma(d5[:, 16:32], sdec[:, 0:16]); P.dma(d5[:, 32:33], ksum, allow_slow_non_contiguous=True)
            d5b = ddbg("dbg5b", [64, 64], BF16)
            P.dma(d5b[:, 0:32], hid1[:, 0:32]); P.dma(d5b[:, 32:64], hid2[:, 0:32])
            P.emit(st)
            return nc, P

        phase_reset()
        F1reg = alloc("F1reg", 128, dt=BF16)
        P.dma(F1reg[0:64, :], F1c); P.dma(F1reg[64:128, :], F1c)
        F1u = F1reg[64:128, :]
        F2s = alloc("F2s", 16384, dt=BF16); P.dma(F2s[:, 0:8192], F2c[:, 0:8192]); P.dma(F2s[:, 8192:16384], F2c[:, 8192:16384])
        F2v = F2s.r("p (k w m) -> p k w m", k=64, w=2)
        Gs_ = alloc("Gs_", 512, dt=BF16); P.dma(Gs_, Gc)
        Gv = Gs_.r("p (h w n) -> p h w n", h=2, w=2)
        Hreg = alloc("Hreg", 8192, dt=BF16)
        D1buf = Hreg[64:128, :].k("D1buf")
        Hs = alloc("Hs", 4096, dt=BF16); P.dma(Hs, Hc)
        Hv = Hs.r("p (b n) -> p b n", n=32)
        bufA = alloc("bufA", 8192, dt=BF16)
        bufB = alloc("bufB", 8192, dt=BF16)
        Zbuf = alloc("Zbuf", 4096, dt=BF16)
        zb = alloc("zb", 4096, dt=BF16)
        evc = {"n": 0}

        def evac(dst, src):
            evc["n"] += 1
            P.cp(dst, src, eng=("act" if evc["n"] % 4 else "dve"))

        def fwd_stages(src_rows, K, on_bank):
            D1 = D1buf[0:K, :].r("p (c b) -> p c b", b=128)
            for q_ in range(4):
                P.dma(D1[:, q_ * 16:(q_ + 1) * 16, :], src_rows[q_ * 16:(q_ + 1) * 16, :].rearrange("c (a b) -> a c b", b=128))
            Bv = bufB.r("p (k c) -> p k c", c=64)
            for c4 in range(16):
                pp = PS[c4 % 2]
                for cc in range(4):
                    P.mm(pp[:, cc * 128:(cc + 1) * 128], D1[:, c4 * 4 + cc, :], F1u[0:K, :], start=(cc == 0), stop=True, skip_group_check=True)
                evac(bufB.r("p (k c) -> p c k", c=64)[:, c4 * 4:(c4 + 1) * 4, :], pp.r("p (c k) -> p c k", c=4))
            for k8 in range(8):
                pp = PS[2 + k8 % 2]
                for kk in range(8):
                    kap = k8 * 8 + kk
                    P.mm(pp[:, kk * 64:(kk + 1) * 64], F2v[:, kap, 0, :], Bv[:, kap, :], start=(kk == 0), stop=False, skip_group_check=True)
                    P.mm(pp[:, kk * 64:(kk + 1) * 64], F2v[:, kap, 1, :], Bv[:, 64 + kap, :], start=False, stop=True, skip_group_check=True)
                on_bank(k8, pp)

        ksb = zb
        NHC = int(os.environ.get('P7_HC', 16))
        for o in range(2):
            for hc in range(NHC):
                def fb(k8, pp):
                    evac(ksb[:, k8 * 512:(k8 + 1) * 512], pp)
                fwd_stages(KERN[o, hc * 64:(hc + 1) * 64, :], 64, fb)
                P.dma(KSP[o, hc], ksb)
        if stop_after <= 7:
            P.emit(st)
            return nc, P

        KAs = alloc("KAs", 4096, dt=BF16); KBs = alloc("KBs", 4096, dt=BF16)
        ysb = alloc("ysb", 4096); z1s = alloc("z1s", 4096)
        ld = [alloc("ld%d" % i, 2048) for i in range(2)]
        P1 = bufA[:, 0:4096]; P2 = bufA[:, 4096:8192]
        P1c = P1.r("p (k c) -> p c k", c=64); P2c = P2.r("p (k c) -> p c k", c=64)
        Zv = Zbuf.r("p (b c) -> p b c", c=64)
        Zw = Zbuf.r("p (b c) -> p c b", c=64)

        def conv(src_dram, o, ct):
            for half in range(2):
                hc = ct * 2 + half
                ks = KSP[o, hc]
                P.dma(KAs[0:64, :], ks[0:64, :]); P.dma(KAs[64:128, :], ks[0:64, :])
                P.dma(KBs[0:64, :], ks[64:128, :]); P.dma(KBs[64:128, :], ks[64:128, :])
                P.ts(KBs[64:128, :], KBs[64:128, :], -1.0, None, ALU.mult, eng="pool")

                def ob(k8, pp):
                    sl_ = slice(k8 * 512, (k8 + 1) * 512)
                    P.tt(P1[:, sl_], pp, KAs[:, sl_], ALU.mult)
                    P.tt(P2[0:64, sl_], pp[64:128, :], KBs[64:128, sl_], ALU.mult)
                    P.tt(P2[64:128, sl_], pp[0:64, :], KBs[0:64, sl_], ALU.mult)
                    P.tt(P1[:, sl_], P1[:, sl_], P2[:, sl_], ALU.add, eng="pool")
                fwd_stages(src_dram[hc * 64:(hc + 1) * 64, :], 32, ob)
                for bh in range(2):
                    for c4 in range(16):
                        pp = PS[4 + c4 % 2]
                        for cc in range(4):
                            c = c4 * 4 + cc
                            P.mm(pp[0:64, cc * 128:(cc + 1) * 128], P1c[:, c, :], Gv[:, bh, 0, :], start=(cc == 0), stop=True, skip_group_check=True)
                        ppv = pp[0:64, :].r("p (c r b) -> p c r b", c=4, r=2)
                        evac(Zw[0:64, c4 * 4:(c4 + 1) * 4, :], ppv[:, :, 0, :])
                        P.cp(Zw[64:128, c4 * 4:(c4 + 1) * 4, :], ppv[:, :, 1, :])
                    for b16 in range(4):
                        pp = PS[6 + b16 % 2]
                        for bb in range(16):
                            b = b16 * 16 + bb
                            bg = bh * 64 + b
                            P.mm(pp[0:64, bb * 32:(bb + 1) * 32], Zv[:, b, :], Hv[:, bg, :], start=(bb == 0), stop=True, skip_group_check=True)
                        bg0 = bh * 64 + b16 * 16
                        P.cp(ysb[half * 64:(half + 1) * 64, :].r("p (a b) -> p b a", b=128)[:, bg0:bg0 + 16, :], pp[0:64, :].r("p (b a) -> p b a", a=32),
                             eng=("act" if half == 0 else "dve"))

        NCT7 = int(os.environ.get('P7_CT', 8))
        for ct in range(NCT7):
            conv(Vbf, 0, ct)
            for tq in range(2):
                tsl = slice(tq * 2048, (tq + 1) * 2048)
                P.dma(ld[0], U_[ct * 128:(ct + 1) * 128, tsl])
                P.dma(ld[1], U_[1024 + ct * 128:1024 + (ct + 1) * 128, tsl])
                P.stt(ysb[:, tsl], ld[0], colv[:, CV["sk0"] + ct:CV["sk0"] + ct + 1], ysb[:, tsl], ALU.mult, ALU.add)
                P.tt(z1s[:, tsl], ysb[:, tsl], ld[1], ALU.mult)
            P.cp(zb, z1s, eng="act")
            P.dma(Z1B[ct * 128:(ct + 1) * 128, :], zb)
            conv(Z1B, 1, ct)
            for tq in range(2):
                tsl = slice(tq * 2048, (tq + 1) * 2048)
                P.dma(ld[0], U_[2048 + ct * 128:2048 + (ct + 1) * 128, tsl])
                P.dma(ld[1], GS2[ct * 128:(ct + 1) * 128, tsl])
                P.stt(ysb[:, tsl], z1s[:, tsl], colv[:, CV["sk1"] + ct:CV["sk1"] + ct + 1], ysb[:, tsl], ALU.mult, ALU.add)
                P.tt(ysb[:, tsl], ysb[:, tsl], ld[0], ALU.mult)
                P.tt(zb[:, tsl], ysb[:, tsl], ld[1], ALU.mult)
            P.dma(ZF[ct * 128:(ct + 1) * 128, :], zb)
        if stop_after <= 8:
            P.emit(st)
            return nc, P

        phase_reset()
        wob2 = alloc("wob2", 8 * 1024, dt=BF16)
        wob2v = wob2.r("p (k n) -> p k n", k=8)
        wst8 = [alloc("wst8_%d" % i, 1024) for i in range(2)]
        for kt in range(8):
            P.dma(wst8[kt % 2], hy_w_out[kt * 128:(kt + 1) * 128, :])
            P.cp(wob2v[:, kt, :], wst8[kt % 2], eng=("act" if kt % 2 else "dve"))
        zT = [alloc("zT%d" % i, 1024, dt=BF16) for i in range(2)]
        xl1t = [alloc("xl1t%d" % i, 1024) for i in range(2)]
        x2t = [alloc("x2t%d" % i, 1024) for i in range(2)]
        ot8 = [alloc("ot8_%d" % i, 1024) for i in range(2)]
        junk8 = alloc("junk8", 1024); ss8 = [alloc("ss8_%d" % i, 1) for i in range(2)]
        ZFv = ZF.rearrange("(k p) t -> p k t", p=128)
        for i in range(32):
            b2 = i % 2
            zTv = zT[b2].r("p (k t) -> p k t", k=8)
            P.dma(zTv, ZFv[:, :, i * 128:(i + 1) * 128])
            P.dma(xl1t[b2], XL1[i * 128:(i + 1) * 128, :])
            for hh in range(2):
                for ct in range(8):
                    P.mm(PS[hh], zTv[:, ct, :], wob2v[:, ct, hh * 512:(hh + 1) * 512], start=(ct == 0), stop=(ct == 7))
                P.tt(x2t[b2][:, hh * 512:(hh + 1) * 512], PS[hh], gate_bc[1][:, hh * 512:(hh + 1) * 512], ALU.mult)
            P.tt(x2t[b2], x2t[b2], xl1t[b2], ALU.add)
            P.act(junk8, x2t[b2], AF.Square)
            P.red(ss8[b2], junk8)
            P.act(ss8[b2], ss8[b2], AF.Sqrt, scale=1.0 / 1024.0, bias=epsc)
            P.rcp(ss8[b2], ss8[b2])
            P.stt(ot8[b2], x2t[b2], ss8[b2], fnw_bc, ALU.mult, ALU.mult)
            P.dma(out[i * 128:(i + 1) * 128, :], ot8[b2], is_output=True)
        P.emit(st)
        return nc, P


def kernel(**inputs):
    nc, _ = build()
    in_maps = [pack_inputs(inputs, b) for b in range(8)]
    res = run_bass_kernel_spmd(nc, in_maps, core_ids=list(range(8)))
    return np.stack([np.asarray(r["out"], dtype=np.float32) for r in res.results], 0)
```

```python
import numpy as np
import concourse.bass as bass
import concourse.mybir as mybir
from concourse.bass_utils import run_bass_kernel_spmd
from contextlib import ExitStack

F32 = mybir.dt.float32
BF16 = mybir.dt.bfloat16
AF = mybir.ActivationFunctionType
ALU = mybir.AluOpType
AX = mybir.AxisListType

NDMA_SLOTS = 24


class _Op:
    __slots__ = ("eng", "idx", "fn", "deps", "is_dma", "dma_id", "waits", "signal", "snap", "semval")

    def __init__(self, eng, idx, fn, deps, is_dma, dma_id):
        self.eng = eng
        self.idx = idx
        self.fn = fn
        self.deps = deps
        self.is_dma = is_dma
        self.dma_id = dma_id
        self.waits = []
        self.signal = False
        self.snap = None
        self.semval = 0


class V:
    __slots__ = ("ap", "key")

    def __init__(self, ap, key):
        self.ap = ap
        self.key = key

    def __getitem__(self, idx):
        return V(self.ap[idx], self.key)

    def r(self, pat, **kw):
        return V(self.ap.rearrange(pat, **kw), self.key)

    def k(self, key):
        return V(self.ap, key)

    def bitcast(self, dt):
        return V(self.ap.bitcast(dt), self.key)


def U(a):
    return a.ap if isinstance(a, V) else a


def bc(a, pos, n):
    ap = U(a)
    dims = [list(d) for d in ap.ap]
    dims.insert(1 + pos, [0, n])
    r = bass.AP(ap.tensor, ap.offset, dims)
    return V(r, a.key) if isinstance(a, V) else r


class Prog:
    ENGS = ("pe", "act", "dve", "pool", "sp")

    def __init__(self, nc, same_engine_sync=True):
        self.nc = nc
        self.streams = {e: [] for e in self.ENGS}
        self.order = []
        self.last_writer = {}
        self.readers = {}
        self.ndma = 0
        self.dma_ops = []
        self.same_engine_sync = same_engine_sync
        self.out_dmas = []
        self.barrier_op = None

    @staticmethod
    def _keys(aps):
        ks = []
        for a in aps:
            if a is None or isinstance(a, (int, float)):
                continue
            if isinstance(a, (str, tuple)):
                ks.append(a)
            elif isinstance(a, V):
                ks.append(a.key)
            elif hasattr(a, "tensor"):
                ks.append(a.tensor.name)
            else:
                ks.append(a.name)
        return ks

    def add(self, eng, fn, rd, wr, is_dma=False):
        rd = self._keys(rd)
        wr = self._keys(wr)
        wr = wr + [k for k in rd if isinstance(k, str) and k.startswith("ps") and k not in wr]
        deps = []
        seen = set()

        def _dep(o):
            if o is not None and id(o) not in seen:
                seen.add(id(o))
                deps.append(o)

        for k in rd:
            _dep(self.last_writer.get(k))
        for k in wr:
            _dep(self.last_writer.get(k))
            for r in self.readers.get(k, ()):
                _dep(r)
        _dep(self.barrier_op)
        dma_id = None
        if is_dma:
            dma_id = self.ndma
            self.ndma += 1
            if dma_id >= NDMA_SLOTS:
                _dep(self.dma_ops[dma_id - NDMA_SLOTS])
        op = _Op(eng, len(self.streams[eng]), fn, deps, is_dma, dma_id)
        if is_dma:
            self.dma_ops.append(op)
        self.streams[eng].append(op)
        self.order.append(op)
        for k in wr:
            self.last_writer[k] = op
            self.readers[k] = []
        for k in rd:
            self.readers.setdefault(k, []).append(op)
        return op

    def mm(self, out, lhsT, rhs, start=True, stop=True, rd=None, wr=None, **kw):
        o_, l_, r_ = U(out), U(lhsT), U(rhs)
        return self.add("pe", lambda e: e.matmul(o_, l_, r_, start=start, stop=stop, **kw),
                        rd if rd is not None else [lhsT, rhs], wr if wr is not None else [out])

    def tr(self, out, in_, ident, rd=None, wr=None):
        o_, i_, d_ = U(out), U(in_), U(ident)
        return self.add("pe", lambda e: e.transpose(o_, i_, d_),
                        rd if rd is not None else [in_, ident], wr if wr is not None else [out])

    def tt(self, out, in0, in1, op, eng="dve"):
        o_, a_, b_ = U(out), U(in0), U(in1)
        return self.add(eng, lambda e: e.tensor_tensor(o_, a_, b_, op), [in0, in1], [out])

    def ts(self, out, in0, s1, s2, op0, op1=None, eng="dve"):
        o_, a_, s1_, s2_ = U(out), U(in0), U(s1), U(s2)
        if op1 is None:
            return self.add(eng, lambda e: e.tensor_scalar(o_, a_, s1_, None, op0), [in0, s1], [out])
        return self.add(eng, lambda e: e.tensor_scalar(o_, a_, s1_, s2_, op0, op1), [in0, s1, s2], [out])

    def stt(self, out, in0, sc, in1, op0, op1, eng="dve"):
        o_, a_, s_, b_ = U(out), U(in0), U(sc), U(in1)
        return self.add(eng, lambda e: e.scalar_tensor_tensor(o_, a_, s_, b_, op0, op1), [in0, sc, in1], [out])

    def cp(self, out, in_, eng="dve"):
        o_, i_ = U(out), U(in_)
        if eng == "act":
            return self.add(eng, lambda e: e.copy(o_, i_), [in_], [out])
        return self.add(eng, lambda e: e.tensor_copy(o_, i_), [in_], [out])

    def red(self, out, in_, op=None, axis=None, eng="dve"):
        o_, i_ = U(out), U(in_)
        op = op or ALU.add
        axis = axis or AX.X
        return self.add(eng, lambda e: e.tensor_reduce(o_, i_, axis, op), [in_], [out])

    def rcp(self, out, in_):
        o_, i_ = U(out), U(in_)
        return self.add("dve", lambda e: e.reciprocal(o_, i_), [in_], [out])

    def mset(self, out, val, eng="dve"):
        o_ = U(out)
        return self.add(eng, lambda e: e.memset(o_, val), [], [out])

    def barrier(self, scr_out, scr_in):
        deps = []
        for e in self.ENGS:
            if self.streams[e]:
                deps.append(self.streams[e][-1])
        last = {}
        for d in self.dma_ops:
            last[d.dma_id % NDMA_SLOTS] = d
        deps.extend(last.values())
        op = self.add("sp", lambda e: e.dma_start(out=scr_out, in_=scr_in), [], [], is_dma=True)
        ids = set(id(x) for x in op.deps)
        for d in deps:
            if id(d) not in ids and d is not op:
                op.deps.append(d)
        self.barrier_op = op
        return op

    def act(self, out, in_, func, bias=None, scale=None, accum_out=None, rd=None, wr=None, eng="act"):
        kw = {}
        if bias is not None:
            kw["bias"] = U(bias)
        if scale is not None:
            kw["scale"] = U(scale)
        if accum_out is not None:
            kw["accum_out"] = U(accum_out)
        r = [in_, bias, scale] if rd is None else rd
        w = [out, accum_out] if wr is None else wr
        o_, i_ = U(out), U(in_)
        return self.add(eng, lambda e: e.activation(o_, i_, func, **kw), r, w)

    def v(self, fn, rd, wr, eng="dve"):
        return self.add(eng, fn, rd, wr)

    def dma(self, out, in_, rd=None, wr=None, q=None, is_output=False, **kw):
        if q is None:
            q = "sp" if isinstance(out, V) else "pool"
        o_, i_ = U(out), U(in_)
        op = self.add(q, lambda e: e.dma_start(out=o_, in_=i_, **kw),
                      rd if rd is not None else [in_], wr if wr is not None else [out], is_dma=True)
        if is_output:
            self.out_dmas.append(op)
        return op

    def emit(self, stack):
        nc = self.nc
        esem = {e: stack.enter_context(nc.semaphore("s_" + e)) for e in self.ENGS}
        dsem = [stack.enter_context(nc.semaphore("d_%d" % i)) for i in range(NDMA_SLOTS)]
        know = {e: ({x: -1 for x in self.ENGS}, {}) for e in self.ENGS}
        for op in self.order:
            kc, kd = know[op.eng]
            for d in op.deps:
                if d.is_dma:
                    if kd.get(d.dma_id % NDMA_SLOTS, -1) >= d.dma_id:
                        continue
                    op.waits.append(d)
                    kd[d.dma_id % NDMA_SLOTS] = d.dma_id
                    sc, sd = d.snap
                    for x, v_ in sc.items():
                        if v_ > kc[x]:
                            kc[x] = v_
                    for x, v_ in sd.items():
                        if v_ > kd.get(x, -1):
                            kd[x] = v_
                else:
                    if d.eng == op.eng and (op.eng == "pe" or not self.same_engine_sync or op.is_dma and False):
                        continue
                    if kc[d.eng] >= d.idx:
                        continue
                    op.waits.append(d)
                    d.signal = True
                    sc, sd = d.snap
                    for x, v_ in sc.items():
                        if v_ > kc[x]:
                            kc[x] = v_
                    for x, v_ in sd.items():
                        if v_ > kd.get(x, -1):
                            kd[x] = v_
                    if d.idx > kc[d.eng]:
                        kc[d.eng] = d.idx
            if not op.is_dma:
                snapc = dict(kc)
                snapc[op.eng] = op.idx
                op.snap = (snapc, dict(kd))
            else:
                snapc = dict(kc)
                op.snap = (snapc, dict(kd))
        for e in self.ENGS:
            c = 0
            for op in self.streams[e]:
                if op.is_dma:
                    continue
                if op.signal:
                    c += 1
                op.semval = c
        nwaits = sum(len(o.waits) for o in self.order)
        self.stats = dict(nops=len(self.order), nwaits=nwaits, ndma=self.ndma,
                          per_eng={e: len(s) for e, s in self.streams.items()})

        def run_stream(e):
            def body(eng):
                for op in self.streams[e]:
                    for d in op.waits:
                        if d.is_dma:
                            eng.wait_ge(dsem[d.dma_id % NDMA_SLOTS], 16 * (d.dma_id // NDMA_SLOTS + 1))
                        else:
                            eng.wait_ge(esem[d.eng], d.semval)
                    ins = op.fn(eng)
                    if op.is_dma:
                        ins.then_inc(dsem[op.dma_id % NDMA_SLOTS], 16)
                    elif op.signal:
                        ins.then_inc(esem[e], 1)
                if e == "sp":
                    last = {}
                    for d in self.dma_ops:
                        last[d.dma_id % NDMA_SLOTS] = d
                    for s, d in last.items():
                        eng.wait_ge(dsem[s], 16 * (d.dma_id // NDMA_SLOTS + 1))
            return body

        with nc.Block() as block:
            block.tensor(run_stream("pe"))
            block.scalar(run_stream("act"))
            block.vector(run_stream("dve"))
            block.gpsimd(run_stream("pool"))
            block.sync(run_stream("sp"))


import math
import ml_dtypes

I32 = mybir.dt.int32
NT_LAT = 32
NT_ALL = 34
EPS = 1e-6

CV = dict(c=0, c_ctx=8, nw0=16, nw1=24, adab0=32, adab1=56, cw0=80, cw1=104, cw2=128, cb=152,
          sk0=176, sk1=184, b3=192, ndelta=224)


def _rope_tab(rot_dim):
    L, GW = 4096, 64
    rows = np.repeat(np.arange(L // GW, dtype=np.int32), GW).astype(np.float32)
    cols = np.tile(np.arange(GW, dtype=np.int32), L // GW).astype(np.float32)
    q = rot_dim // 4
    inv = (np.float32(10000.0) ** (-np.arange(q, dtype=np.float32) / np.float32(q))).astype(np.float32)
    ar = (rows[:, None] * inv).astype(np.float32)
    ac = (cols[:, None] * inv).astype(np.float32)
    cr, sr, cc, sc = np.cos(ar), np.sin(ar), np.cos(ac), np.sin(ac)
    C = np.concatenate([cr, cr, cc, cc], 1)
    S = np.concatenate([-sr, sr, -sc, sc], 1)
    return np.concatenate([C, S], 1).astype(np.float32)


_CONSTS = None


def host_consts():
    global _CONSTS
    if _CONSTS is not None:
        return _CONSTS
    bf = ml_dtypes.bfloat16
    c = {}
    c["ident_bf"] = np.eye(128, dtype=np.float32).astype(bf)
    c["ident_f"] = np.eye(128, dtype=np.float32)
    c["ropeA"] = _rope_tab(64)
    c["ropeM"] = _rope_tab(32)
    L = 4096
    n = np.arange(8192)
    tpos = np.where(n < L, n, np.where(n == L, 0, 8192 - n))
    tl = np.linspace(0.0, 1.0, L, dtype=np.float32)
    w = ((2.0 * math.pi / L) * np.arange(L, dtype=np.float32)).astype(np.float32)
    bands = np.linspace(1e-4, 15, 16, dtype=np.float32)
    emb = np.concatenate([tl[:, None], np.cos(w[:, None] * bands), -np.sin(w[:, None] * bands)], 1).astype(np.float32)
    c["embT"] = np.ascontiguousarray(emb[tpos].T)
    sgn = np.where(n < L, 1.0, np.where(n == L, 0.0, -1.0)).astype(np.float32)
    c["tlsgn"] = np.stack([tl[tpos], sgn]).astype(np.float32)
    a = np.arange(64)[:, None]
    kap = np.arange(64)[None, :]
    f1 = np.exp(-2j * np.pi * a * (2 * kap + 1) / 128.0)
    c["F1c"] = np.concatenate([f1.real, f1.imag], 1).astype(np.float32).astype(bf)
    b = np.arange(128)[:, None, None]
    kp = np.arange(64)[None, :, None]
    be = np.arange(64)[None, None, :]
    f2 = np.exp(-2j * np.pi * (b * be / 128.0 + b * (kp + 0.5) / 8192.0))
    f2a = np.concatenate([f2.real, f2.imag], 2)
    f2b = np.concatenate([-f2.imag, f2.real], 2)
    c["F2c"] = np.stack([f2a, f2b], 2).reshape(128, 64 * 2 * 128).astype(np.float32).astype(bf)
    be2 = np.arange(64)[:, None]
    bb = np.arange(128)[None, :]
    g = np.exp(2j * np.pi * bb * be2 / 128.0)
    G = np.zeros((128, 2, 2, 128), np.float64)
    for bh in range(2):
        gs = g[:, bh * 64:(bh + 1) * 64]
        g1 = np.concatenate([np.concatenate([gs.real, gs.imag], 1), np.concatenate([-gs.imag, gs.real], 1)], 0)
        g2 = np.concatenate([g1[64:], -g1[:64]], 0)
        G[:, bh, 0, :] = g1
        G[:, bh, 1, :] = g2
    c["Gc"] = G.reshape(128, 512).astype(np.float32).astype(bf)
    kq = np.arange(64)[:, None, None]
    b3 = np.arange(128)[None, :, None]
    a3 = np.arange(32)[None, None, :]
    h = (2.0 / 8192.0) * np.exp(2j * np.pi * (a3 * (2 * kq + 1) / 128.0 + b3 * (kq + 0.5) / 8192.0))
    c["Hc"] = np.concatenate([h.real, -h.imag], 2).reshape(64, 128 * 64).astype(np.float32).astype(bf)
    deltas = np.linspace(math.log(1e-2) / 1.5, math.log(1e-2) / 0.3, 1024, dtype=np.float32)
    c["_ndelta"] = (-np.abs(deltas)).astype(np.float32)
    _CONSTS = c
    return c


def pack_inputs(inp, b):
    c = host_consts()
    f = lambda a: np.ascontiguousarray(np.asarray(a, dtype=np.float32))
    vec = np.zeros((256, 128), np.float32)

    def put(name, arr):
        arr = f(arr).reshape(-1, 128)
        vec[CV[name]:CV[name] + arr.shape[0]] = arr

    put("c", inp["c"][b]); put("c_ctx", inp["c_ctx"]); put("nw0", inp["norm_w"][0]); put("nw1", inp["norm_w"][1])
    put("adab0", inp["ada_b"][0]); put("adab1", inp["ada_b"][1])
    put("cw0", inp["hy_conv_w"][0][0]); put("cw1", inp["hy_conv_w"][0][1]); put("cw2", inp["hy_conv_w"][0][2])
    put("cb", inp["hy_conv_b"][0]); put("sk0", inp["hy_skip"][0][0]); put("sk1", inp["hy_skip"][0][1])
    put("b3", inp["hy_ffn_b3"][0]); put("ndelta", c["_ndelta"])
    smallv = np.zeros((64, 4), np.float32)
    smallv[:, 0] = f(inp["hy_freq"][0]); smallv[:, 1] = f(inp["hy_ffn_b1"][0]); smallv[:, 2] = f(inp["hy_ffn_b2"][0])
    rowv = np.zeros((4, 1024), np.float32)
    rowv[0, :640] = np.concatenate([np.tile(f(inp["attn_q_norm"][0]), 8), np.tile(f(inp["attn_k_norm"][0]), 2)])
    rowv[0, 640:896] = f(inp["mla_q_norm"][0]); rowv[0, 896:1024] = f(inp["mla_kv_norm"][0])
    rowv[1] = f(inp["final_norm_w"]); rowv[2] = f(inp["ada_b"][0][2048:]); rowv[3] = f(inp["ada_b"][1][2048:])
    d = dict(x=f(inp["x"][b]), ctx=f(inp["ctx"][b]), vecs=vec, smallv=smallv, rowv=rowv,
             ada_w=f(inp["ada_w"]), w_in=f(inp["attn_w_in"][0]), w_uq=f(inp["mla_w_uq"][0]),
             w_ukv=f(inp["mla_w_ukv"][0]), w_out=f(inp["attn_w_out"][0]), hy_w_in=f(inp["hy_w_in"][0]),
             f_w1=f(inp["hy_ffn_w1"][0]), f_w2=f(inp["hy_ffn_w2"][0]), f_w3=f(inp["hy_ffn_w3"][0]),
             hy_w_out=f(inp["hy_w_out"][0]))
    for k in ("ident_bf", "ident_f", "ropeA", "ropeM", "embT", "tlsgn", "F1c", "F2c", "Gc", "Hc"):
        d[k] = c[k]
    return d


ARENA = 50000
DBGSET = {6: ('U_', 'GS2'), 5: ('KERN',), 2: ('QaT', 'KaT', 'QmT', 'KmT', 'Vm', 'Gs'), 3: ('Oall', 'XL1'), 7: ('KSP',), 8: ('ZF', 'Z1B')}


def build(dbg=False, stop_after=99):
    nc = bass.Bass("TRN2", target_bir_lowering=False)
    skind = "Internal"

    def din(name, shape, dt=F32):
        return nc.dram_tensor(name, list(shape), dt, kind="ExternalInput").ap()

    def dscr(name, shape, dt=F32):
        kind = "ExternalOutput" if (dbg and name in DBGSET.get(stop_after, ())) else "Internal"
        return nc.dram_tensor(name, list(shape), dt, kind=kind).ap()

    def ddbg(name, shape, dt=F32):
        return nc.dram_tensor(name, list(shape), dt, kind="ExternalOutput").ap()

    x = din("x", [4096, 1024]); ctx = din("ctx", [256, 1024])
    vecs = din("vecs", [256, 128]); smallv = din("smallv", [64, 4]); rowv = din("rowv", [4, 1024])
    ada_w = din("ada_w", [2, 1024, 3072]); w_in = din("w_in", [1024, 2208]); w_uq = din("w_uq", [256, 768])
    w_ukv = din("w_ukv", [128, 1024]); w_out = din("w_out", [1024, 1024]); hy_w_in = din("hy_w_in", [1024, 4096])
    f_w1 = din("f_w1", [33, 64]); f_w2 = din("f_w2", [64, 64]); f_w3 = din("f_w3", [64, 4096])
    hy_w_out = din("hy_w_out", [1024, 1024])
    ident_bf_d = din("ident_bf", [128, 128], BF16); ident_f_d = din("ident_f", [128, 128])
    ropeA = din("ropeA", [4096, 128]); ropeM = din("ropeM", [4096, 64]); embT = din("embT", [33, 8192])
    tlsgn = din("tlsgn", [2, 8192]); F1c = din("F1c", [64, 128], BF16); F2c = din("F2c", [128, 16384], BF16)
    Gc = din("Gc", [128, 512], BF16); Hc = din("Hc", [64, 8192], BF16)
    out = nc.dram_tensor("out", [4096, 1024], F32, kind="ExternalOutput").ap()

    QaT = dscr("QaT", [8, 64, 4096], BF16); KaT = dscr("KaT", [2, 64, 4352], BF16)
    Va = dscr("Va", [2, 128, 34, 66], BF16)
    QmT = dscr("QmT", [8, 96, 4096], BF16); KmT = dscr("KmT", [8, 96, 4352], BF16)
    Vm = dscr("Vm", [8, 128, 34, 66], BF16)
    Gs = dscr("Gs", [4096, 1024]); Oall = dscr("Oall", [4096, 1024]); XL1 = dscr("XL1", [4096, 1024])
    bscr = dscr("bscr", [2, 16])

    st = ExitStack()
    with st:
        Aten = st.enter_context(nc.sbuf_tensor("A", [128, ARENA], F32))
        PSD = [st.enter_context(nc.psum_tensor("psd%d" % i, [128, 1024], F32)) for i in range(4)]
        PS = [V(PSD[i // 2][:, (i % 2) * 512:(i % 2 + 1) * 512], "ps%d" % i) for i in range(8)]
        PSb = [V(PSD[i // 2][:, (i % 2) * 512:(i % 2 + 1) * 512].bitcast(BF16), "ps%d" % i) for i in range(8)]
        P = Prog(nc)
        al = {"off": 0, "base": 0, "n": 0}

        def alloc(name, ncols, parts=128, dt=F32):
            nf = ncols if dt == F32 or dt == I32 else (ncols + 1) // 2
            nf = (nf + 7) // 8 * 8
            o = al["off"]
            assert o + nf <= ARENA, (name, o, nf)
            al["off"] = o + nf
            ap = Aten[0:parts, o:o + nf]
            if dt != F32:
                ap = ap.bitcast(dt)
            ap = ap[:, 0:ncols]
            al["n"] += 1
            return V(ap, "%s#%d" % (name, al["n"]))

        def phase_reset():
            P.barrier(bscr[0:1, :], bscr[1:2, :])
            al["off"] = al["base"]

        ident_bf = alloc("identb", 128, dt=BF16); ident_f = alloc("identf", 128)
        colv = alloc("colv", 256); smv = alloc("smv", 4, parts=64)
        s2 = alloc("s2", 16); sbc = alloc("sbc", 1024)
        ada_sb = [alloc("ada0", 48), alloc("ada1", 48)]
        gate_bc = [alloc("gbc0", 1024), alloc("gbc1", 1024)]
        nw_bc = alloc("nwbc", 1024); fnw_bc = alloc("fnwbc", 1024)
        Acol = [[alloc("A00", 8), alloc("A01", 8)], [alloc("A10", 8), None]]
        epsc = alloc("eps", 1)
        al["base"] = al["off"]

        P.dma(ident_bf, ident_bf_d); P.dma(ident_f, ident_f_d); P.dma(smv, smallv)
        P.dma(nw_bc, rowv[0].partition_broadcast(128)); P.dma(fnw_bc, rowv[1].partition_broadcast(128))
        P.mset(epsc, EPS)

        vrow = alloc("vrow", 128)
        for j in range(2):
            P.dma(vrow, vecs[j * 128:(j + 1) * 128, :])
            P.tr(PS[0][:, 0:128], vrow, ident_f)
            P.cp(colv[:, j * 128:(j + 1) * 128], PS[0][:, 0:128])
        s2v = s2.r("p (k j) -> p k j", j=2)
        P.act(s2v[:, :, 0], colv[:, 0:8], AF.Silu)
        P.act(s2v[:, :, 1], colv[:, 8:16], AF.Silu)
        sbcv = sbc.r("p (k m) -> p k m", m=128)
        P.cp(sbcv, bc(s2v[:, :, 0], 2, 128))
        awc = [alloc("awc0", 1024), alloc("awc1", 1024)]
        gb_tmp = alloc("gbtmp", 1024)
        for l in range(2):
            awl = ada_w[l].rearrange("(kt p) n -> p kt n", p=128)
            P.dma(gb_tmp, rowv[2 + l].partition_broadcast(128))
            for m in range(24):
                cw = awc[m % 2]
                cwv = cw.r("p (k n) -> p k n", n=128)
                P.dma(cwv, awl[:, :, m * 128:(m + 1) * 128])
                for kt in range(8):
                    P.mm(PS[1][:, 2 * m:2 * m + 2], cwv[:, kt, :], s2v[:, kt, :], start=(kt == 0), stop=(kt == 7))
                if m >= 16:
                    g = m - 16
                    for kt in range(8):
                        P.mm(PS[2 + g // 4][:, (g % 4) * 128:(g % 4 + 1) * 128], sbcv[:, kt, :], cwv[:, kt, :],
                             start=(kt == 0), stop=(kt == 7))
            adv = ada_sb[l].r("p (m j) -> p m j", j=2)
            P.tt(adv, PS[1][:, 0:48].r("p (m j) -> p m j", j=2), bc(colv[:, CV["adab%d" % l]:CV["adab%d" % l] + 24], 1, 2), ALU.add)
            P.tt(gate_bc[l][:, 0:512], PS[2], gb_tmp[:, 0:512], ALU.add)
            P.tt(gate_bc[l][:, 512:1024], PS[3], gb_tmp[:, 512:1024], ALU.add)
            for j in range(2):
                if Acol[l][j] is None:
                    continue
                P.ts(Acol[l][j], adv[:, 8:16, j], 1.0, None, ALU.add)
                P.tt(Acol[l][j], Acol[l][j], colv[:, CV["nw%d" % l]:CV["nw%d" % l] + 8], ALU.mult)
        if stop_after <= 0:
            dbgo = ddbg("dbg0", [128, 1024 + 96 + 256])
            P.dma(dbgo[:, 0:1024], gate_bc[0]); P.dma(dbgo[:, 1024:1072], ada_sb[0]); P.dma(dbgo[:, 1072:1120], ada_sb[1])
            P.dma(dbgo[:, 1120:1376], colv)
            P.emit(st)
            return nc, P

        U_ = dscr("U_", [3072, 4096]); Vbf = dscr("Vbf", [1024, 4096], BF16); GS2 = dscr("GS2", [1024, 4096])
        KERN = dscr("KERN", [2, 1024, 8192], BF16); KSP = dscr("KSP", [2, 16, 128, 4096], BF16)
        Z1B = dscr("Z1B", [1024, 4096], BF16); ZF = dscr("ZF", [1024, 4096], BF16)
        import os
        RUN_L0 = not int(os.environ.get('ONLY_HY', 0))
        if RUN_L0:
            phase_reset()
            hT = alloc("hT", 8 * 4352, dt=BF16)
            hTv = hT.r("p (k t) -> p k t", k=8)

            def make_hT_builder():
                bufs = dict(xts=[alloc("xt%d" % i, 1024) for i in range(3)], junk=alloc("junk", 1024),
                            ss=[alloc("ss%d" % i, 1) for i in range(3)], rs=[alloc("rs%d" % i, 1) for i in range(3)],
                            xs=[alloc("xs%d" % i, 1024, dt=BF16) for i in range(2)])

                def tile(i, src, l, j, hTv_, key):
                    xt = bufs["xts"][i % 3]
                    if not isinstance(src, V):
                        P.dma(xt, src)
                    else:
                        xt = src
                    junk, ss, rs, xs = bufs["junk"], bufs["ss"], bufs["rs"], bufs["xs"]
                    P.act(junk, xt, AF.Square)
                    P.red(ss[i % 3], junk)
                    P.act(rs[i % 3], ss[i % 3], AF.Sqrt, scale=1.0 / 1024.0, bias=epsc)
                    P.rcp(rs[i % 3], rs[i % 3])
                    P.ts(xs[i % 2], xt, rs[i % 3], None, ALU.mult)
                    pb = PSb[6 + (i % 2)]
                    for kt in range(8):
                        P.tr(pb[:, kt * 128:(kt + 1) * 128], xs[i % 2][:, kt * 128:(kt + 1) * 128], ident_bf)
                    adv_ = ada_sb[l].r("p (m j) -> p m j", j=2)
                    for kt in range(8):
                        dst = hTv_[:, kt, i * 128:(i + 1) * 128].k((key, i))
                        if kt % 2 == 0:
                            P.act(dst, pb[:, kt * 128:(kt + 1) * 128], AF.Identity, scale=Acol[l][j][:, kt:kt + 1], bias=adv_[:, kt, j:j + 1])
                        else:
                            P.ts(dst, pb[:, kt * 128:(kt + 1) * 128], Acol[l][j][:, kt:kt + 1], adv_[:, kt, j:j + 1], ALU.mult, ALU.add)
                return tile

            def build_hT(specs, hTv_):
                tile = make_hT_builder()
                for i, (src, l, j) in enumerate(specs):
                    tile(i, src, l, j, hTv_, "hT")

            specs0 = [(ctx[i * 128:(i + 1) * 128, :], 0, 1) for i in range(2)] + [(x[i * 128:(i + 1) * 128, :], 0, 0) for i in range(32)]
            mark = al["off"]
            build_hT(specs0, hTv)
            if stop_after <= 1:
                dbgo = ddbg("dbg1", [128, 8 * 4352], BF16)
                P.dma(dbgo, hT, rd=[("hT", i) for i in range(34)])
                P.emit(st)
                return nc, P

            P.barrier(bscr[0:1, :], bscr[1:2, :])
            al["off"] = mark
            w_in_b = alloc("w_in_b", 8 * 2208, dt=BF16)
            w_in_bv = w_in_b.r("p (k n) -> p k n", k=8)
            wst = [alloc("wst%d" % i, 2208) for i in range(2)]
            for kt in range(8):
                P.dma(wst[kt % 2], w_in[kt * 128:(kt + 1) * 128, :])
                if kt % 2 == 0:
                    P.cp(w_in_bv[:, kt, :], wst[kt % 2], eng="act")
                else:
                    P.cp(w_in_bv[:, kt, :], wst[kt % 2])
            w_uq_b = alloc("w_uq_b", 2 * 768, dt=BF16)
            w_uq_bv = w_uq_b.r("p (k n) -> p k n", k=2)
            for kt in range(2):
                P.dma(wst[kt % 2][:, 0:768], w_uq[kt * 128:(kt + 1) * 128, :])
                P.cp(w_uq_bv[:, kt, :], wst[kt % 2][:, 0:768])
            w_ukv_b = alloc("w_ukv_b", 1024, dt=BF16)
            P.dma(wst[0][:, 0:1024], w_ukv)
            P.cp(w_ukv_b, wst[0][:, 0:1024])

            pr = wst[0]
            sq = alloc("sq", 640)
            ss10 = alloc("ss10", 16); rs10 = alloc("rs10", 16)
            qn = alloc("qn", 640); t1 = alloc("t1", 640); t2 = alloc("t2", 640)
            qkb = alloc("qkb", 640, dt=BF16)
            rp = [alloc("rp%d" % i, 192) for i in range(2)]
            cqn = alloc("cqn", 384); cqb = alloc("cqb", 384, dt=BF16)
            cT = alloc("cT", 384, dt=BF16)
            qm = alloc("qm", 768); qmb = alloc("qmb", 768, dt=BF16)
            kpe = alloc("kpe", 32); kt1 = alloc("kt1", 32); kt2 = alloc("kt2", 32)
            kmb = alloc("kmb", 768, dt=BF16)
            vab = alloc("vab", 132, dt=BF16); vmb = alloc("vmb", 528, dt=BF16)
            gsb = [alloc("gsb0", 1024)] * 2
            stQa = [alloc("stQa0", 8 * 256, parts=64, dt=BF16)]
            stKa = [alloc("stKa0", 2 * 256, parts=64, dt=BF16)]
            stQm = [alloc("stQm0", 8 * 256, parts=96, dt=BF16)]
            stKm = [alloc("stKm0", 8 * 256, parts=96, dt=BF16)]
            P.mset(vab, 1.0)
            P.mset(vmb, 1.0)

            def rope(dst, src, tab, nh, D, tmp1, tmp2):
                q = D // 4
                sv = src.r("p (h f) -> p h f", f=D)
                Cb = bc(tab[:, 0:D], 0, nh)
                P.tt(tmp1.r("p (h f) -> p h f", f=D), sv, Cb, ALU.mult)
                s5 = src.r("p (h a s f) -> p (h a) s f", a=2, s=2, f=q)
                t5 = tmp2.r("p (h a s f) -> p (h a) s f", a=2, s=2, f=q)
                S4 = tab[:, D:2 * D].r("p (a s f) -> p a s f", a=2, s=2)
                for s_ in range(2):
                    for a_ in range(2):
                        srcv = src.r("p (h a s f) -> p h a s f", a=2, s=2, f=q)[:, :, a_, 1 - s_, :]
                        dstv = tmp2.r("p (h a s f) -> p h a s f", a=2, s=2, f=q)[:, :, a_, s_, :]
                        Sb = bc(S4[:, a_, s_, :], 0, nh)
                        P.tt(dstv, srcv, Sb, ALU.mult)
                P.tt(dst, tmp1, tmp2, ALU.add)

            import os
            for tt_ in range(int(os.environ.get('P2_TILES', NT_ALL))):
                lat = tt_ >= 2
                STG = int(os.environ.get('P2_STAGE', 99))
                li = tt_ - 2
                grp = 0
                sl = tt_ % 2
                chunks = [(0, 512), (512, 1024), (1024, 1536), (1536, 2048), (2048, 2208)]
                nchunk = 5 if lat else 3
                for ci in range(nchunk):
                    c0, c1 = chunks[ci]
                    for kt in range(8):
                        P.mm(PS[ci][:, 0:c1 - c0], hTv[:, kt, tt_ * 128:(tt_ + 1) * 128].k(("hT", tt_)), w_in_bv[:, kt, c0:c1],
                             start=(kt == 0), stop=(kt == 7))
                    if ci % 2 == 0:
                        P.cp(pr[:, c0:c1], PS[ci][:, 0:c1 - c0], eng="act")
                    else:
                        P.cp(pr[:, c0:c1], PS[ci][:, 0:c1 - c0])
                if STG <= 1:
                    continue
                if lat:
                    P.dma(rp[tt_ % 2][:, 0:128], ropeA[li * 128:(li + 1) * 128, :])
                    P.dma(rp[tt_ % 2][:, 128:192], ropeM[li * 128:(li + 1) * 128, :])
                h0 = 0 if lat else 512
                nh = 10 if lat else 2
                P.tt(sq[:, h0:640], pr[:, h0:640], pr[:, h0:640], ALU.mult)
                P.red(ss10[:, 0:nh], sq[:, h0:640].r("p (h f) -> p h f", f=64))
                P.act(rs10[:, 0:nh], ss10[:, 0:nh], AF.Sqrt, scale=1.0 / 64.0, bias=epsc)
                P.rcp(rs10[:, 0:nh], rs10[:, 0:nh])
                P.tt(qn[:, h0:640].r("p (h f) -> p h f", f=64), pr[:, h0:640].r("p (h f) -> p h f", f=64), bc(rs10[:, 0:nh], 1, 64), ALU.mult)
                if lat:
                    P.tt(qn, qn, nw_bc[:, 0:640], ALU.mult)
                    rope(qkb, qn, rp[tt_ % 2][:, 0:128], 10, 64, t1, t2)
                else:
                    P.tt(qkb[:, 512:640], qn[:, 512:640], nw_bc[:, 512:640], ALU.mult)
                if STG <= 2:
                    continue
                pb = PSb[5]
                for h in range(h0 // 64, 10):
                    P.tr(pb[0:64, (h % 8) * 128:(h % 8 + 1) * 128] if h < 8 else PSb[6][0:64, (h - 8) * 128:(h - 7) * 128],
                         qkb[:, h * 64:(h + 1) * 64], ident_bf)
                if lat:
                    P.cp(stQa[grp].r("p (h t) -> p h t", h=8)[:, :, sl * 128:(sl + 1) * 128], pb[0:64, :].r("p (h t) -> p h t", h=8), eng="act")
                P.cp(stKa[grp].r("p (h t) -> p h t", h=2)[:, :, sl * 128:(sl + 1) * 128], PSb[6][0:64, 0:256].r("p (h t) -> p h t", h=2))
                if STG <= 3:
                    continue
                P.cp(vab.r("p (g d) -> p g d", d=66)[:, :, 0:64], pr[:, 640:768].r("p (g d) -> p g d", d=64), eng="act")
                P.dma(Va[:, :, tt_, :].rearrange("g p d -> p g d"), vab.r("p (g d) -> p g d", d=66))
                if STG <= 4:
                    continue
                P.tt(sq[:, 0:384], pr[:, 768:1152], pr[:, 768:1152], ALU.mult)
                P.red(ss10[:, 10:11], sq[:, 0:256]); P.red(ss10[:, 11:12], sq[:, 256:384])
                P.act(rs10[:, 10:11], ss10[:, 10:11], AF.Sqrt, scale=1.0 / 256.0, bias=epsc)
                P.act(rs10[:, 11:12], ss10[:, 11:12], AF.Sqrt, scale=1.0 / 128.0, bias=epsc)
                P.rcp(rs10[:, 10:12], rs10[:, 10:12])
                P.stt(cqb[:, 0:256], pr[:, 768:1024], rs10[:, 10:11], nw_bc[:, 640:896], ALU.mult, ALU.mult)
                P.stt(cqb[:, 256:384], pr[:, 1024:1152], rs10[:, 11:12], nw_bc[:, 896:1024], ALU.mult, ALU.mult)
                pb7 = PSb[7]
                j0 = 0 if lat else 2
                for j in range(j0, 3):
                    P.tr(pb7[:, j * 128:(j + 1) * 128], cqb[:, j * 128:(j + 1) * 128], ident_bf)
                P.cp(cT[:, j0 * 128:384], pb7[:, j0 * 128:384], eng="act")
                if STG <= 5:
                    continue
                for hh in range(2):
                    P.mm(PS[hh], cT[:, 256:384], w_ukv_b[:, hh * 512:(hh + 1) * 512])
                if lat:
                    rope(kpe, pr[:, 1152:1184], rp[tt_ % 2][:, 128:192], 1, 32, kt1, kt2)
                    kpe_src = kpe
                else:
                    kpe_src = pr[:, 1152:1184]
                SUB = int(os.environ.get('P2_SUB', 99))
                if SUB <= 0:
                    continue
                kmv = kmb.r("p (h f) -> p h f", f=96)
                for hh in range(2):
                    psv = PS[hh].r("p (h f) -> p h f", f=128)
                    if hh == 0:
                        P.cp(kmv[:, 0:4, 0:64], psv[:, :, 0:64], eng="act")
                    else:
                        P.cp(kmv[:, 4:8, 0:64], psv[:, :, 0:64])
                    if SUB <= 1:
                        continue
                    P.cp(vmb.r("p (h f) -> p h f", f=66)[:, hh * 4:(hh + 1) * 4, 0:64], psv[:, :, 64:128], eng="act")
                if SUB <= 2:
                    continue
                for h in range(8):
                    P.cp(kmv[:, h, 64:96], kpe_src, eng=("act" if h % 2 else "dve"))
                if STG <= 6:
                    continue
                P.dma(Vm[:, :, tt_, :].rearrange("h p d -> p h d"), vmb.r("p (h d) -> p h d", d=66))
                if STG <= 7:
                    continue
                for h in range(8):
                    P.tr(PSb[2 + h // 4][0:96, (h % 4) * 128:(h % 4 + 1) * 128] if False else PSb[3][0:96, h * 128:(h + 1) * 128], kmb[:, h * 96:(h + 1) * 96], ident_bf)
                P.cp(stKm[grp].r("p (h t) -> p h t", h=8)[:, :, sl * 128:(sl + 1) * 128], PSb[3][0:96, :].r("p (h t) -> p h t", h=8), eng="act")
                if lat:
                    for (c0, c1, bank) in ((0, 512, 2), (512, 768, 4)):
                        for j in range(2):
                            P.mm(PS[bank][:, 0:c1 - c0], cT[:, j * 128:(j + 1) * 128], w_uq_bv[:, j, c0:c1], start=(j == 0), stop=(j == 1))
                        P.cp(qm[:, c0:c1], PS[bank][:, 0:c1 - c0], eng=("act" if bank == 2 else "dve"))
                    qmv = qm.r("p (h f) -> p h f", f=96)
                    qpe = t1[:, 0:256]; qpo = t1[:, 256:512]
                    P.cp(qpe.r("p (h f) -> p h f", f=32), qmv[:, :, 64:96])
                    rope(qpo, qpe, rp[tt_ % 2][:, 128:192], 8, 32, t2[:, 0:256], t2[:, 256:512])
                    qmbv = qmb.r("p (h f) -> p h f", f=96)
                    P.cp(qmbv[:, :, 0:64], qmv[:, :, 0:64], eng="act")
                    P.cp(qmbv[:, :, 64:96], qpo.r("p (h f) -> p h f", f=32))
                    for h in range(8):
                        P.tr(PSb[4][0:96, h * 128:(h + 1) * 128], qmb[:, h * 96:(h + 1) * 96], ident_bf)
                    P.cp(stQm[grp].r("p (h t) -> p h t", h=8)[:, :, sl * 128:(sl + 1) * 128], PSb[4][0:96, :].r("p (h t) -> p h t", h=8))
                    P.act(gsb[tt_ % 2], pr[:, 1184:2208], AF.Silu)
                    P.dma(Gs[li * 128:(li + 1) * 128, :], gsb[tt_ % 2])
                flush = (sl == 1)
                if flush:
                    g0 = (tt_ // 2) * 2
                    ntk = tt_ - g0 + 1
                    kcol0 = g0 * 128
                    P.dma(KaT[:, :, kcol0:kcol0 + ntk * 128].rearrange("h d t -> d h t"), stKa[grp].r("p (h t) -> p h t", h=2)[:, :, 0:ntk * 128])
                    P.dma(KmT[:, :, kcol0:kcol0 + ntk * 128].rearrange("h d t -> d h t"), stKm[grp].r("p (h t) -> p h t", h=8)[:, :, 0:ntk * 128])
                    if lat:
                        t_first = max(g0, 2)
                        so = (t_first - g0) * 128
                        nq = tt_ - t_first + 1
                        qc0 = (t_first - 2) * 128
                        P.dma(QaT[:, :, qc0:qc0 + nq * 128].rearrange("h d t -> d h t"), stQa[grp].r("p (h t) -> p h t", h=8)[:, :, so:so + nq * 128])
                        P.dma(QmT[:, :, qc0:qc0 + nq * 128].rearrange("h d t -> d h t"), stQm[grp].r("p (h t) -> p h t", h=8)[:, :, so:so + nq * 128])
            if stop_after <= 2:
                P.barrier(bscr[0:1, :], bscr[1:2, :])
                P.emit(st)
                return nc, P

            phase_reset()
            Kb = [alloc("Kb%d" % i, 4352, parts=96, dt=BF16) for i in range(2)]
            Vb_ = [alloc("Vb%d" % i, 34 * 66, dt=BF16) for i in range(2)]
            Qb = [alloc("Qb%d" % i, 512, parts=96, dt=BF16) for i in range(2)]
            Pt2 = [alloc("Pt2_%d" % i, 1024, dt=BF16) for i in range(2)]
            OTs = [alloc("OTs%d" % i, 512, parts=66) for i in range(2)]
            osb = [alloc("osb%d" % i, 256) for i in range(2)]
            rc = [alloc("rc%d" % i, 4) for i in range(2)]
            NQT = int(os.environ.get('P3_QT', 32))
            NGRP = int(os.environ.get('P3_GRPS', 10))

            def load_kv(g):
                kb__ = Kb[g % 2]; vbv__ = Vb_[g % 2].r("p (t d) -> p t d", d=66)
                if g < 2:
                    P.dma(kb__[0:64, :], KaT[g]); P.dma(vbv__, Va[g])
                else:
                    P.dma(kb__[0:96, :], KmT[g - 2]); P.dma(vbv__, Vm[g - 2])

            units = []
            for g in range(NGRP):
                for u in range(NQT if g < 2 else NQT // 4):
                    units.append((g, u))

            def load_q(ui):
                g, u = units[ui]
                qb_ = Qb[ui % 2]
                if g < 2:
                    P.dma(qb_[0:64, :].r("p (h t) -> p h t", h=4),
                          QaT[4 * g:4 * g + 4, :, u * 128:(u + 1) * 128].rearrange("h d t -> d h t"))
                else:
                    P.dma(qb_[0:96, :], QmT[g - 2][:, u * 512:(u + 1) * 512])

            load_kv(0)
            load_q(0)
            itc = {"n": 0}
            for ui, (g, u) in enumerate(units):
                gqa = g < 2
                kd = 64 if gqa else 96
                scale = (64.0 ** -0.5) if gqa else (96.0 ** -0.5)
                kb_ = Kb[g % 2]
                vbv = Vb_[g % 2].r("p (t d) -> p t d", d=66)
                qb = Qb[ui % 2]
                if u == 0 and g + 1 < NGRP:
                    load_kv(g + 1)
                if ui + 1 < len(units):
                    load_q(ui + 1)
                po = PS[4 + ui % 2]
                it0 = itc["n"]

                def qk2(j):
                    it2 = it0 + j
                    sd = PSD[it2 % 2]
                    for t_ in range(2):
                        kb = 2 * j + t_
                        P.mm(PS[2 * (it2 % 2) + t_], kb_[0:kd, kb * 128:(kb + 1) * 128], qb[0:kd, :])
                    P.act(Pt2[it2 % 2], sd[:, 0:1024], AF.Exp, scale=scale,
                          rd=[PS[2 * (it2 % 2)], PS[2 * (it2 % 2) + 1]], wr=[Pt2[it2 % 2]])

                qk2(0)
                for j in range(17):
                    if j + 1 < 17:
                        qk2(j + 1)
                    pt = Pt2[(it0 + j) % 2]
                    for t_ in range(2):
                        kb = 2 * j + t_
                        P.mm(po[0:66, :], vbv[:, kb, :], pt[:, t_ * 512:(t_ + 1) * 512], start=(kb == 0), stop=(kb == 33))
                itc["n"] += 17
                P.cp(OTs[ui % 2], po[0:66, :])
                pto = PS[6 + ui % 2]
                for h in range(4):
                    P.tr(pto[:, h * 66:(h + 1) * 66], OTs[ui % 2][:, h * 128:(h + 1) * 128], ident_f[0:66, 0:66])
                pov = pto[:, 0:264].r("p (h d) -> p h d", d=66)
                P.rcp(rc[ui % 2], pov[:, :, 64])
                ob = osb[ui % 2]
                P.tt(ob.r("p (h d) -> p h d", d=64), pov[:, :, 0:64], bc(rc[ui % 2], 1, 64), ALU.mult)
                if gqa:
                    P.dma(Oall[u * 128:(u + 1) * 128, g * 256:(g + 1) * 256], ob)
                else:
                    hcol = 512 + (g - 2) * 64
                    P.dma(Oall[u * 512:(u + 1) * 512, hcol:hcol + 64].rearrange("(qi p) d -> p qi d", p=128), ob.r("p (h d) -> p h d", d=64))

            phase_reset()
            h1T = alloc("h1T", 8 * 4096, dt=BF16)
            h1Tv = h1T.r("p (k t) -> p k t", k=8)
            mark_h1 = al["off"]
            hT1_tile = make_hT_builder()
            wo_b = alloc("wo_b", 8 * 1024, dt=BF16)
            wo_bv = wo_b.r("p (k n) -> p k n", k=8)
            wst2 = [alloc("wst2_%d" % i, 1024) for i in range(2)]
            for kt in range(8):
                P.dma(wst2[kt % 2], w_out[kt * 128:(kt + 1) * 128, :])
                P.cp(wo_bv[:, kt, :], wst2[kt % 2], eng=("act" if kt % 2 else "dve"))
            ot = [alloc("ot%d" % i, 1024) for i in range(2)]
            gt = [alloc("gt%d" % i, 1024) for i in range(2)]
            xt2 = [alloc("xt2_%d" % i, 1024) for i in range(2)]
            ogb = alloc("ogb", 1024, dt=BF16)
            ogT = alloc("ogT", 1024, dt=BF16)
            xl = [alloc("xl%d" % i, 1024) for i in range(2)]
            for i in range(NQT if int(os.environ.get('P3_B', 1)) else 0):
                b2 = i % 2
                P.dma(ot[b2], Oall[i * 128:(i + 1) * 128, :])
                P.dma(gt[b2], Gs[i * 128:(i + 1) * 128, :])
                P.dma(xt2[b2], x[i * 128:(i + 1) * 128, :])
                P.tt(ogb, ot[b2], gt[b2], ALU.mult)
                for ct in range(8):
                    P.tr(PSb[5][:, ct * 128:(ct + 1) * 128], ogb[:, ct * 128:(ct + 1) * 128], ident_bf)
                P.cp(ogT, PSb[5], eng="act")
                ogTv = ogT.r("p (k t) -> p k t", k=8)
                for hh in range(2):
                    for ct in range(8):
                        P.mm(PS[hh], ogTv[:, ct, :], wo_bv[:, ct, hh * 512:(hh + 1) * 512], start=(ct == 0), stop=(ct == 7))
                    P.tt(xl[b2][:, hh * 512:(hh + 1) * 512], PS[hh], gate_bc[0][:, hh * 512:(hh + 1) * 512], ALU.mult)
                P.tt(xl[b2], xl[b2], xt2[b2], ALU.add)
                P.dma(XL1[i * 128:(i + 1) * 128, :], xl[b2])
                hT1_tile(i, xl[b2], 1, 0, h1Tv, "h1T")
            if stop_after <= 3:
                dbgo = ddbg("dbg3", [128, 8 * 4096], BF16)
                P.dma(dbgo, h1T, rd=[("h1T", i) for i in range(NQT)])
                P.emit(st)
                return nc, P

            NT6 = int(os.environ.get('P6_TILES', 32))
            P.barrier(bscr[0:1, :], bscr[1:2, :])
            al["off"] = mark_h1
            hwl = hy_w_in.rearrange("(kt p) n -> p kt n", p=128)
            wst6 = [alloc("wst6_%d" % i, 1024) for i in range(2)]
            wcb = [alloc("wcb%d" % i, 1024, dt=BF16) for i in range(2)]
            psb = [alloc("psb%d" % i, 4104) for i in range(2)]
            ub = [alloc("ub%d" % i, 4096) for i in range(2)]
            vbf_sb = alloc("vbf_sb", 4096, dt=BF16)
            for i in range(2):
                P.mset(psb[i][:, 0:1], 0.0)
                P.mset(psb[i][:, 4097:4098], 0.0)
            for j in list(range(NT6)) if NT6 == 32 else [0, 8, 16, 24][:NT6]:
                b2 = j % 2
                P.dma(wst6[b2].r("p (k n) -> p k n", k=8), hwl[:, :, j * 128:(j + 1) * 128])
                P.cp(wcb[b2], wst6[b2], eng=("act" if b2 else "dve"))
                wv = wcb[b2].r("p (k n) -> p k n", k=8)
                for tc in range(8):
                    pp = PS[tc % 4]
                    for kt in range(8):
                        P.mm(pp, wv[:, kt, :], h1Tv[:, kt, tc * 512:(tc + 1) * 512],
                             start=(kt == 0), stop=(kt == 7), rd=[wcb[b2]] + [("h1T", tc * 4 + q_) for q_ in range(4)])
                    if j < 24:
                        P.cp(psb[b2][:, 1 + tc * 512:1 + (tc + 1) * 512], pp, eng=("act" if tc % 2 else "dve"))
                    else:
                        P.act(ub[b2][:, tc * 512:(tc + 1) * 512], pp, AF.Silu)
                if j < 24:
                    P.act(ub[b2], psb[b2][:, 1:4097], AF.Identity, scale=colv[:, CV["cw1"] + j:CV["cw1"] + j + 1], bias=colv[:, CV["cb"] + j:CV["cb"] + j + 1])
                    P.stt(ub[b2], psb[b2][:, 0:4096], colv[:, CV["cw0"] + j:CV["cw0"] + j + 1], ub[b2], ALU.mult, ALU.add)
                    P.stt(ub[b2], psb[b2][:, 2:4098], colv[:, CV["cw2"] + j:CV["cw2"] + j + 1], ub[b2], ALU.mult, ALU.add)
                    P.dma(U_[j * 128:(j + 1) * 128, :], ub[b2])
                    if j < 8:
                        P.cp(vbf_sb, ub[b2], eng="act")
                        P.dma(Vbf[j * 128:(j + 1) * 128, :], vbf_sb)
                else:
                    P.dma(GS2[(j - 24) * 128:(j - 23) * 128, :], ub[b2])
            if stop_after <= 6:
                P.emit(st)
                return nc, P

        phase_reset()
        TWO_PI = 2.0 * math.pi
        big5 = alloc("big5", 8192); hk = alloc("hk", 8192)
        w1s = alloc("w1s", 64, parts=33); embs = big5[0:33, :]
        P.dma(w1s, f_w1); P.dma(embs, embT)
        w2f = alloc("w2f", 64, parts=64); w2b = alloc("w2b", 64, parts=64, dt=BF16)
        P.dma(w2f, f_w2); P.cp(w2b, w2f)
        w3f = hk[0:64, 0:4096]; w3b = alloc("w3b", 4096, parts=64, dt=BF16)
        P.dma(w3f, f_w3); P.cp(w3b, w3f)
        fs = alloc("fs", 4, parts=64)
        P.ts(fs[:, 0:1], smv[:, 0:1], 1.0 / TWO_PI, None, ALU.mult)
        P.tt(fs[:, 1:2], fs[:, 0:1], smv[:, 1:2], ALU.mult)
        P.tt(fs[:, 2:3], fs[:, 0:1], smv[:, 2:3], ALU.mult)
        hid1 = alloc("hid1", 8192, parts=64, dt=BF16); hid2 = alloc("hid2", 8192, parts=64, dt=BF16)
        ubuf = alloc("ubuf", 512, parts=64); ibuf = alloc("ibuf", 512, parts=64, dt=I32); rbuf = alloc("rbuf", 512, parts=64)

        def sin_layer(dst, lhsT, rhs_fn, bias_col):
            for ncx in range(16):
                pp = PS[ncx % 2]
                P.mm(pp[0:64, :], lhsT, rhs_fn(ncx))
                P.act(ubuf, pp[0:64, :], AF.Identity, scale=fs[:, 0:1], bias=fs[:, bias_col:bias_col + 1])
                P.cp(ibuf, ubuf)
                P.cp(rbuf, ibuf)
                P.tt(ubuf, ubuf, rbuf, ALU.subtract)
                P.act(dst[:, ncx * 512:(ncx + 1) * 512], ubuf, AF.Sin, scale=TWO_PI)

        sin_layer(hid1, w1s, lambda ncx: embs[:, ncx * 512:(ncx + 1) * 512], 1)
        sin_layer(hid2, w2b, lambda ncx: hid1[:, ncx * 512:(ncx + 1) * 512], 2)
        tl_bc = big5
        P.dma(tl_bc, tlsgn[0].partition_broadcast(128))
        sdec = alloc("sdec", 8192)
        ksum = alloc("ksum", 1); kernb = alloc("kernb", 8192, dt=BF16); kjunk = kernb
        NCT5 = int(os.environ.get('P5_CT', 8))
        for ct in range(NCT5):
            P.act(sdec, tl_bc, AF.Exp, scale=colv[:, CV["ndelta"] + ct:CV["ndelta"] + ct + 1])
            P.ts(sdec[:, 4096:8192], sdec[:, 4096:8192], -1.0, None, ALU.mult)
            P.mset(sdec[:, 4096:4097], 0.0)
            for o in range(2):
                for dr in range(2):
                    col = (o * 2 + dr) * 1024 + ct * 128
                    bcol = CV["b3"] + (o * 2 + dr) * 8 + ct
                    for ncx in range(8):
                        n0 = dr * 4096 + ncx * 512
                        pp = PS[ncx % 2]
                        P.mm(pp, w3b[:, col:col + 128], hid2[:, n0:n0 + 512])
                        P.act(hk[:, n0:n0 + 512], pp, AF.Identity, bias=colv[:, bcol:bcol + 1])
                P.tt(hk, hk, sdec, ALU.mult)
                P.act(kjunk, hk, AF.Abs)
                P.red(ksum, kjunk)
                P.rcp(ksum, ksum)
                P.act(kernb, hk, AF.Copy, scale=ksum)
                P.dma(KERN[o, ct * 128:(ct + 1) * 128, :], kernb)
        if stop_after <= 5:
            d5 = ddbg("dbg5", [128, 64])
            P.dma(d5[:, 0:16], hk[:, 0:16]); P.dma(d5[:, 16:32], sdec[:, 0:16]); P.dma(d5[:, 32:33], ksum, allow_slow_non_contiguous=True)
            d5b = ddbg("dbg5b", [64, 64], BF16)
            P.dma(d5b[:, 0:32], hid1[:, 0:32]); P.dma(d5b[:, 32:64], hid2[:, 0:32])
            P.emit(st)
            return nc, P

        phase_reset()
        F1reg = alloc("F1reg", 128, dt=BF16)
        P.dma(F1reg[0:64, :], F1c); P.dma(F1reg[64:128, :], F1c)
        F1u = F1reg[64:128, :]
        F2s = alloc("F2s", 16384, dt=BF16); P.dma(F2s[:, 0:8192], F2c[:, 0:8192]); P.dma(F2s[:, 8192:16384], F2c[:, 8192:16384])
        F2v = F2s.r("p (k w m) -> p k w m", k=64, w=2)
        Gs_ = alloc("Gs_", 512, dt=BF16); P.dma(Gs_, Gc)
        Gv = Gs_.r("p (h w n) -> p h w n", h=2, w=2)
        Hreg = alloc("Hreg", 8192, dt=BF16)
        Hs = Hreg[0:64, :].k("Hs"); P.dma(Hs, Hc)
        D1buf = Hreg[64:128, :].k("D1buf")
        Hv = Hs.r("p (b n) -> p b n", n=64)
        bufA = alloc("bufA", 8192, dt=BF16)
        bufB = alloc("bufB", 8192, dt=BF16)
        Zbuf = alloc("Zbuf", 8192, parts=64, dt=BF16)
        zb = alloc("zb", 4096, dt=BF16)
        evc = {"n": 0}

        def evac(dst, src):
            evc["n"] += 1
            P.cp(dst, src, eng=("act" if evc["n"] % 2 else "dve"))

        def fwd_stages(src_rows, K, on_bank):
            D1 = D1buf[0:K, :].r("p (c b) -> p c b", b=128)
            for q_ in range(4):
                P.dma(D1[:, q_ * 16:(q_ + 1) * 16, :], src_rows[q_ * 16:(q_ + 1) * 16, :].rearrange("c (a b) -> a c b", b=128))
            Bv = bufB.r("p (k c) -> p k c", c=64)
            for c4 in range(16):
                pp = PS[c4 % 2]
                for cc in range(4):
                    P.mm(pp[:, cc * 128:(cc + 1) * 128], D1[:, c4 * 4 + cc, :], F1u[0:K, :], start=(cc == 0), stop=True, skip_group_check=True)
                evac(bufB.r("p (k c) -> p c k", c=64)[:, c4 * 4:(c4 + 1) * 4, :], pp.r("p (c k) -> p c k", c=4))
            for k8 in range(8):
                pp = PS[2 + k8 % 2]
                for kk in range(8):
                    kap = k8 * 8 + kk
                    P.mm(pp[:, kk * 64:(kk + 1) * 64], F2v[:, kap, 0, :], Bv[:, kap, :], start=(kk == 0), stop=False, skip_group_check=True)
                    P.mm(pp[:, kk * 64:(kk + 1) * 64], F2v[:, kap, 1, :], Bv[:, 64 + kap, :], start=False, stop=True, skip_group_check=True)
                on_bank(k8, pp)

        ksb = zb
        NHC = int(os.environ.get('P7_HC', 16))
        for o in range(2):
            for hc in range(NHC):
                def fb(k8, pp):
                    evac(ksb[:, k8 * 512:(k8 + 1) * 512], pp)
                fwd_stages(KERN[o, hc * 64:(hc + 1) * 64, :], 64, fb)
                P.dma(KSP[o, hc], ksb)
        if stop_after <= 7:
            P.emit(st)
            return nc, P

        KAs = alloc("KAs", 4096, dt=BF16); KBs = alloc("KBs", 4096, dt=BF16)
        ysb = alloc("ysb", 4096); z1s = alloc("z1s", 4096)
        ld = [alloc("ld%d" % i, 2048) for i in range(2)]
        P1 = bufA[:, 0:4096]; P2 = bufA[:, 4096:8192]
        P1c = P1.r("p (k c) -> p c k", c=64); P2c = P2.r("p (k c) -> p c k", c=64)
        Zv = Zbuf.r("p (b r c) -> p b r c", r=2, c=64)
        Zw = Zbuf.r("p (b r c) -> p c r b", r=2, c=64)

        def conv(src_dram, o, ct):
            for half in range(2):
                hc = ct * 2 + half
                ks = KSP[o, hc]
                P.dma(KAs[0:64, :], ks[0:64, :]); P.dma(KAs[64:128, :], ks[0:64, :])
                P.dma(KBs[0:64, :], ks[64:128, :]); P.dma(KBs[64:128, :], ks[64:128, :])

                def ob(k8, pp):
                    P.tt(P1[:, k8 * 512:(k8 + 1) * 512], pp, KAs[:, k8 * 512:(k8 + 1) * 512], ALU.mult)
                    P.tt(P2[:, k8 * 512:(k8 + 1) * 512], pp, KBs[:, k8 * 512:(k8 + 1) * 512], ALU.mult)
                fwd_stages(src_dram[hc * 64:(hc + 1) * 64, :], 32, ob)
                for bh in range(2):
                    for c4 in range(16):
                        pp = PS[4 + c4 % 2]
                        for cc in range(4):
                            c = c4 * 4 + cc
                            P.mm(pp[0:64, cc * 128:(cc + 1) * 128], P1c[:, c, :], Gv[:, bh, 0, :], start=(cc == 0), stop=False, skip_group_check=True)
                            P.mm(pp[0:64, cc * 128:(cc + 1) * 128], P2c[:, c, :], Gv[:, bh, 1, :], start=False, stop=True, skip_group_check=True)
                        evac(Zw[:, c4 * 4:(c4 + 1) * 4, :, :], pp[0:64, :].r("p (c r b) -> p c r b", c=4, r=2))
                    for b16 in range(4):
                        pp = PS[6 + b16 % 2]
                        for bb in range(16):
                            b = b16 * 16 + bb
                            bg = bh * 64 + b
                            P.mm(pp[0:64, bb * 32:(bb + 1) * 32], Zv[:, b, 0, :], Hv[:, bg, 0:32], start=(bb == 0), stop=False, skip_group_check=True)
                            P.mm(pp[0:64, bb * 32:(bb + 1) * 32], Zv[:, b, 1, :], Hv[:, bg, 32:64], start=False, stop=True, skip_group_check=True)
                        bg0 = bh * 64 + b16 * 16
                        P.cp(ysb[half * 64:(half + 1) * 64, :].r("p (a b) -> p b a", b=128)[:, bg0:bg0 + 16, :], pp[0:64, :].r("p (b a) -> p b a", a=32))

        NCT7 = int(os.environ.get('P7_CT', 8))
        for ct in range(NCT7):
            conv(Vbf, 0, ct)
            for tq in range(2):
                tsl = slice(tq * 2048, (tq + 1) * 2048)
                P.dma(ld[0], U_[ct * 128:(ct + 1) * 128, tsl])
                P.dma(ld[1], U_[1024 + ct * 128:1024 + (ct + 1) * 128, tsl])
                P.stt(ysb[:, tsl], ld[0], colv[:, CV["sk0"] + ct:CV["sk0"] + ct + 1], ysb[:, tsl], ALU.mult, ALU.add)
                P.tt(z1s[:, tsl], ysb[:, tsl], ld[1], ALU.mult)
            P.cp(zb, z1s, eng="act")
            P.dma(Z1B[ct * 128:(ct + 1) * 128, :], zb)
            conv(Z1B, 1, ct)
            for tq in range(2):
                tsl = slice(tq * 2048, (tq + 1) * 2048)
                P.dma(ld[0], U_[2048 + ct * 128:2048 + (ct + 1) * 128, tsl])
                P.dma(ld[1], GS2[ct * 128:(ct + 1) * 128, tsl])
                P.stt(ysb[:, tsl], z1s[:, tsl], colv[:, CV["sk1"] + ct:CV["sk1"] + ct + 1], ysb[:, tsl], ALU.mult, ALU.add)
                P.tt(ysb[:, tsl], ysb[:, tsl], ld[0], ALU.mult)
                P.tt(zb[:, tsl], ysb[:, tsl], ld[1], ALU.mult)
            P.dma(ZF[ct * 128:(ct + 1) * 128, :], zb)
        if stop_after <= 8:
            P.emit(st)
            return nc, P

        phase_reset()
        wob2 = alloc("wob2", 8 * 1024, dt=BF16)
        wob2v = wob2.r("p (k n) -> p k n", k=8)
        wst8 = [alloc("wst8_%d" % i, 1024) for i in range(2)]
        for kt in range(8):
            P.dma(wst8[kt % 2], hy_w_out[kt * 128:(kt + 1) * 128, :])
            P.cp(wob2v[:, kt, :], wst8[kt % 2], eng=("act" if kt % 2 else "dve"))
        zT = [alloc("zT%d" % i, 1024, dt=BF16) for i in range(2)]
        xl1t = [alloc("xl1t%d" % i, 1024) for i in range(2)]
        x2t = [alloc("x2t%d" % i, 1024) for i in range(2)]
        ot8 = [alloc("ot8_%d" % i, 1024) for i in range(2)]
        junk8 = alloc("junk8", 1024); ss8 = [alloc("ss8_%d" % i, 1) for i in range(2)]
        ZFv = ZF.rearrange("(k p) t -> p k t", p=128)
        for i in range(32):
            b2 = i % 2
            zTv = zT[b2].r("p (k t) -> p k t", k=8)
            P.dma(zTv, ZFv[:, :, i * 128:(i + 1) * 128])
            P.dma(xl1t[b2], XL1[i * 128:(i + 1) * 128, :])
            for hh in range(2):
                for ct in range(8):
                    P.mm(PS[hh], zTv[:, ct, :], wob2v[:, ct, hh * 512:(hh + 1) * 512], start=(ct == 0), stop=(ct == 7))
                P.tt(x2t[b2][:, hh * 512:(hh + 1) * 512], PS[hh], gate_bc[1][:, hh * 512:(hh + 1) * 512], ALU.mult)
            P.tt(x2t[b2], x2t[b2], xl1t[b2], ALU.add)
            P.act(junk8, x2t[b2], AF.Square)
            P.red(ss8[b2], junk8)
            P.act(ss8[b2], ss8[b2], AF.Sqrt, scale=1.0 / 1024.0, bias=epsc)
            P.rcp(ss8[b2], ss8[b2])
            P.stt(ot8[b2], x2t[b2], ss8[b2], fnw_bc, ALU.mult, ALU.mult)
            P.dma(out[i * 128:(i + 1) * 128, :], ot8[b2], is_output=True)
        P.emit(st)
        return nc, P


def kernel(**inputs):
    nc, _ = build()
    in_maps = [pack_inputs(inputs, b) for b in range(8)]
    res = run_bass_kernel_spmd(nc, in_maps, core_ids=list(range(8)))
    return np.stack([np.asarray(r["out"], dtype=np.float32) for r in res.results], 0)
```

```python
import numpy as np
import concourse.bass as bass
import concourse.mybir as mybir
from concourse.bass_utils import run_bass_kernel_spmd
from contextlib import ExitStack

F32 = mybir.dt.float32
BF16 = mybir.dt.bfloat16
AF = mybir.ActivationFunctionType
ALU = mybir.AluOpType
AX = mybir.AxisListType

NDMA_SLOTS = 24


class _Op:
    __slots__ = ("eng", "idx", "fn", "deps", "is_dma", "dma_id", "waits", "signal", "snap", "semval")

    def __init__(self, eng, idx, fn, deps, is_dma, dma_id):
        self.eng = eng
        self.idx = idx
        self.fn = fn
        self.deps = deps
        self.is_dma = is_dma
        self.dma_id = dma_id
        self.waits = []
        self.signal = False
        self.snap = None
        self.semval = 0


class V:
    __slots__ = ("ap", "key")

    def __init__(self, ap, key):
        self.ap = ap
        self.key = key

    def __getitem__(self, idx):
        return V(self.ap[idx], self.key)

    def r(self, pat, **kw):
        return V(self.ap.rearrange(pat, **kw), self.key)

    def k(self, key):
        return V(self.ap, key)

    def bitcast(self, dt):
        return V(self.ap.bitcast(dt), self.key)


def U(a):
    return a.ap if isinstance(a, V) else a


def bc(a, pos, n):
    ap = U(a)
    dims = [list(d) for d in ap.ap]
    dims.insert(1 + pos, [0, n])
    r = bass.AP(ap.tensor, ap.offset, dims)
    return V(r, a.key) if isinstance(a, V) else r


class Prog:
    ENGS = ("pe", "act", "dve", "pool", "sp")

    def __init__(self, nc, same_engine_sync=True):
        self.nc = nc
        self.streams = {e: [] for e in self.ENGS}
        self.order = []
        self.last_writer = {}
        self.readers = {}
        self.ndma = 0
        self.dma_ops = []
        self.same_engine_sync = same_engine_sync
        self.out_dmas = []
        self.barrier_op = None

    @staticmethod
    def _keys(aps):
        ks = []
        for a in aps:
            if a is None or isinstance(a, (int, float)):
                continue
            if isinstance(a, (str, tuple)):
                ks.append(a)
            elif isinstance(a, V):
                ks.append(a.key)
            elif hasattr(a, "tensor"):
                ks.append(a.tensor.name)
            else:
                ks.append(a.name)
        return ks

    def add(self, eng, fn, rd, wr, is_dma=False):
        rd = self._keys(rd)
        wr = self._keys(wr)
        wr = wr + [k for k in rd if isinstance(k, str) and k.startswith("ps") and k not in wr]
        deps = []
        seen = set()

        def _dep(o):
            if o is not None and id(o) not in seen:
                seen.add(id(o))
                deps.append(o)

        for k in rd:
            _dep(self.last_writer.get(k))
        for k in wr:
            _dep(self.last_writer.get(k))
            for r in self.readers.get(k, ()):
                _dep(r)
        _dep(self.barrier_op)
        dma_id = None
        if is_dma:
            dma_id = self.ndma
            self.ndma += 1
            if dma_id >= NDMA_SLOTS:
                _dep(self.dma_ops[dma_id - NDMA_SLOTS])
        op = _Op(eng, len(self.streams[eng]), fn, deps, is_dma, dma_id)
        if is_dma:
            self.dma_ops.append(op)
        self.streams[eng].append(op)
        self.order.append(op)
        for k in wr:
            self.last_writer[k] = op
            self.readers[k] = []
        for k in rd:
            self.readers.setdefault(k, []).append(op)
        return op

    def mm(self, out, lhsT, rhs, start=True, stop=True, rd=None, wr=None, **kw):
        o_, l_, r_ = U(out), U(lhsT), U(rhs)
        return self.add("pe", lambda e: e.matmul(o_, l_, r_, start=start, stop=stop, **kw),
                        rd if rd is not None else [lhsT, rhs], wr if wr is not None else [out])

    def tr(self, out, in_, ident, rd=None, wr=None):
        o_, i_, d_ = U(out), U(in_), U(ident)
        return self.add("pe", lambda e: e.transpose(o_, i_, d_),
                        rd if rd is not None else [in_, ident], wr if wr is not None else [out])

    def tt(self, out, in0, in1, op, eng="dve"):
        o_, a_, b_ = U(out), U(in0), U(in1)
        return self.add(eng, lambda e: e.tensor_tensor(o_, a_, b_, op), [in0, in1], [out])

    def ts(self, out, in0, s1, s2, op0, op1=None, eng="dve"):
        o_, a_, s1_, s2_ = U(out), U(in0), U(s1), U(s2)
        if op1 is None:
            return self.add(eng, lambda e: e.tensor_scalar(o_, a_, s1_, None, op0), [in0, s1], [out])
        return self.add(eng, lambda e: e.tensor_scalar(o_, a_, s1_, s2_, op0, op1), [in0, s1, s2], [out])

    def stt(self, out, in0, sc, in1, op0, op1, eng="dve"):
        o_, a_, s_, b_ = U(out), U(in0), U(sc), U(in1)
        return self.add(eng, lambda e: e.scalar_tensor_tensor(o_, a_, s_, b_, op0, op1), [in0, sc, in1], [out])

    def cp(self, out, in_, eng="dve"):
        o_, i_ = U(out), U(in_)
        if eng == "act":
            return self.add(eng, lambda e: e.copy(o_, i_), [in_], [out])
        return self.add(eng, lambda e: e.tensor_copy(o_, i_), [in_], [out])

    def red(self, out, in_, op=None, axis=None, eng="dve"):
        o_, i_ = U(out), U(in_)
        op = op or ALU.add
        axis = axis or AX.X
        return self.add(eng, lambda e: e.tensor_reduce(o_, i_, axis, op), [in_], [out])

    def rcp(self, out, in_):
        o_, i_ = U(out), U(in_)
        return self.add("dve", lambda e: e.reciprocal(o_, i_), [in_], [out])

    def mset(self, out, val, eng="dve"):
        o_ = U(out)
        return self.add(eng, lambda e: e.memset(o_, val), [], [out])

    def barrier(self, scr_out, scr_in):
        deps = []
        for e in self.ENGS:
            if self.streams[e]:
                deps.append(self.streams[e][-1])
        last = {}
        for d in self.dma_ops:
            last[d.dma_id % NDMA_SLOTS] = d
        deps.extend(last.values())
        op = self.add("sp", lambda e: e.dma_start(out=scr_out, in_=scr_in), [], [], is_dma=True)
        ids = set(id(x) for x in op.deps)
        for d in deps:
            if id(d) not in ids and d is not op:
                op.deps.append(d)
        self.barrier_op = op
        return op

    def act(self, out, in_, func, bias=None, scale=None, accum_out=None, rd=None, wr=None, eng="act"):
        kw = {}
        if bias is not None:
            kw["bias"] = U(bias)
        if scale is not None:
            kw["scale"] = U(scale)
        if accum_out is not None:
            kw["accum_out"] = U(accum_out)
        r = [in_, bias, scale] if rd is None else rd
        w = [out, accum_out] if wr is None else wr
        o_, i_ = U(out), U(in_)
        return self.add(eng, lambda e: e.activation(o_, i_, func, **kw), r, w)

    def v(self, fn, rd, wr, eng="dve"):
        return self.add(eng, fn, rd, wr)

    def dma(self, out, in_, rd=None, wr=None, q=None, is_output=False, **kw):
        if q is None:
            q = "sp" if isinstance(out, V) else "pool"
        o_, i_ = U(out), U(in_)
        op = self.add(q, lambda e: e.dma_start(out=o_, in_=i_, **kw),
                      rd if rd is not None else [in_], wr if wr is not None else [out], is_dma=True)
        if is_output:
            self.out_dmas.append(op)
        return op

    def emit(self, stack):
        nc = self.nc
        esem = {e: stack.enter_context(nc.semaphore("s_" + e)) for e in self.ENGS}
        dsem = [stack.enter_context(nc.semaphore("d_%d" % i)) for i in range(NDMA_SLOTS)]
        know = {e: ({x: -1 for x in self.ENGS}, {}) for e in self.ENGS}
        for op in self.order:
            kc, kd = know[op.eng]
            for d in op.deps:
                if d.is_dma:
                    if kd.get(d.dma_id % NDMA_SLOTS, -1) >= d.dma_id:
                        continue
                    op.waits.append(d)
                    kd[d.dma_id % NDMA_SLOTS] = d.dma_id
                    sc, sd = d.snap
                    for x, v_ in sc.items():
                        if v_ > kc[x]:
                            kc[x] = v_
                    for x, v_ in sd.items():
                        if v_ > kd.get(x, -1):
                            kd[x] = v_
                else:
                    if d.eng == op.eng and (op.eng == "pe" or not self.same_engine_sync or op.is_dma and False):
                        continue
                    if kc[d.eng] >= d.idx:
                        continue
                    op.waits.append(d)
                    d.signal = True
                    sc, sd = d.snap
                    for x, v_ in sc.items():
                        if v_ > kc[x]:
                            kc[x] = v_
                    for x, v_ in sd.items():
                        if v_ > kd.get(x, -1):
                            kd[x] = v_
                    if d.idx > kc[d.eng]:
                        kc[d.eng] = d.idx
            if not op.is_dma:
                snapc = dict(kc)
                snapc[op.eng] = op.idx
                op.snap = (snapc, dict(kd))
            else:
                snapc = dict(kc)
                op.snap = (snapc, dict(kd))
        for e in self.ENGS:
            c = 0
            for op in self.streams[e]:
                if op.is_dma:
                    continue
                if op.signal:
                    c += 1
                op.semval = c
        nwaits = sum(len(o.waits) for o in self.order)
        self.stats = dict(nops=len(self.order), nwaits=nwaits, ndma=self.ndma,
                          per_eng={e: len(s) for e, s in self.streams.items()})

        def run_stream(e):
            def body(eng):
                for op in self.streams[e]:
                    for d in op.waits:
                        if d.is_dma:
                            eng.wait_ge(dsem[d.dma_id % NDMA_SLOTS], 16 * (d.dma_id // NDMA_SLOTS + 1))
                        else:
                            eng.wait_ge(esem[d.eng], d.semval)
                    ins = op.fn(eng)
                    if op.is_dma:
                        ins.then_inc(dsem[op.dma_id % NDMA_SLOTS], 16)
                    elif op.signal:
                        ins.then_inc(esem[e], 1)
                if e == "sp":
                    last = {}
                    for d in self.dma_ops:
                        last[d.dma_id % NDMA_SLOTS] = d
                    for s, d in last.items():
                        eng.wait_ge(dsem[s], 16 * (d.dma_id // NDMA_SLOTS + 1))
            return body

        with nc.Block() as block:
            block.tensor(run_stream("pe"))
            block.scalar(run_stream("act"))
            block.vector(run_stream("dve"))
            block.gpsimd(run_stream("pool"))
            block.sync(run_stream("sp"))


import math
import ml_dtypes

I32 = mybir.dt.int32
NT_LAT = 32
NT_ALL = 34
EPS = 1e-6

CV = dict(c=0, c_ctx=8, nw0=16, nw1=24, adab0=32, adab1=56, cw0=80, cw1=104, cw2=128, cb=152,
          sk0=176, sk1=184, b3=192, ndelta=224)


def _rope_tab(rot_dim):
    L, GW = 4096, 64
    rows = np.repeat(np.arange(L // GW, dtype=np.int32), GW).astype(np.float32)
    cols = np.tile(np.arange(GW, dtype=np.int32), L // GW).astype(np.float32)
    q = rot_dim // 4
    inv = (np.float32(10000.0) ** (-np.arange(q, dtype=np.float32) / np.float32(q))).astype(np.float32)
    ar = (rows[:, None] * inv).astype(np.float32)
    ac = (cols[:, None] * inv).astype(np.float32)
    cr, sr, cc, sc = np.cos(ar), np.sin(ar), np.cos(ac), np.sin(ac)
    C = np.concatenate([cr, cr, cc, cc], 1)
    S = np.concatenate([-sr, sr, -sc, sc], 1)
    return np.concatenate([C, S], 1).astype(np.float32)


_CONSTS = None


def host_consts():
    global _CONSTS
    if _CONSTS is not None:
        return _CONSTS
    bf = ml_dtypes.bfloat16
    c = {}
    c["ident_bf"] = np.eye(128, dtype=np.float32).astype(bf)
    c["ident_f"] = np.eye(128, dtype=np.float32)
    c["ropeA"] = _rope_tab(64)
    c["ropeM"] = _rope_tab(32)
    L = 4096
    n = np.arange(8192)
    tpos = np.where(n < L, n, np.where(n == L, 0, 8192 - n))
    tl = np.linspace(0.0, 1.0, L, dtype=np.float32)
    w = ((2.0 * math.pi / L) * np.arange(L, dtype=np.float32)).astype(np.float32)
    bands = np.linspace(1e-4, 15, 16, dtype=np.float32)
    emb = np.concatenate([tl[:, None], np.cos(w[:, None] * bands), -np.sin(w[:, None] * bands)], 1).astype(np.float32)
    c["embT"] = np.ascontiguousarray(emb[tpos].T)
    sgn = np.where(n < L, 1.0, np.where(n == L, 0.0, -1.0)).astype(np.float32)
    c["tlsgn"] = np.stack([tl[tpos], sgn]).astype(np.float32)
    a = np.arange(64)[:, None]
    kap = np.arange(64)[None, :]
    f1 = np.exp(-2j * np.pi * a * (2 * kap + 1) / 128.0)
    c["F1c"] = np.concatenate([f1.real, f1.imag], 1).astype(np.float32).astype(bf)
    b = np.arange(128)[:, None, None]
    kp = np.arange(64)[None, :, None]
    be = np.arange(64)[None, None, :]
    f2 = np.exp(-2j * np.pi * (b * be / 128.0 + b * (kp + 0.5) / 8192.0))
    f2a = np.concatenate([f2.real, f2.imag], 2)
    f2b = np.concatenate([-f2.imag, f2.real], 2)
    c["F2c"] = np.stack([f2a, f2b], 2).reshape(128, 64 * 2 * 128).astype(np.float32).astype(bf)
    be2 = np.arange(64)[:, None]
    bb = np.arange(128)[None, :]
    g = np.exp(2j * np.pi * bb * be2 / 128.0)
    G = np.zeros((128, 2, 2, 128), np.float64)
    for bh in range(2):
        gs = g[:, bh * 64:(bh + 1) * 64]
        g1 = np.concatenate([np.concatenate([gs.real, gs.imag], 1), np.concatenate([-gs.imag, gs.real], 1)], 0)
        g2 = np.concatenate([g1[64:], -g1[:64]], 0)
        G[:, bh, 0, :] = g1
        G[:, bh, 1, :] = g2
    c["Gc"] = G.reshape(128, 512).astype(np.float32).astype(bf)
    kq = np.arange(64)[:, None, None]
    b3 = np.arange(128)[None, :, None]
    a3 = np.arange(32)[None, None, :]
    h = (2.0 / 8192.0) * np.exp(2j * np.pi * (a3 * (2 * kq + 1) / 128.0 + b3 * (kq + 0.5) / 8192.0))
    c["Hc"] = np.concatenate([h.real, -h.imag], 2).reshape(64, 128 * 64).astype(np.float32).astype(bf)
    deltas = np.linspace(math.log(1e-2) / 1.5, math.log(1e-2) / 0.3, 1024, dtype=np.float32)
    c["_ndelta"] = (-np.abs(deltas)).astype(np.float32)
    _CONSTS = c
    return c


def pack_inputs(inp, b):
    c = host_consts()
    f = lambda a: np.ascontiguousarray(np.asarray(a, dtype=np.float32))
    vec = np.zeros((256, 128), np.float32)

    def put(name, arr):
        arr = f(arr).reshape(-1, 128)
        vec[CV[name]:CV[name] + arr.shape[0]] = arr

    put("c", inp["c"][b]); put("c_ctx", inp["c_ctx"]); put("nw0", inp["norm_w"][0]); put("nw1", inp["norm_w"][1])
    put("adab0", inp["ada_b"][0]); put("adab1", inp["ada_b"][1])
    put("cw0", inp["hy_conv_w"][0][0]); put("cw1", inp["hy_conv_w"][0][1]); put("cw2", inp["hy_conv_w"][0][2])
    put("cb", inp["hy_conv_b"][0]); put("sk0", inp["hy_skip"][0][0]); put("sk1", inp["hy_skip"][0][1])
    put("b3", inp["hy_ffn_b3"][0]); put("ndelta", c["_ndelta"])
    smallv = np.zeros((64, 4), np.float32)
    smallv[:, 0] = f(inp["hy_freq"][0]); smallv[:, 1] = f(inp["hy_ffn_b1"][0]); smallv[:, 2] = f(inp["hy_ffn_b2"][0])
    rowv = np.zeros((4, 1024), np.float32)
    rowv[0, :640] = np.concatenate([np.tile(f(inp["attn_q_norm"][0]), 8), np.tile(f(inp["attn_k_norm"][0]), 2)])
    rowv[0, 640:896] = f(inp["mla_q_norm"][0]); rowv[0, 896:1024] = f(inp["mla_kv_norm"][0])
    rowv[1] = f(inp["final_norm_w"]); rowv[2] = f(inp["ada_b"][0][2048:]); rowv[3] = f(inp["ada_b"][1][2048:])
    d = dict(x=f(inp["x"][b]), ctx=f(inp["ctx"][b]), vecs=vec, smallv=smallv, rowv=rowv,
             ada_w=f(inp["ada_w"]), w_in=f(inp["attn_w_in"][0]), w_uq=f(inp["mla_w_uq"][0]),
             w_ukv=f(inp["mla_w_ukv"][0]), w_out=f(inp["attn_w_out"][0]), hy_w_in=f(inp["hy_w_in"][0]),
             f_w1=f(inp["hy_ffn_w1"][0]), f_w2=f(inp["hy_ffn_w2"][0]), f_w3=f(inp["hy_ffn_w3"][0]),
             hy_w_out=f(inp["hy_w_out"][0]))
    for k in ("ident_bf", "ident_f", "ropeA", "ropeM", "embT", "tlsgn", "F1c", "F2c", "Gc", "Hc"):
        d[k] = c[k]
    return d


ARENA = 50000
DBGSET = {6: ('U_', 'GS2'), 5: ('KERN',), 2: ('QaT', 'KaT', 'QmT', 'KmT', 'Vm', 'Gs'), 3: ('Oall', 'XL1'), 7: ('KSP',), 8: ('ZF', 'Z1B')}


def build(dbg=False, stop_after=99):
    nc = bass.Bass("TRN2", target_bir_lowering=False)
    skind = "Internal"

    def din(name, shape, dt=F32):
        return nc.dram_tensor(name, list(shape), dt, kind="ExternalInput").ap()

    def dscr(name, shape, dt=F32):
        kind = "ExternalOutput" if (dbg and name in DBGSET.get(stop_after, ())) else "Internal"
        return nc.dram_tensor(name, list(shape), dt, kind=kind).ap()

    def ddbg(name, shape, dt=F32):
        return nc.dram_tensor(name, list(shape), dt, kind="ExternalOutput").ap()

    x = din("x", [4096, 1024]); ctx = din("ctx", [256, 1024])
    vecs = din("vecs", [256, 128]); smallv = din("smallv", [64, 4]); rowv = din("rowv", [4, 1024])
    ada_w = din("ada_w", [2, 1024, 3072]); w_in = din("w_in", [1024, 2208]); w_uq = din("w_uq", [256, 768])
    w_ukv = din("w_ukv", [128, 1024]); w_out = din("w_out", [1024, 1024]); hy_w_in = din("hy_w_in", [1024, 4096])
    f_w1 = din("f_w1", [33, 64]); f_w2 = din("f_w2", [64, 64]); f_w3 = din("f_w3", [64, 4096])
    hy_w_out = din("hy_w_out", [1024, 1024])
    ident_bf_d = din("ident_bf", [128, 128], BF16); ident_f_d = din("ident_f", [128, 128])
    ropeA = din("ropeA", [4096, 128]); ropeM = din("ropeM", [4096, 64]); embT = din("embT", [33, 8192])
    tlsgn = din("tlsgn", [2, 8192]); F1c = din("F1c", [64, 128], BF16); F2c = din("F2c", [128, 16384], BF16)
    Gc = din("Gc", [128, 512], BF16); Hc = din("Hc", [64, 8192], BF16)
    out = nc.dram_tensor("out", [4096, 1024], F32, kind="ExternalOutput").ap()

    QaT = dscr("QaT", [8, 64, 4096], BF16); KaT = dscr("KaT", [2, 64, 4352], BF16)
    Va = dscr("Va", [2, 128, 34, 66], BF16)
    QmT = dscr("QmT", [8, 96, 4096], BF16); KmT = dscr("KmT", [8, 96, 4352], BF16)
    Vm = dscr("Vm", [8, 128, 34, 66], BF16)
    Gs = dscr("Gs", [4096, 1024]); Oall = dscr("Oall", [4096, 1024]); XL1 = dscr("XL1", [4096, 1024])
    bscr = dscr("bscr", [2, 16])

    st = ExitStack()
    with st:
        Aten = st.enter_context(nc.sbuf_tensor("A", [128, ARENA], F32))
        PSD = [st.enter_context(nc.psum_tensor("psd%d" % i, [128, 1024], F32)) for i in range(4)]
        PS = [V(PSD[i // 2][:, (i % 2) * 512:(i % 2 + 1) * 512], "ps%d" % i) for i in range(8)]
        PSb = [V(PSD[i // 2][:, (i % 2) * 512:(i % 2 + 1) * 512].bitcast(BF16), "ps%d" % i) for i in range(8)]
        P = Prog(nc)
        al = {"off": 0, "base": 0, "n": 0}

        def alloc(name, ncols, parts=128, dt=F32):
            nf = ncols if dt == F32 or dt == I32 else (ncols + 1) // 2
            nf = (nf + 7) // 8 * 8
            o = al["off"]
            assert o + nf <= ARENA, (name, o, nf)
            al["off"] = o + nf
            ap = Aten[0:parts, o:o + nf]
            if dt != F32:
                ap = ap.bitcast(dt)
            ap = ap[:, 0:ncols]
            al["n"] += 1
            return V(ap, "%s#%d" % (name, al["n"]))

        def phase_reset():
            P.barrier(bscr[0:1, :], bscr[1:2, :])
            al["off"] = al["base"]

        ident_bf = alloc("identb", 128, dt=BF16); ident_f = alloc("identf", 128)
        colv = alloc("colv", 256); smv = alloc("smv", 4, parts=64)
        s2 = alloc("s2", 16); sbc = alloc("sbc", 1024)
        ada_sb = [alloc("ada0", 48), alloc("ada1", 48)]
        gate_bc = [alloc("gbc0", 1024), alloc("gbc1", 1024)]
        nw_bc = alloc("nwbc", 1024); fnw_bc = alloc("fnwbc", 1024)
        Acol = [[alloc("A00", 8), alloc("A01", 8)], [alloc("A10", 8), None]]
        epsc = alloc("eps", 1)
        al["base"] = al["off"]

        P.dma(bscr, ident_f_d[0:2, 0:16])
        P.dma(ident_bf, ident_bf_d); P.dma(ident_f, ident_f_d); P.dma(smv, smallv)
        P.dma(nw_bc, rowv[0].partition_broadcast(128)); P.dma(fnw_bc, rowv[1].partition_broadcast(128))
        P.mset(epsc, EPS)

        vrow = alloc("vrow", 128)
        for j in range(2):
            P.dma(vrow, vecs[j * 128:(j + 1) * 128, :])
            P.tr(PS[0][:, 0:128], vrow, ident_f)
            P.cp(colv[:, j * 128:(j + 1) * 128], PS[0][:, 0:128])
        s2v = s2.r("p (k j) -> p k j", j=2)
        P.act(s2v[:, :, 0], colv[:, 0:8], AF.Silu)
        P.act(s2v[:, :, 1], colv[:, 8:16], AF.Silu)
        sbcv = sbc.r("p (k m) -> p k m", m=128)
        P.cp(sbcv, bc(s2v[:, :, 0], 2, 128))
        awc = [alloc("awc0", 1024), alloc("awc1", 1024)]
        gb_tmp = alloc("gbtmp", 1024)
        for l in range(2):
            awl = ada_w[l].rearrange("(kt p) n -> p kt n", p=128)
            P.dma(gb_tmp, rowv[2 + l].partition_broadcast(128))
            for m in range(24):
                cw = awc[m % 2]
                cwv = cw.r("p (k n) -> p k n", n=128)
                P.dma(cwv, awl[:, :, m * 128:(m + 1) * 128])
                for kt in range(8):
                    P.mm(PS[1][:, 2 * m:2 * m + 2], cwv[:, kt, :], s2v[:, kt, :], start=(kt == 0), stop=(kt == 7))
                if m >= 16:
                    g = m - 16
                    for kt in range(8):
                        P.mm(PS[2 + g // 4][:, (g % 4) * 128:(g % 4 + 1) * 128], sbcv[:, kt, :], cwv[:, kt, :],
                             start=(kt == 0), stop=(kt == 7))
            adv = ada_sb[l].r("p (m j) -> p m j", j=2)
            P.tt(adv, PS[1][:, 0:48].r("p (m j) -> p m j", j=2), bc(colv[:, CV["adab%d" % l]:CV["adab%d" % l] + 24], 1, 2), ALU.add)
            P.tt(gate_bc[l][:, 0:512], PS[2], gb_tmp[:, 0:512], ALU.add)
            P.tt(gate_bc[l][:, 512:1024], PS[3], gb_tmp[:, 512:1024], ALU.add)
            for j in range(2):
                if Acol[l][j] is None:
                    continue
                P.ts(Acol[l][j], adv[:, 8:16, j], 1.0, None, ALU.add)
                P.tt(Acol[l][j], Acol[l][j], colv[:, CV["nw%d" % l]:CV["nw%d" % l] + 8], ALU.mult)
        if stop_after <= 0:
            dbgo = ddbg("dbg0", [128, 1024 + 96 + 256])
            P.dma(dbgo[:, 0:1024], gate_bc[0]); P.dma(dbgo[:, 1024:1072], ada_sb[0]); P.dma(dbgo[:, 1072:1120], ada_sb[1])
            P.dma(dbgo[:, 1120:1376], colv)
            P.emit(st)
            return nc, P

        U_ = dscr("U_", [3072, 4096]); Vbf = dscr("Vbf", [1024, 4096], BF16); GS2 = dscr("GS2", [1024, 4096])
        KERN = dscr("KERN", [2, 1024, 8192], BF16); KSP = dscr("KSP", [2, 16, 128, 4096], BF16)
        Z1B = dscr("Z1B", [1024, 4096], BF16); ZF = dscr("ZF", [1024, 4096], BF16)
        import os
        RUN_L0 = not int(os.environ.get('ONLY_HY', 0))
        if RUN_L0:
            phase_reset()
            hT = alloc("hT", 8 * 4352, dt=BF16)
            hTv = hT.r("p (k t) -> p k t", k=8)

            def make_hT_builder():
                bufs = dict(xts=[alloc("xt%d" % i, 1024) for i in range(3)], junk=alloc("junk", 1024),
                            ss=[alloc("ss%d" % i, 1) for i in range(3)], rs=[alloc("rs%d" % i, 1) for i in range(3)],
                            xs=[alloc("xs%d" % i, 1024, dt=BF16) for i in range(2)])

                def tile(i, src, l, j, hTv_, key):
                    xt = bufs["xts"][i % 3]
                    if not isinstance(src, V):
                        P.dma(xt, src)
                    else:
                        xt = src
                    junk, ss, rs, xs = bufs["junk"], bufs["ss"], bufs["rs"], bufs["xs"]
                    P.act(junk, xt, AF.Square)
                    P.red(ss[i % 3], junk)
                    P.act(rs[i % 3], ss[i % 3], AF.Sqrt, scale=1.0 / 1024.0, bias=epsc)
                    P.rcp(rs[i % 3], rs[i % 3])
                    P.ts(xs[i % 2], xt, rs[i % 3], None, ALU.mult)
                    pb = PSb[6 + (i % 2)]
                    for kt in range(8):
                        P.tr(pb[:, kt * 128:(kt + 1) * 128], xs[i % 2][:, kt * 128:(kt + 1) * 128], ident_bf)
                    adv_ = ada_sb[l].r("p (m j) -> p m j", j=2)
                    for kt in range(8):
                        dst = hTv_[:, kt, i * 128:(i + 1) * 128].k((key, i))
                        if kt % 2 == 0:
                            P.act(dst, pb[:, kt * 128:(kt + 1) * 128], AF.Identity, scale=Acol[l][j][:, kt:kt + 1], bias=adv_[:, kt, j:j + 1])
                        else:
                            P.ts(dst, pb[:, kt * 128:(kt + 1) * 128], Acol[l][j][:, kt:kt + 1], adv_[:, kt, j:j + 1], ALU.mult, ALU.add)
                return tile

            def build_hT(specs, hTv_):
                tile = make_hT_builder()
                for i, (src, l, j) in enumerate(specs):
                    tile(i, src, l, j, hTv_, "hT")

            specs0 = [(ctx[i * 128:(i + 1) * 128, :], 0, 1) for i in range(2)] + [(x[i * 128:(i + 1) * 128, :], 0, 0) for i in range(32)]
            mark = al["off"]
            build_hT(specs0, hTv)
            if stop_after <= 1:
                dbgo = ddbg("dbg1", [128, 8 * 4352], BF16)
                P.dma(dbgo, hT, rd=[("hT", i) for i in range(34)])
                P.emit(st)
                return nc, P

            P.barrier(bscr[0:1, :], bscr[1:2, :])
            al["off"] = mark
            w_in_b = alloc("w_in_b", 8 * 2208, dt=BF16)
            w_in_bv = w_in_b.r("p (k n) -> p k n", k=8)
            wst = [alloc("wst%d" % i, 2208) for i in range(2)]
            for kt in range(8):
                P.dma(wst[kt % 2], w_in[kt * 128:(kt + 1) * 128, :])
                if kt % 2 == 0:
                    P.cp(w_in_bv[:, kt, :], wst[kt % 2], eng="act")
                else:
                    P.cp(w_in_bv[:, kt, :], wst[kt % 2])
            w_uq_b = alloc("w_uq_b", 2 * 768, dt=BF16)
            w_uq_bv = w_uq_b.r("p (k n) -> p k n", k=2)
            for kt in range(2):
                P.dma(wst[kt % 2][:, 0:768], w_uq[kt * 128:(kt + 1) * 128, :])
                P.cp(w_uq_bv[:, kt, :], wst[kt % 2][:, 0:768])
            w_ukv_b = alloc("w_ukv_b", 1024, dt=BF16)
            P.dma(wst[0][:, 0:1024], w_ukv)
            P.cp(w_ukv_b, wst[0][:, 0:1024])

            pr = wst[0]
            sq = alloc("sq", 640)
            ss10 = alloc("ss10", 16); rs10 = alloc("rs10", 16)
            qn = alloc("qn", 640); t1 = alloc("t1", 640); t2 = alloc("t2", 640)
            qkb = alloc("qkb", 640, dt=BF16)
            rp = [alloc("rp%d" % i, 192) for i in range(2)]
            cqn = alloc("cqn", 384); cqb = alloc("cqb", 384, dt=BF16)
            cT = alloc("cT", 384, dt=BF16)
            qm = alloc("qm", 768); qmb = alloc("qmb", 768, dt=BF16)
            kpe = alloc("kpe", 32); kt1 = alloc("kt1", 32); kt2 = alloc("kt2", 32)
            kmb = alloc("kmb", 768, dt=BF16)
            vab = alloc("vab", 132, dt=BF16); vmb = alloc("vmb", 528, dt=BF16)
            gsb = [alloc("gsb0", 1024)] * 2
            stQa = [alloc("stQa0", 8 * 256, parts=64, dt=BF16)]
            stKa = [alloc("stKa0", 2 * 256, parts=64, dt=BF16)]
            stQm = [alloc("stQm0", 8 * 256, parts=96, dt=BF16)]
            stKm = [alloc("stKm0", 8 * 256, parts=96, dt=BF16)]
            P.mset(vab, 1.0)
            P.mset(vmb, 1.0)

            def rope(dst, src, tab, nh, D, tmp1, tmp2):
                q = D // 4
                sv = src.r("p (h f) -> p h f", f=D)
                Cb = bc(tab[:, 0:D], 0, nh)
                P.tt(tmp1.r("p (h f) -> p h f", f=D), sv, Cb, ALU.mult)
                s5 = src.r("p (h a s f) -> p (h a) s f", a=2, s=2, f=q)
                t5 = tmp2.r("p (h a s f) -> p (h a) s f", a=2, s=2, f=q)
                S4 = tab[:, D:2 * D].r("p (a s f) -> p a s f", a=2, s=2)
                for s_ in range(2):
                    for a_ in range(2):
                        srcv = src.r("p (h a s f) -> p h a s f", a=2, s=2, f=q)[:, :, a_, 1 - s_, :]
                        dstv = tmp2.r("p (h a s f) -> p h a s f", a=2, s=2, f=q)[:, :, a_, s_, :]
                        Sb = bc(S4[:, a_, s_, :], 0, nh)
                        P.tt(dstv, srcv, Sb, ALU.mult)
                P.tt(dst, tmp1, tmp2, ALU.add)

            import os
            for tt_ in range(int(os.environ.get('P2_TILES', NT_ALL))):
                lat = tt_ >= 2
                STG = int(os.environ.get('P2_STAGE', 99))
                li = tt_ - 2
                grp = 0
                sl = tt_ % 2
                chunks = [(0, 512), (512, 1024), (1024, 1536), (1536, 2048), (2048, 2208)]
                nchunk = 5 if lat else 3
                for ci in range(nchunk):
                    c0, c1 = chunks[ci]
                    for kt in range(8):
                        P.mm(PS[ci][:, 0:c1 - c0], hTv[:, kt, tt_ * 128:(tt_ + 1) * 128].k(("hT", tt_)), w_in_bv[:, kt, c0:c1],
                             start=(kt == 0), stop=(kt == 7))
                    if ci % 2 == 0:
                        P.cp(pr[:, c0:c1], PS[ci][:, 0:c1 - c0], eng="act")
                    else:
                        P.cp(pr[:, c0:c1], PS[ci][:, 0:c1 - c0])
                if STG <= 1:
                    continue
                if lat:
                    P.dma(rp[tt_ % 2][:, 0:128], ropeA[li * 128:(li + 1) * 128, :])
                    P.dma(rp[tt_ % 2][:, 128:192], ropeM[li * 128:(li + 1) * 128, :])
                h0 = 0 if lat else 512
                nh = 10 if lat else 2
                P.tt(sq[:, h0:640], pr[:, h0:640], pr[:, h0:640], ALU.mult)
                P.red(ss10[:, 0:nh], sq[:, h0:640].r("p (h f) -> p h f", f=64))
                P.act(rs10[:, 0:nh], ss10[:, 0:nh], AF.Sqrt, scale=1.0 / 64.0, bias=epsc)
                P.rcp(rs10[:, 0:nh], rs10[:, 0:nh])
                P.tt(qn[:, h0:640].r("p (h f) -> p h f", f=64), pr[:, h0:640].r("p (h f) -> p h f", f=64), bc(rs10[:, 0:nh], 1, 64), ALU.mult)
                if lat:
                    P.tt(qn, qn, nw_bc[:, 0:640], ALU.mult)
                    rope(qkb, qn, rp[tt_ % 2][:, 0:128], 10, 64, t1, t2)
                else:
                    P.tt(qkb[:, 512:640], qn[:, 512:640], nw_bc[:, 512:640], ALU.mult)
                if STG <= 2:
                    continue
                pb = PSb[5]
                for h in range(h0 // 64, 10):
                    P.tr(pb[0:64, (h % 8) * 128:(h % 8 + 1) * 128] if h < 8 else PSb[6][0:64, (h - 8) * 128:(h - 7) * 128],
                         qkb[:, h * 64:(h + 1) * 64], ident_bf)
                if lat:
                    P.cp(stQa[grp].r("p (h t) -> p h t", h=8)[:, :, sl * 128:(sl + 1) * 128], pb[0:64, :].r("p (h t) -> p h t", h=8), eng="act")
                P.cp(stKa[grp].r("p (h t) -> p h t", h=2)[:, :, sl * 128:(sl + 1) * 128], PSb[6][0:64, 0:256].r("p (h t) -> p h t", h=2))
                if STG <= 3:
                    continue
                P.cp(vab.r("p (g d) -> p g d", d=66)[:, :, 0:64], pr[:, 640:768].r("p (g d) -> p g d", d=64), eng="act")
                P.dma(Va[:, :, tt_, :].rearrange("g p d -> p g d"), vab.r("p (g d) -> p g d", d=66))
                if STG <= 4:
                    continue
                P.tt(sq[:, 0:384], pr[:, 768:1152], pr[:, 768:1152], ALU.mult)
                P.red(ss10[:, 10:11], sq[:, 0:256]); P.red(ss10[:, 11:12], sq[:, 256:384])
                P.act(rs10[:, 10:11], ss10[:, 10:11], AF.Sqrt, scale=1.0 / 256.0, bias=epsc)
                P.act(rs10[:, 11:12], ss10[:, 11:12], AF.Sqrt, scale=1.0 / 128.0, bias=epsc)
                P.rcp(rs10[:, 10:12], rs10[:, 10:12])
                P.stt(cqb[:, 0:256], pr[:, 768:1024], rs10[:, 10:11], nw_bc[:, 640:896], ALU.mult, ALU.mult)
                P.stt(cqb[:, 256:384], pr[:, 1024:1152], rs10[:, 11:12], nw_bc[:, 896:1024], ALU.mult, ALU.mult)
                pb7 = PSb[7]
                j0 = 0 if lat else 2
                for j in range(j0, 3):
                    P.tr(pb7[:, j * 128:(j + 1) * 128], cqb[:, j * 128:(j + 1) * 128], ident_bf)
                P.cp(cT[:, j0 * 128:384], pb7[:, j0 * 128:384], eng="act")
                if STG <= 5:
                    continue
                for hh in range(2):
                    P.mm(PS[hh], cT[:, 256:384], w_ukv_b[:, hh * 512:(hh + 1) * 512])
                if lat:
                    rope(kpe, pr[:, 1152:1184], rp[tt_ % 2][:, 128:192], 1, 32, kt1, kt2)
                    kpe_src = kpe
                else:
                    kpe_src = pr[:, 1152:1184]
                SUB = int(os.environ.get('P2_SUB', 99))
                if SUB <= 0:
                    continue
                kmv = kmb.r("p (h f) -> p h f", f=96)
                for hh in range(2):
                    psv = PS[hh].r("p (h f) -> p h f", f=128)
                    if hh == 0:
                        P.cp(kmv[:, 0:4, 0:64], psv[:, :, 0:64], eng="act")
                    else:
                        P.cp(kmv[:, 4:8, 0:64], psv[:, :, 0:64])
                    if SUB <= 1:
                        continue
                    P.cp(vmb.r("p (h f) -> p h f", f=66)[:, hh * 4:(hh + 1) * 4, 0:64], psv[:, :, 64:128], eng="act")
                if SUB <= 2:
                    continue
                for h in range(8):
                    P.cp(kmv[:, h, 64:96], kpe_src, eng=("act" if h % 2 else "dve"))
                if STG <= 6:
                    continue
                P.dma(Vm[:, :, tt_, :].rearrange("h p d -> p h d"), vmb.r("p (h d) -> p h d", d=66))
                if STG <= 7:
                    continue
                for h in range(8):
                    P.tr(PSb[2 + h // 4][0:96, (h % 4) * 128:(h % 4 + 1) * 128] if False else PSb[3][0:96, h * 128:(h + 1) * 128], kmb[:, h * 96:(h + 1) * 96], ident_bf)
                P.cp(stKm[grp].r("p (h t) -> p h t", h=8)[:, :, sl * 128:(sl + 1) * 128], PSb[3][0:96, :].r("p (h t) -> p h t", h=8), eng="act")
                if lat:
                    for (c0, c1, bank) in ((0, 512, 2), (512, 768, 4)):
                        for j in range(2):
                            P.mm(PS[bank][:, 0:c1 - c0], cT[:, j * 128:(j + 1) * 128], w_uq_bv[:, j, c0:c1], start=(j == 0), stop=(j == 1))
                        P.cp(qm[:, c0:c1], PS[bank][:, 0:c1 - c0], eng=("act" if bank == 2 else "dve"))
                    qmv = qm.r("p (h f) -> p h f", f=96)
                    qpe = t1[:, 0:256]; qpo = t1[:, 256:512]
                    P.cp(qpe.r("p (h f) -> p h f", f=32), qmv[:, :, 64:96])
                    rope(qpo, qpe, rp[tt_ % 2][:, 128:192], 8, 32, t2[:, 0:256], t2[:, 256:512])
                    qmbv = qmb.r("p (h f) -> p h f", f=96)
                    P.cp(qmbv[:, :, 0:64], qmv[:, :, 0:64], eng="act")
                    P.cp(qmbv[:, :, 64:96], qpo.r("p (h f) -> p h f", f=32))
                    for h in range(8):
                        P.tr(PSb[4][0:96, h * 128:(h + 1) * 128], qmb[:, h * 96:(h + 1) * 96], ident_bf)
                    P.cp(stQm[grp].r("p (h t) -> p h t", h=8)[:, :, sl * 128:(sl + 1) * 128], PSb[4][0:96, :].r("p (h t) -> p h t", h=8))
                    P.act(gsb[tt_ % 2], pr[:, 1184:2208], AF.Silu)
                    P.dma(Gs[li * 128:(li + 1) * 128, :], gsb[tt_ % 2])
                flush = (sl == 1)
                if flush:
                    g0 = (tt_ // 2) * 2
                    ntk = tt_ - g0 + 1
                    kcol0 = g0 * 128
                    P.dma(KaT[:, :, kcol0:kcol0 + ntk * 128].rearrange("h d t -> d h t"), stKa[grp].r("p (h t) -> p h t", h=2)[:, :, 0:ntk * 128])
                    P.dma(KmT[:, :, kcol0:kcol0 + ntk * 128].rearrange("h d t -> d h t"), stKm[grp].r("p (h t) -> p h t", h=8)[:, :, 0:ntk * 128])
                    if lat:
                        t_first = max(g0, 2)
                        so = (t_first - g0) * 128
                        nq = tt_ - t_first + 1
                        qc0 = (t_first - 2) * 128
                        P.dma(QaT[:, :, qc0:qc0 + nq * 128].rearrange("h d t -> d h t"), stQa[grp].r("p (h t) -> p h t", h=8)[:, :, so:so + nq * 128])
                        P.dma(QmT[:, :, qc0:qc0 + nq * 128].rearrange("h d t -> d h t"), stQm[grp].r("p (h t) -> p h t", h=8)[:, :, so:so + nq * 128])
            if stop_after <= 2:
                P.barrier(bscr[0:1, :], bscr[1:2, :])
                P.emit(st)
                return nc, P

            phase_reset()
            Kb = [alloc("Kb%d" % i, 4352, parts=96, dt=BF16) for i in range(2)]
            Vb_ = [alloc("Vb%d" % i, 34 * 66, dt=BF16) for i in range(2)]
            Qb = [alloc("Qb%d" % i, 512, parts=96, dt=BF16) for i in range(2)]
            Pt2 = [alloc("Pt2_%d" % i, 1024, dt=BF16) for i in range(2)]
            OTs = [alloc("OTs%d" % i, 512, parts=66) for i in range(2)]
            osb = [alloc("osb%d" % i, 256) for i in range(2)]
            rc = [alloc("rc%d" % i, 4) for i in range(2)]
            NQT = int(os.environ.get('P3_QT', 32))
            NGRP = int(os.environ.get('P3_GRPS', 10))

            def load_kv(g):
                kb__ = Kb[g % 2]; vbv__ = Vb_[g % 2].r("p (t d) -> p t d", d=66)
                if g < 2:
                    P.dma(kb__[0:64, :], KaT[g]); P.dma(vbv__, Va[g])
                else:
                    P.dma(kb__[0:96, :], KmT[g - 2]); P.dma(vbv__, Vm[g - 2])

            units = []
            for g in range(NGRP):
                for u in range(NQT if g < 2 else NQT // 4):
                    units.append((g, u))

            def load_q(ui):
                g, u = units[ui]
                qb_ = Qb[ui % 2]
                if g < 2:
                    P.dma(qb_[0:64, :].r("p (h t) -> p h t", h=4),
                          QaT[4 * g:4 * g + 4, :, u * 128:(u + 1) * 128].rearrange("h d t -> d h t"))
                else:
                    P.dma(qb_[0:96, :], QmT[g - 2][:, u * 512:(u + 1) * 512])

            load_kv(0)
            load_q(0)
            itc = {"n": 0}
            for ui, (g, u) in enumerate(units):
                gqa = g < 2
                kd = 64 if gqa else 96
                scale = (64.0 ** -0.5) if gqa else (96.0 ** -0.5)
                kb_ = Kb[g % 2]
                vbv = Vb_[g % 2].r("p (t d) -> p t d", d=66)
                qb = Qb[ui % 2]
                if u == 0 and g + 1 < NGRP:
                    load_kv(g + 1)
                if ui + 1 < len(units):
                    load_q(ui + 1)
                po = PS[4 + ui % 2]
                it0 = itc["n"]

                def qk2(j):
                    it2 = it0 + j
                    sd = PSD[it2 % 2]
                    for t_ in range(2):
                        kb = 2 * j + t_
                        P.mm(PS[2 * (it2 % 2) + t_], kb_[0:kd, kb * 128:(kb + 1) * 128], qb[0:kd, :])
                    P.act(Pt2[it2 % 2], sd[:, 0:1024], AF.Exp, scale=scale,
                          rd=[PS[2 * (it2 % 2)], PS[2 * (it2 % 2) + 1]], wr=[Pt2[it2 % 2]])

                qk2(0)
                for j in range(17):
                    if j + 1 < 17:
                        qk2(j + 1)
                    pt = Pt2[(it0 + j) % 2]
                    for t_ in range(2):
                        kb = 2 * j + t_
                        P.mm(po[0:66, :], vbv[:, kb, :], pt[:, t_ * 512:(t_ + 1) * 512], start=(kb == 0), stop=(kb == 33))
                itc["n"] += 17
                P.cp(OTs[ui % 2], po[0:66, :])
                pto = PS[6 + ui % 2]
                for h in range(4):
                    P.tr(pto[:, h * 66:(h + 1) * 66], OTs[ui % 2][:, h * 128:(h + 1) * 128], ident_f[0:66, 0:66])
                pov = pto[:, 0:264].r("p (h d) -> p h d", d=66)
                P.rcp(rc[ui % 2], pov[:, :, 64])
                ob = osb[ui % 2]
                P.tt(ob.r("p (h d) -> p h d", d=64), pov[:, :, 0:64], bc(rc[ui % 2], 1, 64), ALU.mult)
                if gqa:
                    P.dma(Oall[u * 128:(u + 1) * 128, g * 256:(g + 1) * 256], ob)
                else:
                    hcol = 512 + (g - 2) * 64
                    P.dma(Oall[u * 512:(u + 1) * 512, hcol:hcol + 64].rearrange("(qi p) d -> p qi d", p=128), ob.r("p (h d) -> p h d", d=64))

            phase_reset()
            h1T = alloc("h1T", 8 * 4096, dt=BF16)
            h1Tv = h1T.r("p (k t) -> p k t", k=8)
            mark_h1 = al["off"]
            hT1_tile = make_hT_builder()
            wo_b = alloc("wo_b", 8 * 1024, dt=BF16)
            wo_bv = wo_b.r("p (k n) -> p k n", k=8)
            wst2 = [alloc("wst2_%d" % i, 1024) for i in range(2)]
            for kt in range(8):
                P.dma(wst2[kt % 2], w_out[kt * 128:(kt + 1) * 128, :])
                P.cp(wo_bv[:, kt, :], wst2[kt % 2], eng=("act" if kt % 2 else "dve"))
            ot = [alloc("ot%d" % i, 1024) for i in range(2)]
            gt = [alloc("gt%d" % i, 1024) for i in range(2)]
            xt2 = [alloc("xt2_%d" % i, 1024) for i in range(2)]
            ogb = alloc("ogb", 1024, dt=BF16)
            ogT = alloc("ogT", 1024, dt=BF16)
            xl = [alloc("xl%d" % i, 1024) for i in range(2)]
            for i in range(NQT if int(os.environ.get('P3_B', 1)) else 0):
                b2 = i % 2
                P.dma(ot[b2], Oall[i * 128:(i + 1) * 128, :])
                P.dma(gt[b2], Gs[i * 128:(i + 1) * 128, :])
                P.dma(xt2[b2], x[i * 128:(i + 1) * 128, :])
                P.tt(ogb, ot[b2], gt[b2], ALU.mult)
                for ct in range(8):
                    P.tr(PSb[5][:, ct * 128:(ct + 1) * 128], ogb[:, ct * 128:(ct + 1) * 128], ident_bf)
                P.cp(ogT, PSb[5], eng="act")
                ogTv = ogT.r("p (k t) -> p k t", k=8)
                for hh in range(2):
                    for ct in range(8):
                        P.mm(PS[hh], ogTv[:, ct, :], wo_bv[:, ct, hh * 512:(hh + 1) * 512], start=(ct == 0), stop=(ct == 7))
                    P.tt(xl[b2][:, hh * 512:(hh + 1) * 512], PS[hh], gate_bc[0][:, hh * 512:(hh + 1) * 512], ALU.mult)
                P.tt(xl[b2], xl[b2], xt2[b2], ALU.add)
                P.dma(XL1[i * 128:(i + 1) * 128, :], xl[b2])
                hT1_tile(i, xl[b2], 1, 0, h1Tv, "h1T")
            if stop_after <= 3:
                dbgo = ddbg("dbg3", [128, 8 * 4096], BF16)
                P.dma(dbgo, h1T, rd=[("h1T", i) for i in range(NQT)])
                P.emit(st)
                return nc, P

            NT6 = int(os.environ.get('P6_TILES', 32))
            P.barrier(bscr[0:1, :], bscr[1:2, :])
            al["off"] = mark_h1
            hwl = hy_w_in.rearrange("(kt p) n -> p kt n", p=128)
            wst6 = [alloc("wst6_%d" % i, 1024) for i in range(2)]
            wcb = [alloc("wcb%d" % i, 1024, dt=BF16) for i in range(2)]
            psb = [alloc("psb%d" % i, 4104) for i in range(2)]
            ub = [alloc("ub%d" % i, 4096) for i in range(2)]
            vbf_sb = alloc("vbf_sb", 4096, dt=BF16)
            for i in range(2):
                P.mset(psb[i][:, 0:1], 0.0)
                P.mset(psb[i][:, 4097:4098], 0.0)
            for j in list(range(NT6)) if NT6 == 32 else [0, 8, 16, 24][:NT6]:
                b2 = j % 2
                P.dma(wst6[b2].r("p (k n) -> p k n", k=8), hwl[:, :, j * 128:(j + 1) * 128])
                P.cp(wcb[b2], wst6[b2], eng=("act" if b2 else "dve"))
                wv = wcb[b2].r("p (k n) -> p k n", k=8)
                for tc in range(8):
                    pp = PS[tc % 4]
                    for kt in range(8):
                        P.mm(pp, wv[:, kt, :], h1Tv[:, kt, tc * 512:(tc + 1) * 512],
                             start=(kt == 0), stop=(kt == 7), rd=[wcb[b2]] + [("h1T", tc * 4 + q_) for q_ in range(4)])
                    if j < 24:
                        P.cp(psb[b2][:, 1 + tc * 512:1 + (tc + 1) * 512], pp, eng=("act" if tc % 2 else "dve"))
                    else:
                        P.act(ub[b2][:, tc * 512:(tc + 1) * 512], pp, AF.Silu)
                if j < 24:
                    P.act(ub[b2], psb[b2][:, 1:4097], AF.Identity, scale=colv[:, CV["cw1"] + j:CV["cw1"] + j + 1], bias=colv[:, CV["cb"] + j:CV["cb"] + j + 1])
                    P.stt(ub[b2], psb[b2][:, 0:4096], colv[:, CV["cw0"] + j:CV["cw0"] + j + 1], ub[b2], ALU.mult, ALU.add)
                    P.stt(ub[b2], psb[b2][:, 2:4098], colv[:, CV["cw2"] + j:CV["cw2"] + j + 1], ub[b2], ALU.mult, ALU.add)
                    P.dma(U_[j * 128:(j + 1) * 128, :], ub[b2])
                    if j < 8:
                        P.cp(vbf_sb, ub[b2], eng="act")
                        P.dma(Vbf[j * 128:(j + 1) * 128, :], vbf_sb)
                else:
                    P.dma(GS2[(j - 24) * 128:(j - 23) * 128, :], ub[b2])
            if stop_after <= 6:
                P.emit(st)
                return nc, P

        phase_reset()
        TWO_PI = 2.0 * math.pi
        big5 = alloc("big5", 8192); hk = alloc("hk", 8192)
        w1s = alloc("w1s", 64, parts=33); embs = big5[0:33, :]
        P.dma(w1s, f_w1); P.dma(embs, embT)
        w2f = alloc("w2f", 64, parts=64); w2b = alloc("w2b", 64, parts=64, dt=BF16)
        P.dma(w2f, f_w2); P.cp(w2b, w2f)
        w3f = hk[0:64, 0:4096]; w3b = alloc("w3b", 4096, parts=64, dt=BF16)
        P.dma(w3f, f_w3); P.cp(w3b, w3f)
        fs = alloc("fs", 4, parts=64)
        P.ts(fs[:, 0:1], smv[:, 0:1], 1.0 / TWO_PI, None, ALU.mult)
        P.tt(fs[:, 1:2], fs[:, 0:1], smv[:, 1:2], ALU.mult)
        P.tt(fs[:, 2:3], fs[:, 0:1], smv[:, 2:3], ALU.mult)
        hid1 = alloc("hid1", 8192, parts=64, dt=BF16); hid2 = alloc("hid2", 8192, parts=64, dt=BF16)
        ubuf = alloc("ubuf", 512, parts=64); ibuf = alloc("ibuf", 512, parts=64, dt=I32); rbuf = alloc("rbuf", 512, parts=64)

        def sin_layer(dst, lhsT, rhs_fn, bias_col):
            for ncx in range(16):
                pp = PS[ncx % 2]
                P.mm(pp[0:64, :], lhsT, rhs_fn(ncx))
                P.act(ubuf, pp[0:64, :], AF.Identity, scale=fs[:, 0:1], bias=fs[:, bias_col:bias_col + 1])
                P.cp(ibuf, ubuf)
                P.cp(rbuf, ibuf)
                P.tt(ubuf, ubuf, rbuf, ALU.subtract)
                P.act(dst[:, ncx * 512:(ncx + 1) * 512], ubuf, AF.Sin, scale=TWO_PI)

        sin_layer(hid1, w1s, lambda ncx: embs[:, ncx * 512:(ncx + 1) * 512], 1)
        sin_layer(hid2, w2b, lambda ncx: hid1[:, ncx * 512:(ncx + 1) * 512], 2)
        tl_bc = big5
        P.dma(tl_bc, tlsgn[0].partition_broadcast(128))
        sdec = alloc("sdec", 8192)
        ksum = alloc("ksum", 1); kernb = alloc("kernb", 8192, dt=BF16); kjunk = kernb
        NCT5 = int(os.environ.get('P5_CT', 8))
        for ct in range(NCT5):
            P.act(sdec, tl_bc, AF.Exp, scale=colv[:, CV["ndelta"] + ct:CV["ndelta"] + ct + 1])
            P.ts(sdec[:, 4096:8192], sdec[:, 4096:8192], -1.0, None, ALU.mult)
            P.mset(sdec[:, 4096:4097], 0.0)
            for o in range(2):
                for dr in range(2):
                    col = (o * 2 + dr) * 1024 + ct * 128
                    bcol = CV["b3"] + (o * 2 + dr) * 8 + ct
                    for ncx in range(8):
                        n0 = dr * 4096 + ncx * 512
                        pp = PS[ncx % 2]
                        P.mm(pp, w3b[:, col:col + 128], hid2[:, n0:n0 + 512])
                        P.act(hk[:, n0:n0 + 512], pp, AF.Identity, bias=colv[:, bcol:bcol + 1])
                P.tt(hk, hk, sdec, ALU.mult)
                P.act(kjunk, hk, AF.Abs)
                P.red(ksum, kjunk)
                P.rcp(ksum, ksum)
                P.act(kernb, hk, AF.Copy, scale=ksum)
                P.dma(KERN[o, ct * 128:(ct + 1) * 128, :], kernb)
        if stop_after <= 5:
            d5 = ddbg("dbg5", [128, 64])
            P.dma(d5[:, 0:16], hk[:, 0:16]); P.dma(d5[:, 16:32], sdec[:, 0:16]); P.dma(d5[:, 32:33], ksum, allow_slow_non_contiguous=True)
            d5b = ddbg("dbg5b", [64, 64], BF16)
            P.dma(d5b[:, 0:32], hid1[:, 0:32]); P.dma(d5b[:, 32:64], hid2[:, 0:32])
            P.emit(st)
            return nc, P

        phase_reset()
        F1reg = alloc("F1reg", 128, dt=BF16)
        P.dma(F1reg[0:64, :], F1c); P.dma(F1reg[64:128, :], F1c)
        F1u = F1reg[64:128, :]
        F2s = alloc("F2s", 16384, dt=BF16); P.dma(F2s[:, 0:8192], F2c[:, 0:8192]); P.dma(F2s[:, 8192:16384], F2c[:, 8192:16384])
        F2v = F2s.r("p (k w m) -> p k w m", k=64, w=2)
        Gs_ = alloc("Gs_", 512, dt=BF16); P.dma(Gs_, Gc)
        Gv = Gs_.r("p (h w n) -> p h w n", h=2, w=2)
        Hreg = alloc("Hreg", 8192, dt=BF16)
        Hs = Hreg[0:64, :].k("Hs"); P.dma(Hs, Hc)
        D1buf = Hreg[64:128, :].k("D1buf")
        Hv = Hs.r("p (b n) -> p b n", n=64)
        bufA = alloc("bufA", 8192, dt=BF16)
        bufB = alloc("bufB", 8192, dt=BF16)
        Zbuf = alloc("Zbuf", 8192, parts=64, dt=BF16)
        zb = alloc("zb", 4096, dt=BF16)
        evc = {"n": 0}

        def evac(dst, src):
            evc["n"] += 1
            P.cp(dst, src, eng=("act" if evc["n"] % 2 else "dve"))

        def fwd_stages(src_rows, K, on_bank):
            D1 = D1buf[0:K, :].r("p (c b) -> p c b", b=128)
            for q_ in range(4):
                P.dma(D1[:, q_ * 16:(q_ + 1) * 16, :], src_rows[q_ * 16:(q_ + 1) * 16, :].rearrange("c (a b) -> a c b", b=128))
            Bv = bufB.r("p (c k) -> p k c", c=64)
            for c4 in range(16):
                pp = PS[c4 % 2]
                for cc in range(4):
                    P.mm(pp[:, cc * 128:(cc + 1) * 128], D1[:, c4 * 4 + cc, :], F1u[0:K, :], start=(cc == 0), stop=True, skip_group_check=True)
                evac(bufB[:, c4 * 512:(c4 + 1) * 512], pp)
            for k8 in range(8):
                pp = PS[2 + k8 % 2]
                for kk in range(8):
                    kap = k8 * 8 + kk
                    P.mm(pp[:, kk * 64:(kk + 1) * 64], F2v[:, kap, 0, :], Bv[:, kap, :], start=(kk == 0), stop=False, skip_group_check=True)
                    P.mm(pp[:, kk * 64:(kk + 1) * 64], F2v[:, kap, 1, :], Bv[:, 64 + kap, :], start=False, stop=True, skip_group_check=True)
                on_bank(k8, pp)

        ksb = zb
        NHC = int(os.environ.get('P7_HC', 16))
        for o in range(2):
            for hc in range(NHC):
                def fb(k8, pp):
                    evac(ksb[:, k8 * 512:(k8 + 1) * 512], pp)
                fwd_stages(KERN[o, hc * 64:(hc + 1) * 64, :], 64, fb)
                P.dma(KSP[o, hc], ksb)
        if stop_after <= 7:
            P.emit(st)
            return nc, P

        KAs = alloc("KAs", 4096, dt=BF16); KBs = alloc("KBs", 4096, dt=BF16)
        ysb = alloc("ysb", 4096); z1s = alloc("z1s", 4096)
        ld = [alloc("ld%d" % i, 2048) for i in range(2)]
        P1 = bufA[:, 0:4096]; P2 = bufA[:, 4096:8192]
        P1c = P1.r("p (k c) -> p c k", c=64); P2c = P2.r("p (k c) -> p c k", c=64)
        Zv = Zbuf.r("p (c r b) -> p b r c", r=2, c=64)

        def conv(src_dram, o, ct):
            for half in range(2):
                hc = ct * 2 + half
                ks = KSP[o, hc]
                P.dma(KAs[0:64, :], ks[0:64, :]); P.dma(KAs[64:128, :], ks[0:64, :])
                P.dma(KBs[0:64, :], ks[64:128, :]); P.dma(KBs[64:128, :], ks[64:128, :])

                def ob(k8, pp):
                    P.tt(P1[:, k8 * 512:(k8 + 1) * 512], pp, KAs[:, k8 * 512:(k8 + 1) * 512], ALU.mult)
                    P.tt(P2[:, k8 * 512:(k8 + 1) * 512], pp, KBs[:, k8 * 512:(k8 + 1) * 512], ALU.mult)
                fwd_stages(src_dram[hc * 64:(hc + 1) * 64, :], 32, ob)
                for bh in range(2):
                    for c4 in range(16):
                        pp = PS[4 + c4 % 2]
                        for cc in range(4):
                            c = c4 * 4 + cc
                            P.mm(pp[0:64, cc * 128:(cc + 1) * 128], P1c[:, c, :], Gv[:, bh, 0, :], start=(cc == 0), stop=False, skip_group_check=True)
                            P.mm(pp[0:64, cc * 128:(cc + 1) * 128], P2c[:, c, :], Gv[:, bh, 1, :], start=False, stop=True, skip_group_check=True)
                        evac(Zbuf[:, c4 * 512:(c4 + 1) * 512], pp[0:64, :])
                    for b16 in range(4):
                        pp = PS[6 + b16 % 2]
                        for bb in range(16):
                            b = b16 * 16 + bb
                            bg = bh * 64 + b
                            P.mm(pp[0:64, bb * 32:(bb + 1) * 32], Zv[:, b, 0, :], Hv[:, bg, 0:32], start=(bb == 0), stop=False, skip_group_check=True)
                            P.mm(pp[0:64, bb * 32:(bb + 1) * 32], Zv[:, b, 1, :], Hv[:, bg, 32:64], start=False, stop=True, skip_group_check=True)
                        bg0 = bh * 64 + b16 * 16
                        P.cp(ysb[half * 64:(half + 1) * 64, :].r("p (a b) -> p b a", b=128)[:, bg0:bg0 + 16, :], pp[0:64, :].r("p (b a) -> p b a", a=32))

        NCT7 = int(os.environ.get('P7_CT', 8))
        for ct in range(NCT7):
            conv(Vbf, 0, ct)
            for tq in range(2):
                tsl = slice(tq * 2048, (tq + 1) * 2048)
                P.dma(ld[0], U_[ct * 128:(ct + 1) * 128, tsl])
                P.dma(ld[1], U_[1024 + ct * 128:1024 + (ct + 1) * 128, tsl])
                P.stt(ysb[:, tsl], ld[0], colv[:, CV["sk0"] + ct:CV["sk0"] + ct + 1], ysb[:, tsl], ALU.mult, ALU.add)
                P.tt(z1s[:, tsl], ysb[:, tsl], ld[1], ALU.mult)
            P.cp(zb, z1s, eng="act")
            P.dma(Z1B[ct * 128:(ct + 1) * 128, :], zb)
            conv(Z1B, 1, ct)
            for tq in range(2):
                tsl = slice(tq * 2048, (tq + 1) * 2048)
                P.dma(ld[0], U_[2048 + ct * 128:2048 + (ct + 1) * 128, tsl])
                P.dma(ld[1], GS2[ct * 128:(ct + 1) * 128, tsl])
                P.stt(ysb[:, tsl], z1s[:, tsl], colv[:, CV["sk1"] + ct:CV["sk1"] + ct + 1], ysb[:, tsl], ALU.mult, ALU.add)
                P.tt(ysb[:, tsl], ysb[:, tsl], ld[0], ALU.mult)
                P.tt(zb[:, tsl], ysb[:, tsl], ld[1], ALU.mult)
            P.dma(ZF[ct * 128:(ct + 1) * 128, :], zb)
        if stop_after <= 8:
            P.emit(st)
            return nc, P

        phase_reset()
        wob2 = alloc("wob2", 8 * 1024, dt=BF16)
        wob2v = wob2.r("p (k n) -> p k n", k=8)
        wst8 = [alloc("wst8_%d" % i, 1024) for i in range(2)]
        for kt in range(8):
            P.dma(wst8[kt % 2], hy_w_out[kt * 128:(kt + 1) * 128, :])
            P.cp(wob2v[:, kt, :], wst8[kt % 2], eng=("act" if kt % 2 else "dve"))
        zT = [alloc("zT%d" % i, 1024, dt=BF16) for i in range(2)]
        xl1t = [alloc("xl1t%d" % i, 1024) for i in range(2)]
        x2t = [alloc("x2t%d" % i, 1024) for i in range(2)]
        ot8 = [alloc("ot8_%d" % i, 1024) for i in range(2)]
        junk8 = alloc("junk8", 1024); ss8 = [alloc("ss8_%d" % i, 1) for i in range(2)]
        ZFv = ZF.rearrange("(k p) t -> p k t", p=128)
        for i in range(32):
            b2 = i % 2
            zTv = zT[b2].r("p (k t) -> p k t", k=8)
            P.dma(zTv, ZFv[:, :, i * 128:(i + 1) * 128])
            P.dma(xl1t[b2], XL1[i * 128:(i + 1) * 128, :])
            for hh in range(2):
                for ct in range(8):
                    P.mm(PS[hh], zTv[:, ct, :], wob2v[:, ct, hh * 512:(hh + 1) * 512], start=(ct == 0), stop=(ct == 7))
                P.tt(x2t[b2][:, hh * 512:(hh + 1) * 512], PS[hh], gate_bc[1][:, hh * 512:(hh + 1) * 512], ALU.mult)
            P.tt(x2t[b2], x2t[b2], xl1t[b2], ALU.add)
            P.act(junk8, x2t[b2], AF.Square)
            P.red(ss8[b2], junk8)
            P.act(ss8[b2], ss8[b2], AF.Sqrt, scale=1.0 / 1024.0, bias=epsc)
            P.rcp(ss8[b2], ss8[b2])
            P.stt(ot8[b2], x2t[b2], ss8[b2], fnw_bc, ALU.mult, ALU.mult)
            P.dma(out[i * 128:(i + 1) * 128, :], ot8[b2], is_output=True)
        P.emit(st)
        return nc, P


def kernel(**inputs):
    nc, _ = build()
    in_maps = [pack_inputs(inputs, b) for b in range(8)]
    res = run_bass_kernel_spmd(nc, in_maps, core_ids=list(range(8)))
    return np.stack([np.asarray(r["out"], dtype=np.float32) for r in res.results], 0)
```
